# Optimizing a Trainium2 kernel written in Bass

```python
import jax, jax.numpy as jnp
from jax import lax
import numpy as np

D_MODEL = 1024
BATCH = 4
SEQ = 8192
DEPTH = 1

CHUNK = 64
CONV_DIM = D_MODEL
CONV_WIDTH = 31
GLA_HEADS = 4
GLA_KEY_DIM = D_MODEL // 2
GLA_VAL_DIM = D_MODEL
GLA_HEAD_K = GLA_KEY_DIM // GLA_HEADS
GLA_HEAD_V = GLA_VAL_DIM // GLA_HEADS
GATE_RANK = 16
GATE_TEMP = 16.0
N_BRANCHES = 2
D_FF = 2816
FFN_CONV_WIDTH = 3
EPS = 1e-6

IN_SPLITS = (CONV_DIM, CONV_DIM, GLA_KEY_DIM, GLA_KEY_DIM, GLA_VAL_DIM, GLA_VAL_DIM,
             GATE_RANK, N_BRANCHES * D_MODEL)
IN_DIM = sum(IN_SPLITS)

kernel_name = "hybrid_conformer_gla_convffn"


def rmsnorm(x, g):
    xf = x.astype(jnp.float32)
    y = xf * lax.rsqrt(jnp.mean(xf * xf, axis=-1, keepdims=True) + EPS)
    return (y * g.astype(jnp.float32)).astype(x.dtype)


def layernorm(x, g, b):
    xf = x.astype(jnp.float32)
    mu = jnp.mean(xf, axis=-1, keepdims=True)
    xc = xf - mu
    y = xc * lax.rsqrt(jnp.mean(xc * xc, axis=-1, keepdims=True) + EPS)
    return (y * g.astype(jnp.float32) + b.astype(jnp.float32)).astype(x.dtype)


def causal_dwconv(x, w, b):
    width, ch = w.shape
    y = lax.conv_general_dilated(
        x, w[:, None, :].astype(x.dtype), window_strides=(1,),
        padding=[(width - 1, 0)], dimension_numbers=('NWC', 'WIO', 'NWC'),
        feature_group_count=ch)
    return y + b.astype(x.dtype)


def gla_chunk_causal(q, k, v, log_a):
    bsz, seq, heads, dk = q.shape
    dv = v.shape[-1]
    nc = seq // CHUNK

    def to_chunks(t):
        return t.reshape(bsz, nc, CHUNK, heads, t.shape[-1]).transpose(1, 0, 3, 2, 4)

    qc, kc, vc, ac = (to_chunks(t) for t in (q, k, v, log_a))

    def step(state, inp):
        q_c, k_c, v_c, a_c = inp
        cum = jnp.cumsum(a_c.astype(jnp.float32), axis=-2)
        total = cum[..., -1:, :]
        k_dec = k_c.astype(jnp.float32) * jnp.exp(total - cum)
        state = (state * jnp.exp(total[..., 0, :])[..., None]
                 + jnp.einsum('bhck,bhcv->bhkv', k_dec, v_c.astype(jnp.float32)))
        o_c = jnp.einsum('bhck,bhkv->bhcv', q_c.astype(jnp.float32), state)
        return state, o_c

    s0 = jnp.zeros((bsz, heads, dk, dv), jnp.float32)
    _, o = lax.scan(step, s0, (qc, kc, vc, ac))
    return o.transpose(1, 0, 3, 2, 4).reshape(bsz, seq, heads, dv)


def setup_inputs(seed: int = 0) -> dict:
    key = jax.random.key(seed)
    ks = jax.random.split(key, 24)
    f32 = jnp.float32

    def w(k, shape, fan_in):
        return jax.random.normal(k, shape, f32) * (fan_in ** -0.5)

    def gain(k, shape):
        return 1.0 + 0.02 * jax.random.normal(k, shape, f32)

    def bias(k, shape, s=0.01):
        return s * jax.random.normal(k, shape, f32)

    return {
        "x": jax.random.normal(ks[0], (BATCH, SEQ, D_MODEL), f32),
        "norm_mix": gain(ks[1], (DEPTH, D_MODEL)),
        "w_in": w(ks[2], (DEPTH, D_MODEL, IN_DIM), D_MODEL),
        "b_merge": bias(ks[3], (DEPTH, N_BRANCHES * D_MODEL), 0.1),
        "conv_dw": w(ks[4], (DEPTH, CONV_WIDTH, CONV_DIM), CONV_WIDTH),
        "conv_dw_b": bias(ks[5], (DEPTH, CONV_DIM)),
        "conv_ln_g": gain(ks[6], (DEPTH, CONV_DIM)),
        "conv_ln_b": bias(ks[7], (DEPTH, CONV_DIM)),
        "w_conv_out": w(ks[8], (DEPTH, CONV_DIM, D_MODEL), CONV_DIM),
        "w_gk2": w(ks[9], (DEPTH, GATE_RANK, GLA_KEY_DIM), GATE_RANK),
        "b_gk": bias(ks[10], (DEPTH, GLA_KEY_DIM), 0.1),
        "gla_norm": gain(ks[11], (DEPTH, GLA_HEADS, GLA_HEAD_V)),
        "w_gla_out": w(ks[12], (DEPTH, GLA_VAL_DIM, D_MODEL), GLA_VAL_DIM),
        "w_out": w(ks[13], (DEPTH, D_MODEL, D_MODEL), D_MODEL),
        "norm_ffn": gain(ks[14], (DEPTH, D_MODEL)),
        "w_up": w(ks[15], (DEPTH, D_MODEL, 2 * D_FF), D_MODEL),
        "ffn_dw": w(ks[16], (DEPTH, FFN_CONV_WIDTH, 2 * D_FF), FFN_CONV_WIDTH),
        "ffn_dw_b": bias(ks[17], (DEPTH, 2 * D_FF)),
        "w_down": w(ks[18], (DEPTH, D_FF, D_MODEL), D_FF),
        "norm_final": gain(ks[19], (D_MODEL,)),
    }


def reference(x, norm_mix, w_in, b_merge, conv_dw, conv_dw_b, conv_ln_g, conv_ln_b,
              w_conv_out, w_gk2, b_gk, gla_norm, w_gla_out, w_out, norm_ffn, w_up,
              ffn_dw, ffn_dw_b, w_down, norm_final):
    bsz, seq, _ = x.shape
    split_idx = [int(i) for i in np.cumsum(IN_SPLITS)[:-1]]
    for l in range(DEPTH):
        h = rmsnorm(x, norm_mix[l])
        proj = h @ w_in[l].astype(x.dtype)
        c_val, c_gate, q, k, v, g_out, g_low, m_logits = jnp.split(proj, split_idx, axis=-1)

        a = c_val * jax.nn.sigmoid(c_gate)
        a = causal_dwconv(a, conv_dw[l], conv_dw_b[l])
        a = jax.nn.silu(layernorm(a, conv_ln_g[l], conv_ln_b[l]))
        branch_a = a @ w_conv_out[l].astype(x.dtype)

        gk = (g_low @ w_gk2[l].astype(x.dtype) + b_gk[l].astype(x.dtype)).astype(jnp.float32)
        log_a = jax.nn.log_sigmoid(gk) / GATE_TEMP
        qh = q.reshape(bsz, seq, GLA_HEADS, GLA_HEAD_K) * (GLA_HEAD_K ** -0.5)
        kh = k.reshape(bsz, seq, GLA_HEADS, GLA_HEAD_K)
        vh = v.reshape(bsz, seq, GLA_HEADS, GLA_HEAD_V)
        ah = log_a.reshape(bsz, seq, GLA_HEADS, GLA_HEAD_K)
        o = gla_chunk_causal(qh, kh, vh, ah)
        o = rmsnorm(o, gla_norm[l]).astype(x.dtype)
        o = o.reshape(bsz, seq, GLA_VAL_DIM) * jax.nn.silu(g_out)
        branch_b = o @ w_gla_out[l].astype(x.dtype)

        gates = jax.nn.sigmoid(m_logits + b_merge[l].astype(x.dtype))
        g_a, g_b = jnp.split(gates, N_BRANCHES, axis=-1)
        x = x + (g_a * branch_a + g_b * branch_b) @ w_out[l].astype(x.dtype)

        h = rmsnorm(x, norm_ffn[l])
        u = causal_dwconv(h @ w_up[l].astype(x.dtype), ffn_dw[l], ffn_dw_b[l])
        u_gate, u_val = jnp.split(u, 2, axis=-1)
        x = x + (jax.nn.silu(u_gate) * u_val) @ w_down[l].astype(x.dtype)
    return rmsnorm(x, norm_final)
```

```python
import contextlib
import numpy as np
import concourse.bass as bass
import concourse.mybir as mybir
from concourse.bass_utils import run_bass_kernel_spmd

F32 = mybir.dt.float32
BF16 = mybir.dt.bfloat16
AF = mybir.ActivationFunctionType
ALU = mybir.AluOpType

P = 128
D = 1024
KC = 8
TW = 512
NB = 4
IN_DIM = 7184
DFF = 2816
EPS = 1e-6
O_CVAL, O_CGATE, O_Q, O_K, O_V, O_GO, O_GL, O_M = 0, 1024, 2048, 2560, 3072, 4096, 5120, 5136

C_GMIX, C_GFFN, C_BM, C_CB, C_LNG, C_LNB, C_GN, C_FB = 0, 8, 16, 32, 40, 48, 56, 64
C_C31, C_F3, C_FLAG, C_EPS, C_ONE, NCOLS = 108, 356, 488, 489, 490, 496

NSLOT = 5
ENGS = ("pe", "act", "dve", "pool", "sp")


class Res:
    __slots__ = ("name", "last_w", "readers", "const", "sem", "semval")

    def __init__(self, name, const=False):
        self.name = name
        self.last_w = None
        self.readers = {}
        self.const = const
        self.sem = None
        self.semval = 0


class Builder:
    def __init__(self, nc, stack, dry=False):
        self.nc = nc
        self.stack = stack
        self.dry = dry
        self.q = {e: [] for e in ENGS}
        self.cnt = {e: 0 for e in ENGS}
        self.waited = {e: {} for e in ENGS}
        self.sems = {}
        for e in ENGS:
            self.sems[e] = None if dry else stack.enter_context(nc.semaphore("sem_" + e))
        self.nres = 0

    def res(self, name, const=False):
        return Res(name, const)

    def _dma_sem(self, r):
        if r.sem is None:
            key = "d%d" % len(self.sems)
            self.sems[key] = None if self.dry else self.stack.enter_context(self.nc.semaphore(key))
            r.sem = key
        return r.sem

    def _waits(self, eng, reads, writes):
        need = {}

        def add(ev, own_ok):
            if ev is None:
                return
            k, v = ev
            if k == eng:
                if eng == "pe" or not own_ok:
                    return
            if need.get(k, 0) < v:
                need[k] = v

        for r in reads:
            add(r.last_w, True)
        for r in writes:
            add(r.last_w, True)
            for k, v in r.readers.items():
                add((k, v), False)
        out = []
        w = self.waited[eng]
        for k, v in need.items():
            if w.get(k, 0) < v:
                w[k] = v
                out.append((k, v))
        return out

    def _commit(self, ev, reads, writes):
        for r in reads:
            if not r.const:
                k, v = ev
                if r.readers.get(k, 0) < v:
                    r.readers[k] = v
        for r in writes:
            r.last_w = ev
            r.readers = {}

    def op(self, eng, fn, reads=(), writes=()):
        waits = self._waits(eng, reads, writes)
        self.cnt[eng] += 1
        ev = (eng, self.cnt[eng])
        self.q[eng].append((waits, fn, None))
        self._commit(ev, reads, writes)

    def dma(self, eng, fn, semres, reads=(), writes=()):
        waits = self._waits(eng, reads, writes)
        key = self._dma_sem(semres)
        holder = [0]
        self.q[eng].append((waits, fn, (key, holder)))
        return key, holder

    def run(self, eng, e):
        for waits, fn, dm in self.q[eng]:
            for k, v in waits:
                e.wait_ge(self.sems[k], v)
            r = fn(e)
            if dm is None:
                r.then_inc(self.sems[eng], 1)
            else:
                for ins in r:
                    ins.then_inc(self.sems[dm[0]], 16)


def slab_specs():
    S = {}
    for s in range(4):
        S["GLU%d" % s] = [("w_in", 0, 8, O_CVAL + 256 * s, 256, 0), ("w_in", 0, 8, O_CGATE + 256 * s, 256, 256)]
    S["K"] = [("w_in", 0, 8, O_K, 512, 0)]
    S["V0"] = [("w_in", 0, 8, O_V, 512, 0)]
    S["V1"] = [("w_in", 0, 8, O_V + 512, 512, 0)]
    S["Q"] = [("w_in", 0, 8, O_Q, 512, 0)]
    S["GO0"] = [("w_in", 0, 8, O_GO, 512, 0)]
    S["GO1"] = [("w_in", 0, 8, O_GO + 512, 512, 0)]
    for s in range(4):
        S["M%d" % s] = [("w_in", 0, 8, O_M + 512 * s, 512, 0)]
    for s in range(2):
        S["CO%d" % s] = [("w_conv_out", 0, 8, 512 * s, 512, 0)]
        S["GLO%d" % s] = [("w_gla_out", 0, 8, 512 * s, 512, 0)]
        S["WO%d" % s] = [("w_out", 0, 8, 512 * s, 512, 0)]
    for s in range(11):
        S["UP%d" % s] = [("w_up", 0, 8, 256 * s, 256, 0), ("w_up", 0, 8, DFF + 256 * s, 256, 256)]
    for hf in range(2):
        for pt in range(3):
            nk = 8 if pt < 2 else 6
            S["DN%d" % (hf * 3 + pt)] = [("w_down", 8 * pt * 128, nk, 512 * hf, 512, 0)]
    for c in range(8):
        S["D31_%d" % c] = "diag"
    for f in range(6):
        S["DF%d" % f] = "diag"
    return S


def build_program(npre, nmain):
    NTOK = (npre + nmain) * TW
    NOUT = (nmain - 1) * TW
    nc = bass.Bass("TRN2", target_bir_lowering=False)

    def din(name, shape):
        return nc.dram_tensor(name, shape, F32, kind="ExternalInput").ap()

    x_d = din("x", [NTOK, D])
    W = {
        "w_in": din("w_in", [D, IN_DIM]),
        "w_conv_out": din("w_conv_out", [D, D]),
        "w_gla_out": din("w_gla_out", [D, D]),
        "w_out": din("w_out", [D, D]),
        "w_up": din("w_up", [D, 2 * DFF]),
        "w_down": din("w_down", [DFF, D]),
    }
    cols_d = din("cols", [P, NCOLS])
    gfin_d = din("gfin", [P, D])
    cst_d = din("cst", [P, 384])
    wgk_d = din("wgk", [17, 512])
    mt_d = din("mtot", [P, 2])
    out_d = nc.dram_tensor("out", [NOUT, D], F32, kind="ExternalOutput").ap()

    specs = slab_specs()
    names = list(specs.keys())
    sidx = {n: i for i, n in enumerate(names)}
    scr = nc.dram_tensor("wscr", [len(names), P, 4096], BF16, kind="Internal").ap()

    with contextlib.ExitStack() as stack:

        def sb(name, shape, dt):
            return stack.enter_context(nc.sbuf_tensor("sb_" + name, shape, dt))

        cols = sb("cols", [P, NCOLS], F32)
        gfin = sb("gfin", [P, D], F32)
        cst = sb("cst", [P, 384], F32)
        mt = sb("mtot", [P, 2], F32)
        identb = sb("identb", [P, P], BF16)
        onesb = sb("onesb", [P, P], BF16)
        mrevb = sb("mrevb", [P, P], BF16)
        mtb = sb("mtb", [P, 2], BF16)
        wgk = sb("wgk", [17, 512], BF16)
        wgl = sb("wgl", [P, KC, 16], BF16)
        xtm = [sb("xtm%d" % i, [P, NB, D], F32) for i in range(2)]
        arA = sb("arenaA", [P, 24 * 512], BF16)
        arB = sb("arenaB", [P, 16 * 512], BF16)
        arC = sb("arenaC", [P, 16 * 512], BF16)
        cT = arA[:, 0:16 * 512].bitcast(F32).rearrange("p (k c) -> p k c", c=TW)
        sT = arA[:, 16 * 512:24 * 512].rearrange("p (k c) -> p k c", c=TW)
        htm = sb("htm", [P, NB, D], BF16)
        gT = arA[:, 0:22 * 512].rearrange("p (k c) -> p k c", c=TW)
        oT = arB[:, :].bitcast(F32).rearrange("p (k c) -> p k c", c=TW)
        h2T = arB[:, 0:8 * 512].rearrange("p (k c) -> p k c", c=TW)
        mixT = arB[:, 8 * 512:16 * 512].rearrange("p (k c) -> p k c", c=TW)
        vv = arC[:, 0:8 * 512].rearrange("p (b f) -> p b f", f=D)
        kd = arC[:, 8 * 512:12 * 512].rearrange("p (b f) -> p b f", f=512)
        qT = arC[:, 12 * 512:16 * 512].rearrange("p (h c) -> p h c", c=TW)
        gab = arC[:, :].rearrange("p (k c) -> p k c", c=TW)
        vv1 = arA[:, 0:8 * 512].rearrange("p (b f) -> p b f", f=D)
        kd1 = arA[:, 8 * 512:12 * 512].rearrange("p (b f) -> p b f", f=512)
        vvS, kdS = [vv, vv1], [kd, kd1]
        hT = sb("hT", [P, KC, TW], BF16)
        aT = sb("aT", [P, KC, 32 + TW], BF16)
        glT = sb("glT", [32, TW], BF16)
        NTF = 8
        tmpf = [sb("tmpf%d" % i, [P, TW], F32) for i in range(NTF)]
        dtotS = [sb("dtot0", [P, 32], F32), sb("dtot1", [P, 32], F32)]
        S_f = sb("S_f", [P, 4, 256], F32)
        S_b = sb("S_b", [P, 4, 256], BF16)
        sg = sb("sg", [P, KC, TW], BF16)
        tmpb = [sb("tmpb%d" % i, [P, TW], BF16) for i in range(4)]
        ub = [sb("ub%d" % i, [P, TW + 2], BF16) for i in range(3)]
        uhalo = sb("uhalo", [P, 44, 2], BF16)
        sgt = sb("sgt", [P, 2, TW], BF16)
        junk = sb("junk", [P, D], BF16)
        ssq = sb("ssq", [P, 8], F32)
        rstd = sb("rstd", [P, 8], F32)
        NCOLS_OF = {}
        slabs = [sb("slab%d" % i, [P, 4096], BF16) for i in range(NSLOT)]
        banks = [stack.enter_context(nc.psum_tensor("bank%d" % i, [P, 512], F32)) for i in range(8)]
        banks_bf = [b.bitcast(BF16) for b in banks]

        def emit(B, seq):
            dry = seq is None
            rec = []
            R = {}

            def rs(name, const=False):
                if name not in R:
                    R[name] = B.res(name, const)
                return R[name]

            r_bank = [rs("bank%d" % i) for i in range(8)]
            r_slab = [rs("slab%d" % i) for i in range(NSLOT)]
            r_scr = [rs("scr%d" % i) for i in range(len(names))]
            gA = [rs("gA%d" % i) for i in range(24)]
            gB = [rs("gB%d" % i) for i in range(16)]
            gC = [rs("gC%d" % i) for i in range(16)]
            r_cT = [[gA[2 * k], gA[2 * k + 1]] for k in range(KC)]
            r_sT = [[gA[16 + k]] for k in range(KC)]
            r_htm = [[rs("htm%d" % b)] for b in range(NB)]
            r_gT = [[gA[j]] for j in range(22)]
            r_oT = [[gB[2 * k], gB[2 * k + 1]] for k in range(KC)]
            r_oT_all = list(gB)
            r_h2T = [[gB[k]] for k in range(KC)]
            r_mixT = [[gB[8 + k]] for k in range(KC)]
            r_vvS = [[[gC[2 * b], gC[2 * b + 1]] for b in range(NB)], [[gA[2 * b], gA[2 * b + 1]] for b in range(NB)]]
            r_kdS = [[[gC[8 + b]] for b in range(NB)], [[gA[8 + b]] for b in range(NB)]]
            r_qT = [[gC[12 + h]] for h in range(4)]
            r_gab = [[gC[k]] for k in range(16)]
            r_hT = [[rs("hT%d" % k)] for k in range(KC)]
            r_aT = [[rs("aT%d" % k)] for k in range(KC)]
            r_sg = [[rs("sg%d" % k)] for k in range(KC)]
            r_tmpf = [rs("tmpf%d" % i) for i in range(NTF)]
            r_tmpb = [rs("tmpb%d" % i) for i in range(4)]
            r_ub = [rs("ub%d" % i) for i in range(3)]
            r_sgt = [rs("sgt0"), rs("sgt1")]
            r_xtm = [rs("xtm0"), rs("xtm1")]
            r_glT, r_S, r_Sb, r_uhalo = rs("glT"), rs("S_f"), rs("S_b"), rs("uhalo")
            r_dtotS, r_ssq, r_rstd = [rs("dtot0"), rs("dtot1")], rs("ssq"), rs("rstd")

            def flat(lst):
                out = []
                for x_ in lst:
                    out += x_
                return out

            bank_rr = [0]
            pinned = set()
            C0 = [0]

            def nbank():
                while True:
                    i = bank_rr[0] % 8
                    bank_rr[0] += 1
                    if i not in pinned:
                        return i

            rrc = {"tf": 0, "tb": 0, "ub": 0}

            def nxt(key, n):
                v = rrc[key] % n
                rrc[key] += 1
                return v

            def colap(c):
                return cols[:, c:c + 1]

            def dma_op(eng, fn, semres, n, reads=(), writes=()):
                B.dma(eng, fn, semres, reads=reads, writes=writes)
                semres.semval += 16 * n
                ev = (semres.sem, semres.semval)
                B._commit(ev, reads, writes)
                return ev

            r_cols, r_gfin, r_cst, r_mt = rs("cols", True), rs("gfin", True), rs("cst", True), rs("mtot", True)
            r_wgl, r_identb, r_onesb, r_wgk = rs("wgl", True), rs("identb", True), rs("onesb", True), rs("wgk", True)
            dma_op("sp", lambda e: [e.dma_start(out=cols[:, :], in_=cols_d[:, :])], r_cols, 1, writes=[r_cols])
            dma_op("sp", lambda e: [e.dma_start(out=cst[:, :], in_=cst_d[:, :])], r_cst, 1, writes=[r_cst])
            dma_op("sp", lambda e: [e.dma_start(out=gfin[:, :], in_=gfin_d[:, :])], r_gfin, 1, writes=[r_gfin])
            dma_op("sp", lambda e: [e.dma_start(out=mt[:, :], in_=mt_d[:, :])], r_mt, 1, writes=[r_mt])
            dma_op("sp", lambda e: [e.dma_start(out=tmpf[0][0:17, :], in_=wgk_d[:, :])], r_tmpf[0], 1,
                   writes=[r_tmpf[0]])
            dma_op("pool", lambda e: [e.dma_start(
                out=wgl[:, :, :], in_=W["w_in"][:, O_GL:O_GL + 16].rearrange("(kc p) c -> p kc c", p=P))],
                r_wgl, 1, writes=[r_wgl])
            B.op("dve", lambda e: e.tensor_copy(out=identb[:, :], in_=cst[:, 0:128]), reads=[r_cst], writes=[r_identb])
            B.op("dve", lambda e: e.tensor_copy(out=onesb[:, :], in_=cst[:, 256:384]), reads=[r_cst], writes=[r_onesb])
            r_mrevb, r_mtb = rs("mrevb", True), rs("mtb", True)
            B.op("dve", lambda e: e.tensor_copy(out=mrevb[:, :], in_=cst[:, 128:256]), reads=[r_cst], writes=[r_mrevb])
            B.op("dve", lambda e: e.tensor_copy(out=mtb[:, :], in_=mt[:, :]), reads=[r_mt], writes=[r_mtb])
            B.op("dve", lambda e: e.tensor_copy(out=wgk[:, :], in_=tmpf[0][0:17, :]), reads=[r_tmpf[0]],
                 writes=[r_wgk])
            B.op("dve", lambda e: e.memset(glT[:, :], 1.0), writes=[r_glT])
            B.op("dve", lambda e: e.memset(aT[:, :, :], 0.0), writes=flat(r_aT))
            B.op("dve", lambda e: e.memset(S_f[:, :, :], 0.0), writes=[r_S])
            B.op("dve", lambda e: e.memset(uhalo[:, :, :], 0.0), writes=[r_uhalo])

            def convert(name):
                i = sidx[name]
                pieces = specs[name]
                dstv = scr[i].rearrange("p (kc c) -> p kc c", c=512)

                def fn(e, pieces=pieces, dstv=dstv):
                    out = []
                    for (src, r0, nk, c0, ncol, coff) in pieces:
                        s_ap = W[src][r0:r0 + nk * P, c0:c0 + ncol].rearrange("(kc p) c -> p kc c", p=P)
                        out.append(e.dma_start(out=dstv[:, 0:nk, coff:coff + ncol], in_=s_ap))
                    return out
                dma_op("pool", fn, r_scr[i], len(pieces), writes=[r_scr[i]])

            slot_rr = [0]

            dhalf = [0]

            dhalf = [0]

            def build_diag(name, runs):
                i = sidx[name]
                h = dhalf[0] % 2
                dhalf[0] += 1
                base = h * 4096
                res_h = gB[8 * h:8 * h + 8]
                nblk = 0
                for (m0, col0, cnt) in runs:
                    nblk = max(nblk, m0 + cnt)
                    ov = arB[:, base + m0 * P:base + (m0 + cnt) * P].rearrange("p (j m) -> p j m", m=P)
                    ia = identb[:, :].unsqueeze(1).broadcast_to([P, cnt, P])
                    wa = cols[:, col0:col0 + cnt].unsqueeze(2).broadcast_to([P, cnt, P])
                    B.op("dve", lambda e, ov=ov, ia=ia, wa=wa: e.tensor_tensor(out=ov, in0=ia, in1=wa, op=ALU.mult),
                         reads=[r_identb, r_cols], writes=res_h)
                n = nblk * P
                dma_op("sp", lambda e, i=i, n=n, base=base: [e.dma_start(out=scr[i][:, 0:n],
                                                                         in_=arB[:, base:base + n])],
                       r_scr[i], 1, reads=res_h, writes=[r_scr[i]])

            order1 = ["K", "V0", "V1"] + ["GLU%d" % s for s in range(4)] + ["Q", "GO0", "GO1"]
            order2 = (["M%d" % s for s in range(4)] + ["CO0", "CO1", "GLO0", "GLO1", "WO0", "WO1"] +
                      ["UP%d" % s for s in range(11)] + ["DN%d" % s for s in range(6)])
            for n_ in order1[:3]:
                convert(n_)
            conv_jobs = order1[3:] + order2

            def df_runs(f):
                out = []
                for ss in (2 * f, 2 * f + 1):
                    if ss > 10:
                        break
                    mb = (ss - 2 * f) * 12
                    out.append((mb, C_F3 + (2 * ss) * 3, 6))
                    out.append((mb + 6, C_F3 + (22 + 2 * ss) * 3, 6))
                return out
            diag_jobs = ([("D31_%d" % c, [(0, C_C31 + c * 31, 31)]) for c in range(8)] +
                         [("DF%d" % f, df_runs(f)) for f in range(6)])

            def lazy_diag(n):
                for _ in range(n):
                    if diag_jobs:
                        nm, ci_ = diag_jobs.pop(0)
                        build_diag(nm, ci_)
                for _ in range(n * 4 if n < 99 else 999):
                    if conv_jobs:
                        convert(conv_jobs.pop(0))

            def ncols_of(nm):
                if nm.startswith("D31"):
                    return 31 * P
                if nm.startswith("DF"):
                    return (24 if nm != "DF5" else 12) * P
                if nm in ("DN2", "DN5"):
                    return 6 * 512
                return 4096
            st = {"issued": 0, "pos": 0}
            slot_of = {}
            live = set()

            def next_slab(name):
                pos = st["pos"]
                st["pos"] += 1
                live.add(pos)
                if dry:
                    rec.append(name)
                    return slabs[0], r_slab[0], pos
                assert seq[pos] == name, (seq[pos], name)
                lim = min(len(seq), min(live) + NSLOT)
                assert pos < lim, "too many live slabs"
                while st["issued"] < lim:
                    nm = seq[st["issued"]]
                    if nm in conv_jobs:
                        conv_jobs.remove(nm)
                        convert(nm)
                    for dj in list(diag_jobs):
                        if dj[0] == nm:
                            diag_jobs.remove(dj)
                            build_diag(dj[0], dj[1])
                    ncols = ncols_of(nm)
                    i = sidx[nm]
                    slot = slot_rr[0] % NSLOT
                    slot_rr[0] += 1
                    sl = slabs[slot]
                    dma_op("sp", lambda e, sl=sl, i=i, ncols=ncols: [e.dma_start(out=sl[:, 0:ncols],
                                                                                  in_=scr[i][:, 0:ncols])],
                           r_slab[slot], 1, reads=[r_scr[i]], writes=[r_slab[slot]])
                    slot_of[st["issued"]] = slot
                    st["issued"] += 1
                s_ = slot_of[pos]
                return slabs[s_], r_slab[s_], pos

            def release(h):
                live.discard(h)

            def load_x(ti):
                slot = ti % 2
                t0 = ti * TW
                src = x_d[t0:t0 + TW, :].rearrange("(b p) f -> p b f", p=P)
                dma_op("sp", lambda e, src=src, slot=slot: [e.dma_start(out=xtm[slot][:, :, :], in_=src)],
                       r_xtm[slot], 1, writes=[r_xtm[slot]])

            def rms_stats(slot):
                for b in range(NB):
                    B.op("act", lambda e, b=b, slot=slot: e.activation(
                        out=junk[:, :], in_=xtm[slot][:, b, :], func=AF.Square, accum_out=ssq[:, b:b + 1]),
                        reads=[r_xtm[slot]], writes=[r_ssq])
                B.op("act", lambda e: e.activation(out=rstd[:, 4:8], in_=ssq[:, 0:4], func=AF.Ln,
                                                    bias=colap(C_EPS), scale=1.0 / D),
                     reads=[r_ssq, r_cols], writes=[r_rstd])
                B.op("act", lambda e: e.activation(out=rstd[:, 0:4], in_=rstd[:, 4:8], func=AF.Exp, scale=-0.5),
                     reads=[r_rstd], writes=[r_rstd])

            def normA(slot):
                rms_stats(slot)
                for b in range(NB):
                    B.op("dve", lambda e, b=b, slot=slot: e.tensor_scalar(
                        out=htm[:, b, :], in0=xtm[slot][:, b, :], scalar1=rstd[:, b:b + 1], scalar2=None,
                        op0=ALU.mult), reads=[r_xtm[slot], r_rstd], writes=r_htm[b])

            def normB(dstT, r_dstT, gcol):
                for fc in range(KC):
                    bi = nbank()

                    def tr(e, fc=fc, bi=bi):
                        last = None
                        for b in range(NB):
                            last = e.transpose(out=banks_bf[bi][:, b * P:(b + 1) * P],
                                               in_=htm[:, b, fc * P:(fc + 1) * P], identity=identb[:, :])
                        return last
                    B.op("pe", tr, reads=flat(r_htm) + [r_identb], writes=[r_bank[bi]])
                    B.op("act", lambda e, fc=fc, bi=bi: e.activation(
                        out=dstT[:, fc, :], in_=banks_bf[bi][:, 0:TW], func=AF.Identity, scale=colap(gcol + fc)),
                        reads=[r_bank[bi], r_cols], writes=r_dstT[fc])

            def mm_feat(bi, slab, r_sl, col0, rhs_list, r_rhs):
                sv = slab[:, :].rearrange("p (kc c) -> p kc c", c=512)
                n = len(rhs_list)
                c0 = C0[0]

                def fn(e):
                    last = None
                    for i, (k, ap) in enumerate(rhs_list):
                        last = e.matmul(banks[bi][:, c0:TW], lhsT=sv[:, k, col0:col0 + P], rhs=ap,
                                        start=(i == 0), stop=(i == n - 1))
                    return last
                B.op("pe", fn, reads=[r_sl] + list(r_rhs), writes=[r_bank[bi]])

            def mm_tok(bi, lhs_list, r_act, slab, r_sl):
                sv = slab[:, :].rearrange("p (kc c) -> p kc c", c=512)
                n = len(lhs_list)

                def fn(e):
                    last = None
                    for i, (k, ap) in enumerate(lhs_list):
                        last = e.matmul(banks[bi][:, :], lhsT=ap, rhs=sv[:, k, :], start=(i == 0),
                                        stop=(i == n - 1))
                    return last
                B.op("pe", fn, reads=[r_sl] + list(r_act), writes=[r_bank[bi]])

            def hT_rhs():
                return [(k, hT[:, k, C0[0]:TW]) for k in range(KC)]

            def hT_blk(b):
                return [(k, hT[:, k, b * P:(b + 1) * P]) for k in range(KC)]

            def rstd_from(dst, r_dst, src_ap, r_src, scale):
                B.op("act", lambda e: e.activation(out=dst, in_=src_ap, func=AF.Ln, bias=colap(C_EPS), scale=scale),
                     reads=list(r_src) + [r_cols], writes=[r_dst])
                B.op("act", lambda e: e.activation(out=dst, in_=dst, func=AF.Exp, scale=-0.5),
                     reads=[r_dst], writes=[r_dst])

            def front_normA(ti):
                normA(ti % 2)

            def front_normB():
                normB(hT, r_hT, C_GMIX)

            def front_glu(c0g=0):
                c0 = c0g
                old_c0 = C0[0]
                C0[0] = c0g
                for s in range(4):
                    sl, r_sl, hd = next_slab("GLU%d" % s)
                    for j in range(2):
                        cc = 2 * s + j
                        ba, bg = nbank(), nbank()
                        mm_feat(ba, sl, r_sl, j * P, hT_rhs(), flat(r_hT))
                        mm_feat(bg, sl, r_sl, 256 + j * P, hT_rhs(), flat(r_hT))
                        t = nxt("tb", 4)
                        B.op("act", lambda e, bg=bg, t=t: e.activation(out=tmpb[t][:, c0:TW], in_=banks[bg][:, c0:TW],
                                                                         func=AF.Sigmoid),
                             reads=[r_bank[bg]], writes=[r_tmpb[t]])
                        B.op("dve", lambda e, ba=ba, t=t, cc=cc: e.tensor_tensor(
                            out=aT[:, cc, 32 + c0:32 + TW], in0=banks[ba][:, c0:TW], in1=tmpb[t][:, c0:TW], op=ALU.mult),
                            reads=[r_bank[ba], r_tmpb[t]], writes=r_aT[cc])
                    release(hd)
                C0[0] = old_c0

            def proj_phase(main, vset=0, filler=None):
                vv_, kd_, dtot_ = vvS[vset], kdS[vset], dtotS[vset]
                r_vv, r_kd, r_dtot = r_vvS[vset], r_kdS[vset], r_dtotS[vset]

                def fill(n):
                    if filler is not None:
                        for _ in range(n):
                            if filler:
                                filler.pop(0)()

                def vgrp(vs, b, sl, r_sl):
                    bi = nbank()
                    mm_tok(bi, hT_blk(b), flat(r_hT), sl, r_sl)
                    B.op("act", lambda e, bi=bi, b=b, vs=vs: e.activation(
                        out=vv_[:, b, vs * 512:(vs + 1) * 512], in_=banks[bi][:, :], func=AF.Copy),
                        reads=[r_bank[bi]], writes=r_vv[b])

                def qgrp(h, sl, r_sl):
                    c0 = C0[0]
                    bi = nbank()
                    mm_feat(bi, sl, r_sl, h * P, hT_rhs(), flat(r_hT))
                    B.op("act", lambda e, bi=bi, h=h: e.activation(out=qT[:, h, c0:TW], in_=banks[bi][:, c0:TW],
                                                                    func=AF.Copy, scale=float(128 ** -0.5)),
                         reads=[r_bank[bi]], writes=r_qT[h])

                def gogrp(gc, j, sl, r_sl):
                    c0 = C0[0]
                    bi = nbank()
                    mm_feat(bi, sl, r_sl, j * P, hT_rhs(), flat(r_hT))
                    B.op("act", lambda e, bi=bi, gc=gc: e.activation(out=sg[:, gc, c0:TW], in_=banks[bi][:, c0:TW],
                                                                      func=AF.Silu),
                         reads=[r_bank[bi]], writes=r_sg[gc])

                bi = nbank()

                def fn(e, bi=bi):
                    last = None
                    for k in range(KC):
                        last = e.matmul(banks[bi][0:16, :], lhsT=wgl[:, k, :], rhs=hT[:, k, :],
                                        start=(k == 0), stop=(k == KC - 1))
                    return last
                B.op("pe", fn, reads=[r_wgl] + flat(r_hT), writes=[r_bank[bi]])
                B.op("act", lambda e, bi=bi: e.activation(out=glT[0:16, :], in_=banks[bi][0:16, :], func=AF.Copy),
                     reads=[r_bank[bi]], writes=[r_glT])
                fill(2)
                slV0, r_slV0, hV0 = next_slab("V0")
                for b in range(NB):
                    vgrp(0, b, slV0, r_slV0)
                release(hV0)
                slabK, r_slK, hK = next_slab("K")
                bt = nbank()
                pinned.add(bt)
                slV1, r_slV1, hV1 = next_slab("V1")
                vdone = [0]
                for pair in range(2):
                    blks = (2 * pair, 2 * pair + 1)
                    las, Es = {}, {}
                    for b in blks:
                        bi = nbank()
                        B.op("pe", lambda e, b=b, bi=bi: e.matmul(banks[bi][:, :], lhsT=glT[0:17, b * P:(b + 1) * P],
                                                                  rhs=wgk[0:17, :], start=True, stop=True),
                             reads=[r_glT, r_wgk], writes=[r_bank[bi]])
                        f_e, f_la = nxt("tf", NTF), nxt("tb", 4)
                        las[b] = f_la
                        B.op("act", lambda e, bi=bi, f_e=f_e: e.activation(out=tmpf[f_e][:, :], in_=banks[bi][:, :],
                                                                            func=AF.Exp, scale=-1.0),
                             reads=[r_bank[bi]], writes=[r_tmpf[f_e]])
                        B.op("act", lambda e, f_e=f_e, f_la=f_la: e.activation(
                            out=tmpb[f_la][:, :], in_=tmpf[f_e][:, :], func=AF.Ln, bias=colap(C_ONE), scale=1.0),
                            reads=[r_tmpf[f_e], r_cols], writes=[r_tmpb[f_la]])
                    for _ in range(2):
                        vgrp(1, vdone[0], slV1, r_slV1)
                        vdone[0] += 1
                    fill(1)
                    for b in blks:
                        f_la = las[b]
                        bi2 = nbank()
                        B.op("pe", lambda e, bi2=bi2, f_la=f_la: e.matmul(banks[bi2][:, :], lhsT=mrevb[:, :],
                                                                          rhs=tmpb[f_la][:, :], start=True, stop=True),
                             reads=[r_tmpb[f_la], r_mrevb], writes=[r_bank[bi2]])

                        def tot(e, b=b, f_la=f_la, bt=bt):
                            last = None
                            for h in range(4):
                                c0 = h * 8 + b * 2
                                last = e.matmul(banks[bt][:, c0:c0 + 2], lhsT=tmpb[f_la][:, h * P:(h + 1) * P],
                                                rhs=mtb[:, :], start=True, stop=True)
                            return last
                        B.op("pe", tot, reads=[r_tmpb[f_la], r_mtb], writes=[r_bank[bt]])
                        f_E = nxt("tf", NTF)
                        Es[b] = f_E
                        B.op("act", lambda e, bi2=bi2, f_E=f_E: e.activation(out=tmpf[f_E][:, :],
                                                                              in_=banks[bi2][:, :], func=AF.Exp),
                             reads=[r_bank[bi2]], writes=[r_tmpf[f_E]])
                    fill(1)
                    for b in blks:
                        f_E = Es[b]
                        bk = nbank()
                        mm_tok(bk, hT_blk(b), flat(r_hT), slabK, r_slK)
                        B.op("dve", lambda e, bk=bk, f_E=f_E, b=b: e.tensor_tensor(
                            out=kd_[:, b, :], in0=banks[bk][:, :], in1=tmpf[f_E][:, :], op=ALU.mult),
                            reads=[r_bank[bk], r_tmpf[f_E]], writes=r_kd[b])
                release(hK)
                release(hV1)
                B.op("act", lambda e, bt=bt: e.activation(out=dtot_[:, :], in_=banks[bt][:, 0:32], func=AF.Exp),
                     reads=[r_bank[bt]], writes=[r_dtot])
                pinned.discard(bt)
                if main:
                    sl, r_sl, hd = next_slab("Q")
                    for h in range(4):
                        qgrp(h, sl, r_sl)
                    release(hd)
                    for s in range(2):
                        sl, r_sl, hd = next_slab("GO%d" % s)
                        for j in range(4):
                            gogrp(4 * s + j, j, sl, r_sl)
                        release(hd)

            def gla_kv(c, vset=0):
                kd, vv, dtot = kdS[vset], vvS[vset], dtotS[vset]
                r_kd, r_vv, r_dtot = r_kdS[vset], r_vvS[vset], r_dtotS[vset]
                b, r0 = c // 2, (c % 2) * 64
                bkv = [nbank(), nbank()]

                def kvfn(e, b=b, r0=r0, bkv=bkv):
                    last = None
                    for h in range(4):
                        last = e.matmul(banks[bkv[h // 2]][:, (h % 2) * 256:(h % 2) * 256 + 256],
                                        lhsT=kd[r0:r0 + 64, b, h * P:(h + 1) * P],
                                        rhs=vv[r0:r0 + 64, b, h * 256:(h + 1) * 256], start=True, stop=True)
                    return last
                B.op("pe", kvfn, reads=r_kd[b] + r_vv[b], writes=[r_bank[bkv[0]], r_bank[bkv[1]]])
                for h in range(4):
                    B.op("dve", lambda e, h=h, c=c, bkv=bkv: e.scalar_tensor_tensor(
                        out=S_f[:, h, :], in0=S_f[:, h, :], scalar=dtot[:, h * 8 + c:h * 8 + c + 1],
                        in1=banks[bkv[h // 2]][:, (h % 2) * 256:(h % 2) * 256 + 256], op0=ALU.mult, op1=ALU.add),
                        reads=[r_S, r_dtot, r_bank[bkv[h // 2]]], writes=[r_S])

            def gla_sb(c):
                B.op("act", lambda e: e.activation(out=S_b[:, :, :], in_=S_f[:, :, :], func=AF.Copy),
                     reads=[r_S], writes=[r_Sb])

            def gla_o(c):
                bo = nbank()

                def ofn(e, c=c, bo=bo):
                    last = None
                    for h in range(4):
                        for hf in range(2):
                            j = 2 * h + hf
                            last = e.matmul(banks[bo][:, j * 64:(j + 1) * 64],
                                            lhsT=S_b[:, h, hf * P:(hf + 1) * P], rhs=qT[:, h, c * 64:(c + 1) * 64],
                                            start=True, stop=True)
                    return last
                B.op("pe", ofn, reads=[r_Sb] + flat(r_qT), writes=[r_bank[bo]])
                B.op("act", lambda e, c=c, bo=bo: e.activation(
                    out=oT[:, :, c * 64:(c + 1) * 64], in_=banks[bo][:, :].rearrange("p (a b) -> p a b", b=64),
                    func=AF.Copy), reads=[r_bank[bo]], writes=r_oT_all)

            NPE = 26

            def conv_dve(cc):
                c0 = C0[0]
                fa = nxt("tf", NTF)
                for j in range(NPE, 31):
                    wcol = colap(C_C31 + cc * 31 + j)
                    if j == NPE:
                        B.op("dve", lambda e, cc=cc, j=j, fa=fa, wcol=wcol: e.tensor_scalar(
                            out=tmpf[fa][:, c0:TW], in0=aT[:, cc, j + 2 + c0:j + 2 + TW], scalar1=wcol, scalar2=None,
                            op0=ALU.mult), reads=r_aT[cc] + [r_cols], writes=[r_tmpf[fa]])
                    else:
                        B.op("dve", lambda e, cc=cc, j=j, fa=fa, wcol=wcol: e.scalar_tensor_tensor(
                            out=tmpf[fa][:, c0:TW], in0=aT[:, cc, j + 2 + c0:j + 2 + TW], scalar=wcol,
                            in1=tmpf[fa][:, c0:TW], op0=ALU.mult, op1=ALU.add),
                            reads=r_aT[cc] + [r_cols, r_tmpf[fa]], writes=[r_tmpf[fa]])
                return fa

            def conv_chunk(cc, b1, b2, fa):
                c0 = C0[0]
                sl, r_sl, hd = next_slab("D31_%d" % cc)
                bi = nbank()

                def cv(e, cc=cc, bi=bi, sl=sl):
                    last = None
                    for j in range(NPE):
                        last = e.matmul(banks[bi][:, c0:TW], lhsT=sl[:, j * P:(j + 1) * P],
                                        rhs=aT[:, cc, j + 2 + c0:j + 2 + TW], start=(j == 0), stop=(j == NPE - 1))
                    return last
                B.op("pe", cv, reads=[r_sl] + r_aT[cc], writes=[r_bank[bi]])
                release(hd)
                B.op("dve", lambda e, cc=cc, bi=bi, fa=fa: e.scalar_tensor_tensor(
                    out=cT[:, cc, c0:TW], in0=banks[bi][:, c0:TW], scalar=colap(C_CB + cc), in1=tmpf[fa][:, c0:TW],
                    op0=ALU.add, op1=ALU.add), reads=[r_bank[bi], r_cols, r_tmpf[fa]], writes=r_cT[cc])
                t1, t2 = nxt("tb", 4), nxt("tb", 4)
                B.op("act", lambda e, cc=cc, t1=t1: e.activation(out=tmpb[t1][:, c0:TW], in_=cT[:, cc, c0:TW],
                                                                  func=AF.Copy),
                     reads=r_cT[cc], writes=[r_tmpb[t1]])
                B.op("act", lambda e, cc=cc, t2=t2: e.activation(out=tmpb[t2][:, c0:TW], in_=cT[:, cc, c0:TW],
                                                                  func=AF.Square),
                     reads=r_cT[cc], writes=[r_tmpb[t2]])
                return (cc, t1, t2)

            def conv_stats(args, b1, b2):
                c0 = C0[0]
                cc, t1, t2 = args
                B.op("pe", lambda e, cc=cc, t1=t1: e.matmul(banks[b1][:, c0:TW], lhsT=onesb[:, :], rhs=tmpb[t1][:, c0:TW],
                                                            start=(cc == 0), stop=(cc == KC - 1)),
                     reads=[r_onesb, r_tmpb[t1]], writes=[r_bank[b1]])
                B.op("pe", lambda e, cc=cc, t2=t2: e.matmul(banks[b2][:, c0:TW], lhsT=onesb[:, :], rhs=tmpb[t2][:, c0:TW],
                                                            start=(cc == 0), stop=(cc == KC - 1)),
                     reads=[r_onesb, r_tmpb[t2]], writes=[r_bank[b2]])

            def mgates(s):
                c0 = C0[0]
                sl, r_sl, hd = next_slab("M%d" % s)
                for j in range(4):
                    mc = 4 * s + j
                    bi = nbank()
                    mm_feat(bi, sl, r_sl, j * P, hT_rhs(), flat(r_hT))
                    B.op("act", lambda e, bi=bi, mc=mc: e.activation(out=gab[:, mc, c0:TW], in_=banks[bi][:, c0:TW],
                                                                      func=AF.Sigmoid, bias=colap(C_BM + mc)),
                         reads=[r_bank[bi], r_cols], writes=r_gab[mc])
                release(hd)

            def branch(nm, srcT, r_srcT, goff, accumulate):
                c0 = C0[0]
                for s in range(2):
                    sl, r_sl, hd = next_slab("%s%d" % (nm, s))
                    for j in range(4):
                        dc = 4 * s + j
                        bi = nbank()
                        mm_feat(bi, sl, r_sl, j * P, [(k, srcT[:, k, c0:TW]) for k in range(KC)], flat(r_srcT))
                        if not accumulate:
                            B.op("dve", lambda e, bi=bi, dc=dc: e.tensor_tensor(
                                out=mixT[:, dc, c0:TW], in0=banks[bi][:, c0:TW], in1=gab[:, goff + dc, c0:TW], op=ALU.mult),
                                reads=[r_bank[bi]] + r_gab[goff + dc], writes=r_mixT[dc])
                        else:
                            t = nxt("tb", 4)
                            B.op("dve", lambda e, bi=bi, dc=dc, t=t: e.tensor_tensor(
                                out=tmpb[t][:, c0:TW], in0=banks[bi][:, c0:TW], in1=gab[:, goff + dc, c0:TW], op=ALU.mult),
                                reads=[r_bank[bi]] + r_gab[goff + dc], writes=[r_tmpb[t]])
                            B.op("dve", lambda e, dc=dc, t=t: e.tensor_tensor(
                                out=mixT[:, dc, c0:TW], in0=mixT[:, dc, c0:TW], in1=tmpb[t][:, c0:TW], op=ALU.add),
                                reads=[r_tmpb[t]] + r_mixT[dc], writes=r_mixT[dc])
                    release(hd)

            out_events = []

            pend_kv = []

            def pre_body(ti, hoist_normA, hoist_normB, hoist_glu, lazy):
                vset = ti % 2
                hoist_normA()
                proj_phase(False, vset, pend_kv)
                hoist_normB()
                while pend_kv:
                    pend_kv.pop(0)()
                lazy()
                for c in range(8):
                    pend_kv.append(lambda c=c, vset=vset: gla_kv(c, vset))
                if ti == npre - 1:
                    while pend_kv:
                        pend_kv.pop(0)()
                    hoist_glu()

            def main_body(ti, mi, hoist_normA, hoist_normB, hoist_glu):
                halo = (mi == 0)
                slot = ti % 2
                c0 = 384 if halo else 0
                b0 = c0 // P
                C0[0] = c0
                proj_phase(True)
                b1, b2 = nbank(), nbank()
                pinned.add(b1)
                pinned.add(b2)
                pend = None
                fa_next = conv_dve(0)
                for c in range(8):
                    gla_kv(c)
                    if c * 64 >= c0:
                        gla_sb(c)
                    fa_cur = fa_next
                    if c + 1 < 8:
                        fa_next = conv_dve(c + 1)
                    cur = conv_chunk(c, b1, b2, fa_cur)
                    if pend is not None:
                        conv_stats(pend, b1, b2)
                    pend = cur
                    if c * 64 >= c0:
                        gla_o(c)
                conv_stats(pend, b1, b2)
                B.op("dve", lambda e: e.tensor_copy(out=aT[:, :, 0:32], in_=aT[:, :, TW:TW + 32]),
                     reads=flat(r_aT), writes=flat(r_aT))
                f_mu, f_var, f_rs, f_nmr = nxt("tf", NTF), nxt("tf", NTF), nxt("tf", NTF), nxt("tf", NTF)
                B.op("dve", lambda e: e.tensor_scalar(out=tmpf[f_mu][:, c0:TW], in0=banks[b1][:, c0:TW], scalar1=1.0 / D,
                                                      scalar2=None, op0=ALU.mult),
                     reads=[r_bank[b1]], writes=[r_tmpf[f_mu]])
                B.op("dve", lambda e: e.tensor_tensor(out=tmpf[f_var][:, c0:TW], in0=tmpf[f_mu][:, c0:TW],
                                                      in1=tmpf[f_mu][:, c0:TW], op=ALU.mult),
                     reads=[r_tmpf[f_mu]], writes=[r_tmpf[f_var]])
                B.op("dve", lambda e: e.scalar_tensor_tensor(out=tmpf[f_var][:, c0:TW], in0=banks[b2][:, c0:TW],
                                                             scalar=1.0 / D, in1=tmpf[f_var][:, c0:TW], op0=ALU.mult,
                                                             op1=ALU.subtract),
                     reads=[r_bank[b2], r_tmpf[f_var]], writes=[r_tmpf[f_var]])
                pinned.discard(b1)
                pinned.discard(b2)
                rstd_from(tmpf[f_rs][:, c0:TW], r_tmpf[f_rs], tmpf[f_var][:, c0:TW], [r_tmpf[f_var]], 1.0)
                B.op("dve", lambda e: e.scalar_tensor_tensor(out=tmpf[f_nmr][:, c0:TW], in0=tmpf[f_mu][:, c0:TW], scalar=-1.0,
                                                             in1=tmpf[f_rs][:, c0:TW], op0=ALU.mult, op1=ALU.mult),
                     reads=[r_tmpf[f_mu], r_tmpf[f_rs]], writes=[r_tmpf[f_nmr]])
                o_rs = []

                def o_sq(h):
                    ts = []
                    for hf in range(2):
                        t = nxt("tb", 4)
                        ts.append(t)
                        B.op("act", lambda e, h=h, hf=hf, t=t: e.activation(out=tmpb[t][:, c0:TW],
                                                                             in_=oT[:, 2 * h + hf, c0:TW], func=AF.Square),
                             reads=r_oT[2 * h + hf], writes=[r_tmpb[t]])
                    return ts

                def o_stat(h, ts):
                    bi = nbank()

                    def stf(e, bi=bi, ts=ts):
                        e.matmul(banks[bi][:, c0:TW], lhsT=onesb[:, :], rhs=tmpb[ts[0]][:, c0:TW], start=True, stop=False)
                        return e.matmul(banks[bi][:, c0:TW], lhsT=onesb[:, :], rhs=tmpb[ts[1]][:, c0:TW], start=False,
                                        stop=True)
                    B.op("pe", stf, reads=[r_onesb, r_tmpb[ts[0]], r_tmpb[ts[1]]], writes=[r_bank[bi]])
                    f = nxt("tf", NTF)
                    rstd_from(tmpf[f][:, c0:TW], r_tmpf[f], banks[bi][:, c0:TW], [r_bank[bi]], 1.0 / 256)
                    o_rs.append(f)
                tsa, tsb = o_sq(0), o_sq(1)
                hoist_normA()
                mgates(0)
                o_stat(0, tsa)
                o_stat(1, tsb)
                tsa, tsb = o_sq(2), o_sq(3)
                mgates(1)
                o_stat(2, tsa)
                o_stat(3, tsb)
                for cc in range(KC):
                    B.op("dve", lambda e, cc=cc: e.tensor_tensor(out=cT[:, cc, c0:TW], in0=cT[:, cc, c0:TW],
                                                                 in1=tmpf[f_rs][:, c0:TW], op=ALU.mult),
                         reads=r_cT[cc] + [r_tmpf[f_rs]], writes=r_cT[cc])
                    B.op("dve", lambda e, cc=cc: e.tensor_tensor(out=cT[:, cc, c0:TW], in0=cT[:, cc, c0:TW],
                                                                 in1=tmpf[f_nmr][:, c0:TW], op=ALU.add),
                         reads=r_cT[cc] + [r_tmpf[f_nmr]], writes=r_cT[cc])
                    B.op("act", lambda e, cc=cc: e.activation(out=sT[:, cc, c0:TW], in_=cT[:, cc, c0:TW], func=AF.Silu,
                                                              bias=colap(C_LNB + cc), scale=colap(C_LNG + cc)),
                         reads=r_cT[cc] + [r_cols], writes=r_sT[cc])
                for h in range(4):
                    f = o_rs[h]
                    for hf in range(2):
                        vc = 2 * h + hf
                        B.op("dve", lambda e, vc=vc, f=f: e.tensor_tensor(out=oT[:, vc, c0:TW], in0=oT[:, vc, c0:TW],
                                                                           in1=tmpf[f][:, c0:TW], op=ALU.mult),
                             reads=r_oT[vc] + [r_tmpf[f]], writes=r_oT[vc])
                        B.op("dve", lambda e, vc=vc: e.scalar_tensor_tensor(
                            out=sg[:, vc, c0:TW], in0=oT[:, vc, c0:TW], scalar=colap(C_GN + vc), in1=sg[:, vc, c0:TW],
                            op0=ALU.mult, op1=ALU.mult), reads=r_oT[vc] + [r_cols] + r_sg[vc], writes=r_sg[vc])
                mgates(2)
                mgates(3)
                hoist_normB()
                branch("CO", sT, r_sT, 0, False)
                branch("GLO", sg, r_sg, 8, True)
                for hf in range(2):
                    sl, r_sl, hd = next_slab("WO%d" % hf)
                    for b in range(b0, NB):
                        bi = nbank()
                        lhs = [(k, mixT[:, k, b * P:(b + 1) * P]) for k in range(KC)]
                        mm_tok(bi, lhs, flat(r_mixT), sl, r_sl)
                        B.op("dve", lambda e, bi=bi, b=b, hf=hf, slot=slot: e.tensor_tensor(
                            out=xtm[slot][:, b, hf * 512:(hf + 1) * 512],
                            in0=xtm[slot][:, b, hf * 512:(hf + 1) * 512], in1=banks[bi][:, :], op=ALU.add),
                            reads=[r_bank[bi], r_xtm[slot]], writes=[r_xtm[slot]])
                    release(hd)
                normA(slot)
                hoist_glu()
                normB(h2T, r_h2T, C_GFFN)
                dfs = None
                pend3 = None

                def conv3(args):
                    s_, q, ci, u, dsl, r_dsl = args
                    bc = nbank()
                    m0 = (s_ % 2) * 12 + q * 3

                    def c3(e, bc=bc, u=u, m0=m0, dsl=dsl):
                        last = None
                        for j in range(3):
                            last = e.matmul(banks[bc][:, c0:TW], lhsT=dsl[:, (m0 + j) * P:(m0 + j + 1) * P],
                                            rhs=ub[u][:, j:j + TW], start=(j == 0), stop=(j == 2))
                        return last
                    B.op("pe", c3, reads=[r_dsl, r_ub[u]], writes=[r_bank[bc]])
                    if q < 2:
                        B.op("act", lambda e, bc=bc, q=q, ci=ci: e.activation(
                            out=sgt[:, q, :], in_=banks[bc][:, c0:TW], func=AF.Silu, bias=colap(C_FB + ci)),
                            reads=[r_bank[bc], r_cols], writes=[r_sgt[q]])
                    else:
                        fc = 2 * s_ + (q - 2)
                        B.op("dve", lambda e, bc=bc, q=q, ci=ci, fc=fc: e.scalar_tensor_tensor(
                            out=gT[:, fc, :], in0=banks[bc][:, c0:TW], scalar=colap(C_FB + ci), in1=sgt[:, q - 2, :],
                            op0=ALU.add, op1=ALU.mult), reads=[r_bank[bc], r_cols, r_sgt[q - 2]],
                            writes=r_gT[fc])

                old_dfs = None
                for s in range(11):
                    if not halo and s % 2 == 0:
                        old_dfs = dfs
                        dfs = next_slab("DF%d" % (s // 2))
                    sl, r_sl, hd = next_slab("UP%d" % s)
                    for q in range(4):
                        ci = 2 * s + q if q < 2 else 22 + 2 * s + (q - 2)
                        bu = nbank()
                        mm_feat(bu, sl, r_sl, q * P, [(k, h2T[:, k, c0:TW]) for k in range(KC)], flat(r_h2T))
                        u = nxt("ub", 3)
                        B.op("act", lambda e, bu=bu, u=u: e.activation(out=ub[u][:, 2 + c0:2 + TW], in_=banks[bu][:, c0:TW],
                                                                        func=AF.Copy),
                             reads=[r_bank[bu]], writes=[r_ub[u]])
                        if halo:
                            B.op("pool", lambda e, u=u, ci=ci: e.tensor_scalar(
                                out=uhalo[:, ci, :], in0=ub[u][:, TW:TW + 2], scalar1=colap(C_FLAG), scalar2=None,
                                op0=ALU.mult), reads=[r_ub[u], r_cols], writes=[r_uhalo])
                            continue
                        B.op("pool", lambda e, u=u, ci=ci: e.tensor_copy(out=ub[u][:, 0:2], in_=uhalo[:, ci, :]),
                             reads=[r_uhalo], writes=[r_ub[u]])
                        B.op("pool", lambda e, u=u, ci=ci: e.tensor_copy(out=uhalo[:, ci, :],
                                                                         in_=ub[u][:, TW:TW + 2]),
                             reads=[r_ub[u]], writes=[r_uhalo])
                        if pend3 is not None:
                            conv3(pend3)
                        pend3 = (s, q, ci, u, dfs[0], dfs[1])
                        if old_dfs is not None and q == 0:
                            release(old_dfs[2])
                            old_dfs = None
                    release(hd)
                if pend3 is not None:
                    conv3(pend3)
                if dfs is not None:
                    release(dfs[2])
                C0[0] = 0
                if halo:
                    return
                for hf in range(2):
                    bks = [nbank() for _ in range(NB)]
                    for b_ in bks:
                        pinned.add(b_)
                    for pt in range(3):
                        sl, r_sl, hd = next_slab("DN%d" % (hf * 3 + pt))
                        nk = 8 if pt < 2 else 6
                        sv = sl[:, :].rearrange("p (kc c) -> p kc c", c=512)

                        def dn(e, pt=pt, nk=nk, sv=sv, bks=bks):
                            last = None
                            for kk in range(nk):
                                fc = 8 * pt + kk
                                for b in range(NB):
                                    last = e.matmul(banks[bks[b]][:, :], lhsT=gT[:, fc, b * P:(b + 1) * P],
                                                    rhs=sv[:, kk, :], start=(fc == 0), stop=(fc == 21))
                            return last
                        B.op("pe", dn, reads=[r_sl] + flat(r_gT[8 * pt:8 * pt + nk]),
                             writes=[r_bank[b_] for b_ in bks])
                        release(hd)
                    for b in range(NB):
                        B.op("dve", lambda e, b=b, hf=hf, bks=bks, slot=slot: e.tensor_tensor(
                            out=xtm[slot][:, b, hf * 512:(hf + 1) * 512],
                            in0=xtm[slot][:, b, hf * 512:(hf + 1) * 512], in1=banks[bks[b]][:, :], op=ALU.add),
                            reads=[r_bank[bks[b]], r_xtm[slot]], writes=[r_xtm[slot]])
                    for b_ in bks:
                        pinned.discard(b_)
                rms_stats(slot)
                for b in range(NB):
                    B.op("dve", lambda e, b=b, slot=slot: e.scalar_tensor_tensor(
                        out=xtm[slot][:, b, :], in0=xtm[slot][:, b, :], scalar=rstd[:, b:b + 1], in1=gfin[:, :],
                        op0=ALU.mult, op1=ALU.mult), reads=[r_xtm[slot], r_rstd, r_gfin], writes=[r_xtm[slot]])
                o0 = (mi - 1) * TW
                dst = out_d[o0:o0 + TW, :].rearrange("(b p) f -> p b f", p=P)
                r_out = rs("outst%d" % slot)
                ev = dma_op("pool", lambda e, dst=dst, slot=slot: [e.dma_start(out=dst, in_=xtm[slot][:, :, :])],
                            r_out, 1, reads=[r_xtm[slot]])
                out_events.append(ev)

            ntile = npre + nmain
            x_loaded = set()

            def ensure_x(t):
                if t < ntile and t not in x_loaded:
                    x_loaded.add(t)
                    load_x(t)
            ensure_x(0)
            ensure_x(1)
            front_normA(0)
            front_normB()
            if npre == 0:
                lazy_diag(99)
                front_glu(256)
            for ti in range(ntile):
                nxt_main = (ti + 1 >= npre)
                has_next = (ti + 1 < ntile)
                if ti < npre:
                    ensure_x(ti + 2)
                ensure_x(ti + 1)

                def hoist_normA(ti=ti, has_next=has_next):
                    if has_next:
                        front_normA(ti + 1)

                def hoist_normB(has_next=has_next):
                    if has_next:
                        old = C0[0]
                        C0[0] = 0
                        front_normB()
                        C0[0] = old

                def hoist_glu(ti=ti, has_next=has_next, nxt_main=nxt_main):
                    if has_next and nxt_main:
                        front_glu(256 if ti + 1 == npre else 0)

                def lazy(ti=ti):
                    lazy_diag(99 if ti == npre - 1 else 2)
                if ti < npre:
                    pre_body(ti, hoist_normA, hoist_normB, hoist_glu, lazy)
                else:
                    main_body(ti, ti - npre, hoist_normA, hoist_normB, hoist_glu)
            if not dry:
                assert st["pos"] == len(seq), (st["pos"], len(seq))
                fin = {}
                for k, v in out_events:
                    fin[k] = max(fin.get(k, 0), v)
                B.q["sp"].append((list(fin.items()), lambda e: e.nop(), None))
            return rec

        seq_rec = emit(Builder(None, None, dry=True), None)
        B = Builder(nc, stack)
        emit(B, seq_rec)

        with nc.Block() as block:
            @block.tensor
            def _(e):
                B.run("pe", e)

            @block.scalar
            def _(e):
                B.run("act", e)

            @block.vector
            def _(e):
                B.run("dve", e)

            @block.gpsimd
            def _(e):
                B.run("pool", e)

            @block.sync
            def _(e):
                B.run("sp", e)
    return nc


def host_consts():
    cst = np.zeros((P, 384), np.float32)
    cst[:, 0:128] = np.eye(P, dtype=np.float32)
    s = np.arange(P)[:, None]
    t = np.arange(P)[None, :]
    cst[:, 128:256] = np.where((s > t) & (s // 64 == t // 64), -1.0 / 16.0, 0.0)
    cst[:, 256:384] = 1.0
    mt = np.zeros((P, 2), np.float32)
    mt[:64, 0] = -1.0 / 16.0
    mt[64:, 1] = -1.0 / 16.0
    return cst, mt


def host_cols(p, flag):
    cols = np.zeros((P, NCOLS), np.float32)

    def put(off, vec):
        v = np.asarray(vec, np.float32).reshape(-1, P).T
        cols[:, off:off + v.shape[1]] = v
    put(C_GMIX, p["norm_mix"][0])
    put(C_GFFN, p["norm_ffn"][0])
    put(C_BM, p["b_merge"][0])
    put(C_CB, p["conv_dw_b"][0])
    put(C_LNG, p["conv_ln_g"][0])
    put(C_LNB, p["conv_ln_b"][0])
    put(C_GN, p["gla_norm"][0].reshape(-1))
    put(C_FB, p["ffn_dw_b"][0])
    cd = np.asarray(p["conv_dw"][0], np.float32)
    cols[:, C_C31:C_C31 + 248] = cd.reshape(31, 8, P).transpose(2, 1, 0).reshape(P, 248)
    fd = np.asarray(p["ffn_dw"][0], np.float32)
    cols[:, C_F3:C_F3 + 132] = fd.reshape(3, 44, P).transpose(2, 1, 0).reshape(P, 132)
    cols[:, C_FLAG] = flag
    cols[:, C_EPS] = EPS
    cols[:, C_ONE] = 1.0
    return cols


def host_diag(p):
    dg = np.zeros((14, P, 4096), np.float32)
    idx = np.arange(P)
    cd = np.asarray(p["conv_dw"][0], np.float32)
    for c in range(8):
        for j in range(31):
            dg[c, idx, j * P + idx] = cd[j, c * P:(c + 1) * P]
    fd = np.asarray(p["ffn_dw"][0], np.float32)
    for f in range(6):
        m = 0
        for ss in (2 * f, 2 * f + 1):
            if ss > 10:
                break
            for q in range(4):
                ci = 2 * ss + q if q < 2 else 22 + 2 * ss + (q - 2)
                for j in range(3):
                    dg[8 + f, idx, m * P + idx] = fd[j, ci * P:(ci + 1) * P]
                    m += 1
    return dg


def make_in_maps(p, xs, flags):
    cst, mt = host_consts()
    gfin = np.ascontiguousarray(np.broadcast_to(np.asarray(p["norm_final"], np.float32)[None, :], (P, D)))
    wgk = np.concatenate([np.asarray(p["w_gk2"][0], np.float32), np.asarray(p["b_gk"][0], np.float32)[None, :]], 0)
    shared = {
        "w_in": np.ascontiguousarray(np.asarray(p["w_in"][0], np.float32)),
        "w_conv_out": np.ascontiguousarray(np.asarray(p["w_conv_out"][0], np.float32)),
        "w_gla_out": np.ascontiguousarray(np.asarray(p["w_gla_out"][0], np.float32)),
        "w_out": np.ascontiguousarray(np.asarray(p["w_out"][0], np.float32)),
        "w_up": np.ascontiguousarray(np.asarray(p["w_up"][0], np.float32)),
        "w_down": np.ascontiguousarray(np.asarray(p["w_down"][0], np.float32)),
        "gfin": gfin, "cst": cst, "mtot": mt, "wgk": np.ascontiguousarray(wgk),
    }
    maps = []
    for xc, fl in zip(xs, flags):
        m = dict(shared)
        m["x"] = np.ascontiguousarray(xc, dtype=np.float32)
        m["cols"] = host_cols(p, fl)
        maps.append(m)
    return maps


NPRE, NMAIN = 7, 9
_CACHE = {}


def kernel(**inputs):
    x = np.asarray(inputs["x"], np.float32)
    bsz, seq, _ = x.shape
    half = seq // 2
    xs, flags = [], []
    for c in range(8):
        b, hf = c // 2, c % 2
        if hf == 1:
            xs.append(x[b])
            flags.append(1.0)
        else:
            xs.append(np.concatenate([np.zeros((half, D), np.float32), x[b, :half]], 0))
            flags.append(0.0)
    maps = make_in_maps(inputs, xs, flags)
    if "nc" not in _CACHE:
        _CACHE["nc"] = build_program(NPRE, NMAIN)
    res = run_bass_kernel_spmd(_CACHE["nc"], maps, core_ids=list(range(8)))
    y = np.empty((bsz, seq, D), np.float32)
    for c in range(8):
        b, hf = c // 2, c % 2
        y[b, hf * half:(hf + 1) * half] = res.results[c]["out"]
    return y
```

```python
import contextlib
import numpy as np
import concourse.bass as bass
import concourse.mybir as mybir
from concourse.bass_utils import run_bass_kernel_spmd

F32 = mybir.dt.float32
BF16 = mybir.dt.bfloat16
AF = mybir.ActivationFunctionType
ALU = mybir.AluOpType

P = 128
D = 1024
KC = 8
TW = 512
NB = 4
IN_DIM = 7184
DFF = 2816
EPS = 1e-6
O_CVAL, O_CGATE, O_Q, O_K, O_V, O_GO, O_GL, O_M = 0, 1024, 2048, 2560, 3072, 4096, 5120, 5136

C_GMIX, C_GFFN, C_BM, C_CB, C_LNG, C_LNB, C_GN, C_FB = 0, 8, 16, 32, 40, 48, 56, 64
C_C31, C_F3, C_FLAG, C_EPS, C_ONE, NCOLS = 108, 356, 488, 489, 490, 496

NSLOT = 5
ENGS = ("pe", "act", "dve", "pool", "sp")


class Res:
    __slots__ = ("name", "last_w", "readers", "const", "sem", "semval")

    def __init__(self, name, const=False):
        self.name = name
        self.last_w = None
        self.readers = {}
        self.const = const
        self.sem = None
        self.semval = 0


class Builder:
    def __init__(self, nc, stack, dry=False):
        self.nc = nc
        self.stack = stack
        self.dry = dry
        self.q = {e: [] for e in ENGS}
        self.cnt = {e: 0 for e in ENGS}
        self.waited = {e: {} for e in ENGS}
        self.sems = {}
        for e in ENGS:
            self.sems[e] = None if dry else stack.enter_context(nc.semaphore("sem_" + e))
        self.nres = 0

    def res(self, name, const=False):
        return Res(name, const)

    def _dma_sem(self, r):
        if r.sem is None:
            key = "d%d" % len(self.sems)
            self.sems[key] = None if self.dry else self.stack.enter_context(self.nc.semaphore(key))
            r.sem = key
        return r.sem

    def _waits(self, eng, reads, writes):
        need = {}

        def add(ev, own_ok):
            if ev is None:
                return
            k, v = ev
            if k == eng:
                if eng == "pe" or not own_ok:
                    return
            if need.get(k, 0) < v:
                need[k] = v

        for r in reads:
            add(r.last_w, True)
        for r in writes:
            add(r.last_w, True)
            for k, v in r.readers.items():
                add((k, v), False)
        out = []
        w = self.waited[eng]
        for k, v in need.items():
            if w.get(k, 0) < v:
                w[k] = v
                out.append((k, v))
        return out

    def _commit(self, ev, reads, writes):
        for r in reads:
            if not r.const:
                k, v = ev
                if r.readers.get(k, 0) < v:
                    r.readers[k] = v
        for r in writes:
            r.last_w = ev
            r.readers = {}

    def op(self, eng, fn, reads=(), writes=()):
        waits = self._waits(eng, reads, writes)
        self.cnt[eng] += 1
        ev = (eng, self.cnt[eng])
        self.q[eng].append((waits, fn, None))
        self._commit(ev, reads, writes)

    def dma(self, eng, fn, semres, reads=(), writes=()):
        waits = self._waits(eng, reads, writes)
        key = self._dma_sem(semres)
        holder = [0]
        self.q[eng].append((waits, fn, (key, holder)))
        return key, holder

    def run(self, eng, e):
        for waits, fn, dm in self.q[eng]:
            for k, v in waits:
                e.wait_ge(self.sems[k], v)
            r = fn(e)
            if dm is None:
                r.then_inc(self.sems[eng], 1)
            else:
                for ins in r:
                    ins.then_inc(self.sems[dm[0]], 16)


def slab_specs():
    S = {}
    for s in range(4):
        S["GLU%d" % s] = [("w_in", 0, 8, O_CVAL + 256 * s, 256, 0), ("w_in", 0, 8, O_CGATE + 256 * s, 256, 256)]
    S["K"] = [("w_in", 0, 8, O_K, 512, 0)]
    S["V0"] = [("w_in", 0, 8, O_V, 512, 0)]
    S["V1"] = [("w_in", 0, 8, O_V + 512, 512, 0)]
    S["Q"] = [("w_in", 0, 8, O_Q, 512, 0)]
    S["GO0"] = [("w_in", 0, 8, O_GO, 512, 0)]
    S["GO1"] = [("w_in", 0, 8, O_GO + 512, 512, 0)]
    for s in range(4):
        S["M%d" % s] = [("w_in", 0, 8, O_M + 512 * s, 512, 0)]
    for s in range(2):
        S["CO%d" % s] = [("w_conv_out", 0, 8, 512 * s, 512, 0)]
        S["GLO%d" % s] = [("w_gla_out", 0, 8, 512 * s, 512, 0)]
        S["WO%d" % s] = [("w_out", 0, 8, 512 * s, 512, 0)]
    for s in range(11):
        S["UP%d" % s] = [("w_up", 0, 8, 256 * s, 256, 0), ("w_up", 0, 8, DFF + 256 * s, 256, 256)]
    for hf in range(2):
        for pt in range(3):
            nk = 8 if pt < 2 else 6
            S["DN%d" % (hf * 3 + pt)] = [("w_down", 8 * pt * 128, nk, 512 * hf, 512, 0)]
    for c in range(8):
        S["D31_%d" % c] = "diag"
    for f in range(6):
        S["DF%d" % f] = "diag"
    return S


def build_program(npre, nmain):
    NTOK = (npre + nmain) * TW
    NOUT = (nmain - 1) * TW
    nc = bass.Bass("TRN2", target_bir_lowering=False)

    def din(name, shape):
        return nc.dram_tensor(name, shape, F32, kind="ExternalInput").ap()

    x_d = din("x", [NTOK, D])
    W = {
        "w_in": din("w_in", [D, IN_DIM]),
        "w_conv_out": din("w_conv_out", [D, D]),
        "w_gla_out": din("w_gla_out", [D, D]),
        "w_out": din("w_out", [D, D]),
        "w_up": din("w_up", [D, 2 * DFF]),
        "w_down": din("w_down", [DFF, D]),
    }
    cols_d = din("cols", [P, NCOLS])
    gfin_d = din("gfin", [P, D])
    cst_d = din("cst", [P, 384])
    wgk_d = din("wgk", [17, 512])
    mt_d = din("mtot", [P, 2])
    out_d = nc.dram_tensor("out", [NOUT, D], F32, kind="ExternalOutput").ap()

    specs = slab_specs()
    names = list(specs.keys())
    sidx = {n: i for i, n in enumerate(names)}
    scr = nc.dram_tensor("wscr", [len(names), P, 4096], BF16, kind="Internal").ap()

    with contextlib.ExitStack() as stack:

        def sb(name, shape, dt):
            return stack.enter_context(nc.sbuf_tensor("sb_" + name, shape, dt))

        cols = sb("cols", [P, NCOLS], F32)
        gfin = sb("gfin", [P, D], F32)
        cst = sb("cst", [P, 384], F32)
        mt = sb("mtot", [P, 2], F32)
        identb = sb("identb", [P, P], BF16)
        onesb = sb("onesb", [P, P], BF16)
        mrevb = sb("mrevb", [P, P], BF16)
        mtb = sb("mtb", [P, 2], BF16)
        wgk = sb("wgk", [17, 512], BF16)
        wgl = sb("wgl", [P, KC, 16], BF16)
        xtm = [sb("xtm%d" % i, [P, NB, D], F32) for i in range(2)]
        arA = sb("arenaA", [P, 24 * 512], BF16)
        arB = sb("arenaB", [P, 16 * 512], BF16)
        arC = sb("arenaC", [P, 16 * 512], BF16)
        cT = arA[:, 0:16 * 512].bitcast(F32).rearrange("p (k c) -> p k c", c=TW)
        sT = arA[:, 16 * 512:24 * 512].rearrange("p (k c) -> p k c", c=TW)
        htm = sb("htm", [P, NB, D], BF16)
        gT = arA[:, 0:22 * 512].rearrange("p (k c) -> p k c", c=TW)
        oT = arB[:, :].bitcast(F32).rearrange("p (k c) -> p k c", c=TW)
        h2T = arB[:, 0:8 * 512].rearrange("p (k c) -> p k c", c=TW)
        mixT = arB[:, 8 * 512:16 * 512].rearrange("p (k c) -> p k c", c=TW)
        vv = arC[:, 0:8 * 512].rearrange("p (b f) -> p b f", f=D)
        kd = arC[:, 8 * 512:12 * 512].rearrange("p (b f) -> p b f", f=512)
        qT = arC[:, 12 * 512:16 * 512].rearrange("p (h c) -> p h c", c=TW)
        gab = arC[:, :].rearrange("p (k c) -> p k c", c=TW)
        vv1 = arA[:, 0:8 * 512].rearrange("p (b f) -> p b f", f=D)
        kd1 = arA[:, 8 * 512:12 * 512].rearrange("p (b f) -> p b f", f=512)
        vvS, kdS = [vv, vv1], [kd, kd1]
        hT = sb("hT", [P, KC, TW], BF16)
        aT = sb("aT", [P, KC, 32 + TW], BF16)
        glT = sb("glT", [32, TW], BF16)
        NTF = 8
        tmpf = [sb("tmpf%d" % i, [P, TW], F32) for i in range(NTF)]
        dtotS = [sb("dtot0", [P, 32], F32), sb("dtot1", [P, 32], F32)]
        S_f = sb("S_f", [P, 4, 256], F32)
        S_b = sb("S_b", [P, 4, 256], BF16)
        sg = sb("sg", [P, KC, TW], BF16)
        tmpb = [sb("tmpb%d" % i, [P, TW], BF16) for i in range(4)]
        ub = [sb("ub%d" % i, [P, TW + 2], BF16) for i in range(3)]
        uhalo = sb("uhalo", [P, 44, 2], BF16)
        sgt = sb("sgt", [P, 2, TW], BF16)
        junk = sb("junk", [P, D], BF16)
        ssq = sb("ssq", [P, 8], F32)
        rstd = sb("rstd", [P, 8], F32)
        NCOLS_OF = {}
        slabs = [sb("slab%d" % i, [P, 4096], BF16) for i in range(NSLOT)]
        banks = [stack.enter_context(nc.psum_tensor("bank%d" % i, [P, 512], F32)) for i in range(8)]
        banks_bf = [b.bitcast(BF16) for b in banks]

        def emit(B, seq):
            dry = seq is None
            rec = []
            R = {}

            def rs(name, const=False):
                if name not in R:
                    R[name] = B.res(name, const)
                return R[name]

            r_bank = [rs("bank%d" % i) for i in range(8)]
            r_slab = [rs("slab%d" % i) for i in range(NSLOT)]
            r_scr = [rs("scr%d" % i) for i in range(len(names))]
            gA = [rs("gA%d" % i) for i in range(24)]
            gB = [rs("gB%d" % i) for i in range(16)]
            gC = [rs("gC%d" % i) for i in range(16)]
            r_cT = [[gA[2 * k], gA[2 * k + 1]] for k in range(KC)]
            r_sT = [[gA[16 + k]] for k in range(KC)]
            r_htm = [[rs("htm%d" % b)] for b in range(NB)]
            r_gT = [[gA[j]] for j in range(22)]
            r_oT = [[gB[2 * k], gB[2 * k + 1]] for k in range(KC)]
            r_oT_all = list(gB)
            r_h2T = [[gB[k]] for k in range(KC)]
            r_mixT = [[gB[8 + k]] for k in range(KC)]
            r_vvS = [[[gC[2 * b], gC[2 * b + 1]] for b in range(NB)], [[gA[2 * b], gA[2 * b + 1]] for b in range(NB)]]
            r_kdS = [[[gC[8 + b]] for b in range(NB)], [[gA[8 + b]] for b in range(NB)]]
            r_qT = [[gC[12 + h]] for h in range(4)]
            r_gab = [[gC[k]] for k in range(16)]
            r_hT = [[rs("hT%d" % k)] for k in range(KC)]
            r_aT = [[rs("aT%d" % k)] for k in range(KC)]
            r_sg = [[rs("sg%d" % k)] for k in range(KC)]
            r_tmpf = [rs("tmpf%d" % i) for i in range(NTF)]
            r_tmpb = [rs("tmpb%d" % i) for i in range(4)]
            r_ub = [rs("ub%d" % i) for i in range(3)]
            r_sgt = [rs("sgt0"), rs("sgt1")]
            r_xtm = [rs("xtm0"), rs("xtm1")]
            r_glT, r_S, r_Sb, r_uhalo = rs("glT"), rs("S_f"), rs("S_b"), rs("uhalo")
            r_dtotS, r_ssq, r_rstd = [rs("dtot0"), rs("dtot1")], rs("ssq"), rs("rstd")

            def flat(lst):
                out = []
                for x_ in lst:
                    out += x_
                return out

            bank_rr = [0]
            pinned = set()
            C0 = [0]

            def nbank():
                while True:
                    i = bank_rr[0] % 8
                    bank_rr[0] += 1
                    if i not in pinned:
                        return i

            rrc = {"tf": 0, "tb": 0, "ub": 0}

            def nxt(key, n):
                v = rrc[key] % n
                rrc[key] += 1
                return v

            def colap(c):
                return cols[:, c:c + 1]

            def dma_op(eng, fn, semres, n, reads=(), writes=()):
                B.dma(eng, fn, semres, reads=reads, writes=writes)
                semres.semval += 16 * n
                ev = (semres.sem, semres.semval)
                B._commit(ev, reads, writes)
                return ev

            r_cols, r_gfin, r_cst, r_mt = rs("cols", True), rs("gfin", True), rs("cst", True), rs("mtot", True)
            r_wgl, r_identb, r_onesb, r_wgk = rs("wgl", True), rs("identb", True), rs("onesb", True), rs("wgk", True)
            dma_op("sp", lambda e: [e.dma_start(out=cols[:, :], in_=cols_d[:, :])], r_cols, 1, writes=[r_cols])
            dma_op("sp", lambda e: [e.dma_start(out=cst[:, :], in_=cst_d[:, :])], r_cst, 1, writes=[r_cst])
            dma_op("sp", lambda e: [e.dma_start(out=gfin[:, :], in_=gfin_d[:, :])], r_gfin, 1, writes=[r_gfin])
            dma_op("sp", lambda e: [e.dma_start(out=mt[:, :], in_=mt_d[:, :])], r_mt, 1, writes=[r_mt])
            dma_op("sp", lambda e: [e.dma_start(out=tmpf[0][0:17, :], in_=wgk_d[:, :])], r_tmpf[0], 1,
                   writes=[r_tmpf[0]])
            dma_op("pool", lambda e: [e.dma_start(
                out=wgl[:, :, :], in_=W["w_in"][:, O_GL:O_GL + 16].rearrange("(kc p) c -> p kc c", p=P))],
                r_wgl, 1, writes=[r_wgl])
            B.op("dve", lambda e: e.tensor_copy(out=identb[:, :], in_=cst[:, 0:128]), reads=[r_cst], writes=[r_identb])
            B.op("dve", lambda e: e.tensor_copy(out=onesb[:, :], in_=cst[:, 256:384]), reads=[r_cst], writes=[r_onesb])
            r_mrevb, r_mtb = rs("mrevb", True), rs("mtb", True)
            B.op("dve", lambda e: e.tensor_copy(out=mrevb[:, :], in_=cst[:, 128:256]), reads=[r_cst], writes=[r_mrevb])
            B.op("dve", lambda e: e.tensor_copy(out=mtb[:, :], in_=mt[:, :]), reads=[r_mt], writes=[r_mtb])
            B.op("dve", lambda e: e.tensor_copy(out=wgk[:, :], in_=tmpf[0][0:17, :]), reads=[r_tmpf[0]],
                 writes=[r_wgk])
            B.op("dve", lambda e: e.memset(glT[:, :], 1.0), writes=[r_glT])
            B.op("dve", lambda e: e.memset(aT[:, :, :], 0.0), writes=flat(r_aT))
            B.op("dve", lambda e: e.memset(S_f[:, :, :], 0.0), writes=[r_S])
            B.op("dve", lambda e: e.memset(uhalo[:, :, :], 0.0), writes=[r_uhalo])

            def convert(name):
                i = sidx[name]
                pieces = specs[name]
                dstv = scr[i].rearrange("p (kc c) -> p kc c", c=512)

                def fn(e, pieces=pieces, dstv=dstv):
                    out = []
                    for (src, r0, nk, c0, ncol, coff) in pieces:
                        s_ap = W[src][r0:r0 + nk * P, c0:c0 + ncol].rearrange("(kc p) c -> p kc c", p=P)
                        out.append(e.dma_start(out=dstv[:, 0:nk, coff:coff + ncol], in_=s_ap))
                    return out
                dma_op("pool", fn, r_scr[i], len(pieces), writes=[r_scr[i]])

            slot_rr = [0]

            dhalf = [0]

            dhalf = [0]

            def build_diag(name, runs):
                i = sidx[name]
                h = dhalf[0] % 2
                dhalf[0] += 1
                base = h * 4096
                res_h = gB[8 * h:8 * h + 8]
                nblk = 0
                for (m0, col0, cnt) in runs:
                    nblk = max(nblk, m0 + cnt)
                    ov = arB[:, base + m0 * P:base + (m0 + cnt) * P].rearrange("p (j m) -> p j m", m=P)
                    ia = identb[:, :].unsqueeze(1).broadcast_to([P, cnt, P])
                    wa = cols[:, col0:col0 + cnt].unsqueeze(2).broadcast_to([P, cnt, P])
                    B.op("dve", lambda e, ov=ov, ia=ia, wa=wa: e.tensor_tensor(out=ov, in0=ia, in1=wa, op=ALU.mult),
                         reads=[r_identb, r_cols], writes=res_h)
                n = nblk * P
                dma_op("sp", lambda e, i=i, n=n, base=base: [e.dma_start(out=scr[i][:, 0:n],
                                                                         in_=arB[:, base:base + n])],
                       r_scr[i], 1, reads=res_h, writes=[r_scr[i]])

            order1 = ["K", "V0", "V1"] + ["GLU%d" % s for s in range(4)] + ["Q", "GO0", "GO1"]
            order2 = (["M%d" % s for s in range(4)] + ["CO0", "CO1", "GLO0", "GLO1", "WO0", "WO1"] +
                      ["UP%d" % s for s in range(11)] + ["DN%d" % s for s in range(6)])
            for n_ in order1[:3]:
                convert(n_)
            conv_jobs = order1[3:] + order2

            def df_runs(f):
                out = []
                for ss in (2 * f, 2 * f + 1):
                    if ss > 10:
                        break
                    mb = (ss - 2 * f) * 12
                    out.append((mb, C_F3 + (2 * ss) * 3, 6))
                    out.append((mb + 6, C_F3 + (22 + 2 * ss) * 3, 6))
                return out
            diag_jobs = ([("D31_%d" % c, [(0, C_C31 + c * 31, 31)]) for c in range(8)] +
                         [("DF%d" % f, df_runs(f)) for f in range(6)])

            def lazy_diag(n):
                for _ in range(n):
                    if diag_jobs:
                        nm, ci_ = diag_jobs.pop(0)
                        build_diag(nm, ci_)
                for _ in range(n * 4 if n < 99 else 999):
                    if conv_jobs:
                        convert(conv_jobs.pop(0))

            def ncols_of(nm):
                if nm.startswith("D31"):
                    return 31 * P
                if nm.startswith("DF"):
                    return (24 if nm != "DF5" else 12) * P
                if nm in ("DN2", "DN5"):
                    return 6 * 512
                return 4096
            st = {"issued": 0, "pos": 0}
            slot_of = {}
            live = set()

            def next_slab(name):
                pos = st["pos"]
                st["pos"] += 1
                live.add(pos)
                if dry:
                    rec.append(name)
                    return slabs[0], r_slab[0], pos
                assert seq[pos] == name, (seq[pos], name)
                lim = min(len(seq), min(live) + NSLOT)
                assert pos < lim, "too many live slabs"
                while st["issued"] < lim:
                    nm = seq[st["issued"]]
                    if nm in conv_jobs:
                        conv_jobs.remove(nm)
                        convert(nm)
                    for dj in list(diag_jobs):
                        if dj[0] == nm:
                            diag_jobs.remove(dj)
                            build_diag(dj[0], dj[1])
                    ncols = ncols_of(nm)
                    i = sidx[nm]
                    slot = slot_rr[0] % NSLOT
                    slot_rr[0] += 1
                    sl = slabs[slot]
                    dma_op("sp", lambda e, sl=sl, i=i, ncols=ncols: [e.dma_start(out=sl[:, 0:ncols],
                                                                                  in_=scr[i][:, 0:ncols])],
                           r_slab[slot], 1, reads=[r_scr[i]], writes=[r_slab[slot]])
                    slot_of[st["issued"]] = slot
                    st["issued"] += 1
                s_ = slot_of[pos]
                return slabs[s_], r_slab[s_], pos

            def release(h):
                live.discard(h)

            def load_x(ti):
                slot = ti % 2
                t0 = ti * TW
                src = x_d[t0:t0 + TW, :].rearrange("(b p) f -> p b f", p=P)
                dma_op("sp", lambda e, src=src, slot=slot: [e.dma_start(out=xtm[slot][:, :, :], in_=src)],
                       r_xtm[slot], 1, writes=[r_xtm[slot]])

            def rms_stats(slot):
                for b in range(NB):
                    B.op("act", lambda e, b=b, slot=slot: e.activation(
                        out=junk[:, :], in_=xtm[slot][:, b, :], func=AF.Square, accum_out=ssq[:, b:b + 1]),
                        reads=[r_xtm[slot]], writes=[r_ssq])
                B.op("act", lambda e: e.activation(out=rstd[:, 4:8], in_=ssq[:, 0:4], func=AF.Ln,
                                                    bias=colap(C_EPS), scale=1.0 / D),
                     reads=[r_ssq, r_cols], writes=[r_rstd])
                B.op("act", lambda e: e.activation(out=rstd[:, 0:4], in_=rstd[:, 4:8], func=AF.Exp, scale=-0.5),
                     reads=[r_rstd], writes=[r_rstd])

            def normA(slot):
                rms_stats(slot)
                for b in range(NB):
                    B.op("dve", lambda e, b=b, slot=slot: e.tensor_scalar(
                        out=htm[:, b, :], in0=xtm[slot][:, b, :], scalar1=rstd[:, b:b + 1], scalar2=None,
                        op0=ALU.mult), reads=[r_xtm[slot], r_rstd], writes=r_htm[b])

            def normB(dstT, r_dstT, gcol):
                for fc in range(KC):
                    bi = nbank()

                    def tr(e, fc=fc, bi=bi):
                        last = None
                        for b in range(NB):
                            last = e.transpose(out=banks_bf[bi][:, b * P:(b + 1) * P],
                                               in_=htm[:, b, fc * P:(fc + 1) * P], identity=identb[:, :])
                        return last
                    B.op("pe", tr, reads=flat(r_htm) + [r_identb], writes=[r_bank[bi]])
                    B.op("act", lambda e, fc=fc, bi=bi: e.activation(
                        out=dstT[:, fc, :], in_=banks_bf[bi][:, 0:TW], func=AF.Identity, scale=colap(gcol + fc)),
                        reads=[r_bank[bi], r_cols], writes=r_dstT[fc])

            def mm_feat(bi, slab, r_sl, col0, rhs_list, r_rhs):
                sv = slab[:, :].rearrange("p (kc c) -> p kc c", c=512)
                n = len(rhs_list)
                c0 = C0[0]

                def fn(e):
                    last = None
                    for i, (k, ap) in enumerate(rhs_list):
                        last = e.matmul(banks[bi][:, c0:TW], lhsT=sv[:, k, col0:col0 + P], rhs=ap,
                                        start=(i == 0), stop=(i == n - 1))
                    return last
                B.op("pe", fn, reads=[r_sl] + list(r_rhs), writes=[r_bank[bi]])

            def mm_tok(bi, lhs_list, r_act, slab, r_sl):
                sv = slab[:, :].rearrange("p (kc c) -> p kc c", c=512)
                n = len(lhs_list)

                def fn(e):
                    last = None
                    for i, (k, ap) in enumerate(lhs_list):
                        last = e.matmul(banks[bi][:, :], lhsT=ap, rhs=sv[:, k, :], start=(i == 0),
                                        stop=(i == n - 1))
                    return last
                B.op("pe", fn, reads=[r_sl] + list(r_act), writes=[r_bank[bi]])

            def hT_rhs():
                return [(k, hT[:, k, C0[0]:TW]) for k in range(KC)]

            def hT_blk(b):
                return [(k, hT[:, k, b * P:(b + 1) * P]) for k in range(KC)]

            def rstd_from(dst, r_dst, src_ap, r_src, scale):
                B.op("act", lambda e: e.activation(out=dst, in_=src_ap, func=AF.Ln, bias=colap(C_EPS), scale=scale),
                     reads=list(r_src) + [r_cols], writes=[r_dst])
                B.op("act", lambda e: e.activation(out=dst, in_=dst, func=AF.Exp, scale=-0.5),
                     reads=[r_dst], writes=[r_dst])

            def front_normA(ti):
                normA(ti % 2)

            def front_normB():
                normB(hT, r_hT, C_GMIX)

            def front_glu(c0g=0):
                c0 = c0g
                old_c0 = C0[0]
                C0[0] = c0g
                for s in range(4):
                    sl, r_sl, hd = next_slab("GLU%d" % s)
                    for j in range(2):
                        cc = 2 * s + j
                        ba, bg = nbank(), nbank()
                        mm_feat(ba, sl, r_sl, j * P, hT_rhs(), flat(r_hT))
                        mm_feat(bg, sl, r_sl, 256 + j * P, hT_rhs(), flat(r_hT))
                        t = nxt("tb", 4)
                        B.op("act", lambda e, bg=bg, t=t: e.activation(out=tmpb[t][:, c0:TW], in_=banks[bg][:, c0:TW],
                                                                         func=AF.Sigmoid),
                             reads=[r_bank[bg]], writes=[r_tmpb[t]])
                        B.op("dve", lambda e, ba=ba, t=t, cc=cc: e.tensor_tensor(
                            out=aT[:, cc, 32 + c0:32 + TW], in0=banks[ba][:, c0:TW], in1=tmpb[t][:, c0:TW], op=ALU.mult),
                            reads=[r_bank[ba], r_tmpb[t]], writes=r_aT[cc])
                    release(hd)
                C0[0] = old_c0

            def proj_phase(main, vset=0, filler=None, after_glow=None):
                vv_, kd_, dtot_ = vvS[vset], kdS[vset], dtotS[vset]
                r_vv, r_kd, r_dtot = r_vvS[vset], r_kdS[vset], r_dtotS[vset]

                def fill(n):
                    if filler is not None:
                        for _ in range(n):
                            if filler:
                                filler.pop(0)()

                def vgrp(vs, b, sl, r_sl):
                    bi = nbank()
                    mm_tok(bi, hT_blk(b), flat(r_hT), sl, r_sl)
                    B.op("act", lambda e, bi=bi, b=b, vs=vs: e.activation(
                        out=vv_[:, b, vs * 512:(vs + 1) * 512], in_=banks[bi][:, :], func=AF.Copy),
                        reads=[r_bank[bi]], writes=r_vv[b])

                def qgrp(h, sl, r_sl):
                    c0 = C0[0]
                    bi = nbank()
                    mm_feat(bi, sl, r_sl, h * P, hT_rhs(), flat(r_hT))
                    B.op("act", lambda e, bi=bi, h=h: e.activation(out=qT[:, h, c0:TW], in_=banks[bi][:, c0:TW],
                                                                    func=AF.Copy, scale=float(128 ** -0.5)),
                         reads=[r_bank[bi]], writes=r_qT[h])

                def gogrp(gc, j, sl, r_sl):
                    c0 = C0[0]
                    bi = nbank()
                    mm_feat(bi, sl, r_sl, j * P, hT_rhs(), flat(r_hT))
                    B.op("act", lambda e, bi=bi, gc=gc: e.activation(out=sg[:, gc, c0:TW], in_=banks[bi][:, c0:TW],
                                                                      func=AF.Silu),
                         reads=[r_bank[bi]], writes=r_sg[gc])

                bi = nbank()

                def fn(e, bi=bi):
                    last = None
                    for k in range(KC):
                        last = e.matmul(banks[bi][0:16, :], lhsT=wgl[:, k, :], rhs=hT[:, k, :],
                                        start=(k == 0), stop=(k == KC - 1))
                    return last
                B.op("pe", fn, reads=[r_wgl] + flat(r_hT), writes=[r_bank[bi]])
                B.op("act", lambda e, bi=bi: e.activation(out=glT[0:16, :], in_=banks[bi][0:16, :], func=AF.Copy),
                     reads=[r_bank[bi]], writes=[r_glT])
                if after_glow is not None:
                    after_glow()
                fill(2)
                slV0, r_slV0, hV0 = next_slab("V0")
                for b in range(NB):
                    vgrp(0, b, slV0, r_slV0)
                release(hV0)
                slabK, r_slK, hK = next_slab("K")
                bt = nbank()
                pinned.add(bt)
                slV1, r_slV1, hV1 = next_slab("V1")
                vdone = [0]
                for pair in range(2):
                    blks = (2 * pair, 2 * pair + 1)
                    las, Es = {}, {}
                    for b in blks:
                        bi = nbank()
                        B.op("pe", lambda e, b=b, bi=bi: e.matmul(banks[bi][:, :], lhsT=glT[0:17, b * P:(b + 1) * P],
                                                                  rhs=wgk[0:17, :], start=True, stop=True),
                             reads=[r_glT, r_wgk], writes=[r_bank[bi]])
                        f_e, f_la = nxt("tf", NTF), nxt("tb", 4)
                        las[b] = f_la
                        B.op("act", lambda e, bi=bi, f_e=f_e: e.activation(out=tmpf[f_e][:, :], in_=banks[bi][:, :],
                                                                            func=AF.Exp, scale=-1.0),
                             reads=[r_bank[bi]], writes=[r_tmpf[f_e]])
                        B.op("act", lambda e, f_e=f_e, f_la=f_la: e.activation(
                            out=tmpb[f_la][:, :], in_=tmpf[f_e][:, :], func=AF.Ln, bias=colap(C_ONE), scale=1.0),
                            reads=[r_tmpf[f_e], r_cols], writes=[r_tmpb[f_la]])
                    for _ in range(2):
                        vgrp(1, vdone[0], slV1, r_slV1)
                        vdone[0] += 1
                    fill(1)
                    for b in blks:
                        f_la = las[b]
                        bi2 = nbank()
                        B.op("pe", lambda e, bi2=bi2, f_la=f_la: e.matmul(banks[bi2][:, :], lhsT=mrevb[:, :],
                                                                          rhs=tmpb[f_la][:, :], start=True, stop=True),
                             reads=[r_tmpb[f_la], r_mrevb], writes=[r_bank[bi2]])

                        def tot(e, b=b, f_la=f_la, bt=bt):
                            last = None
                            for h in range(4):
                                c0 = h * 8 + b * 2
                                last = e.matmul(banks[bt][:, c0:c0 + 2], lhsT=tmpb[f_la][:, h * P:(h + 1) * P],
                                                rhs=mtb[:, :], start=True, stop=True)
                            return last
                        B.op("pe", tot, reads=[r_tmpb[f_la], r_mtb], writes=[r_bank[bt]])
                        f_E = nxt("tf", NTF)
                        Es[b] = f_E
                        B.op("act", lambda e, bi2=bi2, f_E=f_E: e.activation(out=tmpf[f_E][:, :],
                                                                              in_=banks[bi2][:, :], func=AF.Exp),
                             reads=[r_bank[bi2]], writes=[r_tmpf[f_E]])
                    fill(1)
                    for b in blks:
                        f_E = Es[b]
                        bk = nbank()
                        mm_tok(bk, hT_blk(b), flat(r_hT), slabK, r_slK)
                        B.op("dve", lambda e, bk=bk, f_E=f_E, b=b: e.tensor_tensor(
                            out=kd_[:, b, :], in0=banks[bk][:, :], in1=tmpf[f_E][:, :], op=ALU.mult),
                            reads=[r_bank[bk], r_tmpf[f_E]], writes=r_kd[b])
                release(hK)
                release(hV1)
                B.op("act", lambda e, bt=bt: e.activation(out=dtot_[:, :], in_=banks[bt][:, 0:32], func=AF.Exp),
                     reads=[r_bank[bt]], writes=[r_dtot])
                pinned.discard(bt)
                if main:
                    sl, r_sl, hd = next_slab("Q")
                    for h in range(4):
                        qgrp(h, sl, r_sl)
                    release(hd)
                    for s in range(2):
                        sl, r_sl, hd = next_slab("GO%d" % s)
                        for j in range(4):
                            gogrp(4 * s + j, j, sl, r_sl)
                        release(hd)

            def gla_kv(c, vset=0):
                kd, vv, dtot = kdS[vset], vvS[vset], dtotS[vset]
                r_kd, r_vv, r_dtot = r_kdS[vset], r_vvS[vset], r_dtotS[vset]
                b, r0 = c // 2, (c % 2) * 64
                bkv = [nbank(), nbank()]

                def kvfn(e, b=b, r0=r0, bkv=bkv):
                    last = None
                    for h in range(4):
                        last = e.matmul(banks[bkv[h // 2]][:, (h % 2) * 256:(h % 2) * 256 + 256],
                                        lhsT=kd[r0:r0 + 64, b, h * P:(h + 1) * P],
                                        rhs=vv[r0:r0 + 64, b, h * 256:(h + 1) * 256], start=True, stop=True)
                    return last
                B.op("pe", kvfn, reads=r_kd[b] + r_vv[b], writes=[r_bank[bkv[0]], r_bank[bkv[1]]])
                for h in range(4):
                    B.op("dve", lambda e, h=h, c=c, bkv=bkv: e.scalar_tensor_tensor(
                        out=S_f[:, h, :], in0=S_f[:, h, :], scalar=dtot[:, h * 8 + c:h * 8 + c + 1],
                        in1=banks[bkv[h // 2]][:, (h % 2) * 256:(h % 2) * 256 + 256], op0=ALU.mult, op1=ALU.add),
                        reads=[r_S, r_dtot, r_bank[bkv[h // 2]]], writes=[r_S])

            def gla_sb(c):
                B.op("act", lambda e: e.activation(out=S_b[:, :, :], in_=S_f[:, :, :], func=AF.Copy),
                     reads=[r_S], writes=[r_Sb])

            def gla_o(c):
                bo = nbank()

                def ofn(e, c=c, bo=bo):
                    last = None
                    for h in range(4):
                        for hf in range(2):
                            j = 2 * h + hf
                            last = e.matmul(banks[bo][:, j * 64:(j + 1) * 64],
                                            lhsT=S_b[:, h, hf * P:(hf + 1) * P], rhs=qT[:, h, c * 64:(c + 1) * 64],
                                            start=True, stop=True)
                    return last
                B.op("pe", ofn, reads=[r_Sb] + flat(r_qT), writes=[r_bank[bo]])
                B.op("act", lambda e, c=c, bo=bo: e.activation(
                    out=oT[:, :, c * 64:(c + 1) * 64], in_=banks[bo][:, :].rearrange("p (a b) -> p a b", b=64),
                    func=AF.Copy), reads=[r_bank[bo]], writes=r_oT_all)

            NPE = 26

            def conv_dve(cc):
                c0 = C0[0]
                fa = nxt("tf", NTF)
                for j in range(NPE, 31):
                    wcol = colap(C_C31 + cc * 31 + j)
                    if j == NPE:
                        B.op("dve", lambda e, cc=cc, j=j, fa=fa, wcol=wcol: e.tensor_scalar(
                            out=tmpf[fa][:, c0:TW], in0=aT[:, cc, j + 2 + c0:j + 2 + TW], scalar1=wcol, scalar2=None,
                            op0=ALU.mult), reads=r_aT[cc] + [r_cols], writes=[r_tmpf[fa]])
                    else:
                        B.op("dve", lambda e, cc=cc, j=j, fa=fa, wcol=wcol: e.scalar_tensor_tensor(
                            out=tmpf[fa][:, c0:TW], in0=aT[:, cc, j + 2 + c0:j + 2 + TW], scalar=wcol,
                            in1=tmpf[fa][:, c0:TW], op0=ALU.mult, op1=ALU.add),
                            reads=r_aT[cc] + [r_cols, r_tmpf[fa]], writes=[r_tmpf[fa]])
                return fa

            def conv_chunk(cc, b1, b2, fa):
                c0 = C0[0]
                sl, r_sl, hd = next_slab("D31_%d" % cc)
                bi = nbank()

                def cv(e, cc=cc, bi=bi, sl=sl):
                    last = None
                    for j in range(NPE):
                        last = e.matmul(banks[bi][:, c0:TW], lhsT=sl[:, j * P:(j + 1) * P],
                                        rhs=aT[:, cc, j + 2 + c0:j + 2 + TW], start=(j == 0), stop=(j == NPE - 1))
                    return last
                B.op("pe", cv, reads=[r_sl] + r_aT[cc], writes=[r_bank[bi]])
                release(hd)
                B.op("dve", lambda e, cc=cc, bi=bi, fa=fa: e.scalar_tensor_tensor(
                    out=cT[:, cc, c0:TW], in0=banks[bi][:, c0:TW], scalar=colap(C_CB + cc), in1=tmpf[fa][:, c0:TW],
                    op0=ALU.add, op1=ALU.add), reads=[r_bank[bi], r_cols, r_tmpf[fa]], writes=r_cT[cc])
                t1, t2 = nxt("tb", 4), nxt("tb", 4)
                B.op("act", lambda e, cc=cc, t1=t1: e.activation(out=tmpb[t1][:, c0:TW], in_=cT[:, cc, c0:TW],
                                                                  func=AF.Copy),
                     reads=r_cT[cc], writes=[r_tmpb[t1]])
                B.op("act", lambda e, cc=cc, t2=t2: e.activation(out=tmpb[t2][:, c0:TW], in_=cT[:, cc, c0:TW],
                                                                  func=AF.Square),
                     reads=r_cT[cc], writes=[r_tmpb[t2]])
                return (cc, t1, t2)

            def conv_stats(args, b1, b2):
                c0 = C0[0]
                cc, t1, t2 = args
                B.op("pe", lambda e, cc=cc, t1=t1: e.matmul(banks[b1][:, c0:TW], lhsT=onesb[:, :], rhs=tmpb[t1][:, c0:TW],
                                                            start=(cc == 0), stop=(cc == KC - 1)),
                     reads=[r_onesb, r_tmpb[t1]], writes=[r_bank[b1]])
                B.op("pe", lambda e, cc=cc, t2=t2: e.matmul(banks[b2][:, c0:TW], lhsT=onesb[:, :], rhs=tmpb[t2][:, c0:TW],
                                                            start=(cc == 0), stop=(cc == KC - 1)),
                     reads=[r_onesb, r_tmpb[t2]], writes=[r_bank[b2]])

            def mgates(s):
                c0 = C0[0]
                sl, r_sl, hd = next_slab("M%d" % s)
                for j in range(4):
                    mc = 4 * s + j
                    bi = nbank()
                    mm_feat(bi, sl, r_sl, j * P, hT_rhs(), flat(r_hT))
                    B.op("act", lambda e, bi=bi, mc=mc: e.activation(out=gab[:, mc, c0:TW], in_=banks[bi][:, c0:TW],
                                                                      func=AF.Sigmoid, bias=colap(C_BM + mc)),
                         reads=[r_bank[bi], r_cols], writes=r_gab[mc])
                release(hd)

            def branch(nm, srcT, r_srcT, goff, accumulate):
                c0 = C0[0]
                for s in range(2):
                    sl, r_sl, hd = next_slab("%s%d" % (nm, s))
                    for j in range(4):
                        dc = 4 * s + j
                        bi = nbank()
                        mm_feat(bi, sl, r_sl, j * P, [(k, srcT[:, k, c0:TW]) for k in range(KC)], flat(r_srcT))
                        if not accumulate:
                            B.op("dve", lambda e, bi=bi, dc=dc: e.tensor_tensor(
                                out=mixT[:, dc, c0:TW], in0=banks[bi][:, c0:TW], in1=gab[:, goff + dc, c0:TW], op=ALU.mult),
                                reads=[r_bank[bi]] + r_gab[goff + dc], writes=r_mixT[dc])
                        else:
                            t = nxt("tb", 4)
                            B.op("dve", lambda e, bi=bi, dc=dc, t=t: e.tensor_tensor(
                                out=tmpb[t][:, c0:TW], in0=banks[bi][:, c0:TW], in1=gab[:, goff + dc, c0:TW], op=ALU.mult),
                                reads=[r_bank[bi]] + r_gab[goff + dc], writes=[r_tmpb[t]])
                            B.op("dve", lambda e, dc=dc, t=t: e.tensor_tensor(
                                out=mixT[:, dc, c0:TW], in0=mixT[:, dc, c0:TW], in1=tmpb[t][:, c0:TW], op=ALU.add),
                                reads=[r_tmpb[t]] + r_mixT[dc], writes=r_mixT[dc])
                    release(hd)

            out_events = []

            pend_kv = []

            def pre_body(ti, hoist_normA, hoist_normB, hoist_glu, lazy):
                vset = ti % 2
                proj_phase(False, vset, pend_kv, after_glow=hoist_normA)
                hoist_normB()
                while pend_kv:
                    pend_kv.pop(0)()
                lazy()
                for c in range(8):
                    pend_kv.append(lambda c=c, vset=vset: gla_kv(c, vset))
                if ti == npre - 1:
                    while pend_kv:
                        pend_kv.pop(0)()
                    hoist_glu()

            def main_body(ti, mi, hoist_normA, hoist_normB, hoist_glu):
                halo = (mi == 0)
                slot = ti % 2
                c0 = 384 if halo else 0
                b0 = c0 // P
                C0[0] = c0
                proj_phase(True)
                b1, b2 = nbank(), nbank()
                pinned.add(b1)
                pinned.add(b2)
                pend = None
                fa_next = conv_dve(0)
                for c in range(8):
                    gla_kv(c)
                    if c * 64 >= c0:
                        gla_sb(c)
                    fa_cur = fa_next
                    if c + 1 < 8:
                        fa_next = conv_dve(c + 1)
                    cur = conv_chunk(c, b1, b2, fa_cur)
                    if pend is not None:
                        conv_stats(pend, b1, b2)
                    pend = cur
                    if c * 64 >= c0:
                        gla_o(c)
                conv_stats(pend, b1, b2)
                B.op("dve", lambda e: e.tensor_copy(out=aT[:, :, 0:32], in_=aT[:, :, TW:TW + 32]),
                     reads=flat(r_aT), writes=flat(r_aT))
                f_mu, f_var, f_rs, f_nmr = nxt("tf", NTF), nxt("tf", NTF), nxt("tf", NTF), nxt("tf", NTF)
                B.op("dve", lambda e: e.tensor_scalar(out=tmpf[f_mu][:, c0:TW], in0=banks[b1][:, c0:TW], scalar1=1.0 / D,
                                                      scalar2=None, op0=ALU.mult),
                     reads=[r_bank[b1]], writes=[r_tmpf[f_mu]])
                B.op("dve", lambda e: e.tensor_tensor(out=tmpf[f_var][:, c0:TW], in0=tmpf[f_mu][:, c0:TW],
                                                      in1=tmpf[f_mu][:, c0:TW], op=ALU.mult),
                     reads=[r_tmpf[f_mu]], writes=[r_tmpf[f_var]])
                B.op("dve", lambda e: e.scalar_tensor_tensor(out=tmpf[f_var][:, c0:TW], in0=banks[b2][:, c0:TW],
                                                             scalar=1.0 / D, in1=tmpf[f_var][:, c0:TW], op0=ALU.mult,
                                                             op1=ALU.subtract),
                     reads=[r_bank[b2], r_tmpf[f_var]], writes=[r_tmpf[f_var]])
                pinned.discard(b1)
                pinned.discard(b2)
                rstd_from(tmpf[f_rs][:, c0:TW], r_tmpf[f_rs], tmpf[f_var][:, c0:TW], [r_tmpf[f_var]], 1.0)
                B.op("dve", lambda e: e.scalar_tensor_tensor(out=tmpf[f_nmr][:, c0:TW], in0=tmpf[f_mu][:, c0:TW], scalar=-1.0,
                                                             in1=tmpf[f_rs][:, c0:TW], op0=ALU.mult, op1=ALU.mult),
                     reads=[r_tmpf[f_mu], r_tmpf[f_rs]], writes=[r_tmpf[f_nmr]])
                o_rs = []

                def o_sq(h):
                    ts = []
                    for hf in range(2):
                        t = nxt("tb", 4)
                        ts.append(t)
                        B.op("act", lambda e, h=h, hf=hf, t=t: e.activation(out=tmpb[t][:, c0:TW],
                                                                             in_=oT[:, 2 * h + hf, c0:TW], func=AF.Square),
                             reads=r_oT[2 * h + hf], writes=[r_tmpb[t]])
                    return ts

                def o_stat_mm(h, ts):
                    bi = nbank()
                    pinned.add(bi)

                    def stf(e, bi=bi, ts=ts):
                        e.matmul(banks[bi][:, c0:TW], lhsT=onesb[:, :], rhs=tmpb[ts[0]][:, c0:TW], start=True, stop=False)
                        return e.matmul(banks[bi][:, c0:TW], lhsT=onesb[:, :], rhs=tmpb[ts[1]][:, c0:TW], start=False,
                                        stop=True)
                    B.op("pe", stf, reads=[r_onesb, r_tmpb[ts[0]], r_tmpb[ts[1]]], writes=[r_bank[bi]])
                    return bi

                def o_rstd(bi):
                    f = nxt("tf", NTF)
                    rstd_from(tmpf[f][:, c0:TW], r_tmpf[f], banks[bi][:, c0:TW], [r_bank[bi]], 1.0 / 256)
                    o_rs.append(f)
                    pinned.discard(bi)
                hoist_normA()
                tsa, tsb = o_sq(0), o_sq(1)
                mgates(0)
                sb_ = [o_stat_mm(0, tsa), o_stat_mm(1, tsb)]
                tsa, tsb = o_sq(2), o_sq(3)
                mgates(1)
                sb_ += [o_stat_mm(2, tsa), o_stat_mm(3, tsb)]
                for bi_ in sb_:
                    o_rstd(bi_)
                for cc in range(KC):
                    B.op("dve", lambda e, cc=cc: e.tensor_tensor(out=cT[:, cc, c0:TW], in0=cT[:, cc, c0:TW],
                                                                 in1=tmpf[f_rs][:, c0:TW], op=ALU.mult),
                         reads=r_cT[cc] + [r_tmpf[f_rs]], writes=r_cT[cc])
                    B.op("dve", lambda e, cc=cc: e.tensor_tensor(out=cT[:, cc, c0:TW], in0=cT[:, cc, c0:TW],
                                                                 in1=tmpf[f_nmr][:, c0:TW], op=ALU.add),
                         reads=r_cT[cc] + [r_tmpf[f_nmr]], writes=r_cT[cc])
                    B.op("act", lambda e, cc=cc: e.activation(out=sT[:, cc, c0:TW], in_=cT[:, cc, c0:TW], func=AF.Silu,
                                                              bias=colap(C_LNB + cc), scale=colap(C_LNG + cc)),
                         reads=r_cT[cc] + [r_cols], writes=r_sT[cc])
                for h in range(4):
                    f = o_rs[h]
                    for hf in range(2):
                        vc = 2 * h + hf
                        B.op("dve", lambda e, vc=vc, f=f: e.tensor_tensor(out=oT[:, vc, c0:TW], in0=oT[:, vc, c0:TW],
                                                                           in1=tmpf[f][:, c0:TW], op=ALU.mult),
                             reads=r_oT[vc] + [r_tmpf[f]], writes=r_oT[vc])
                        B.op("dve", lambda e, vc=vc: e.scalar_tensor_tensor(
                            out=sg[:, vc, c0:TW], in0=oT[:, vc, c0:TW], scalar=colap(C_GN + vc), in1=sg[:, vc, c0:TW],
                            op0=ALU.mult, op1=ALU.mult), reads=r_oT[vc] + [r_cols] + r_sg[vc], writes=r_sg[vc])
                mgates(2)
                mgates(3)
                hoist_normB()
                branch("CO", sT, r_sT, 0, False)
                branch("GLO", sg, r_sg, 8, True)
                for hf in range(2):
                    sl, r_sl, hd = next_slab("WO%d" % hf)
                    for b in range(b0, NB):
                        bi = nbank()
                        lhs = [(k, mixT[:, k, b * P:(b + 1) * P]) for k in range(KC)]
                        mm_tok(bi, lhs, flat(r_mixT), sl, r_sl)
                        B.op("dve", lambda e, bi=bi, b=b, hf=hf, slot=slot: e.tensor_tensor(
                            out=xtm[slot][:, b, hf * 512:(hf + 1) * 512],
                            in0=xtm[slot][:, b, hf * 512:(hf + 1) * 512], in1=banks[bi][:, :], op=ALU.add),
                            reads=[r_bank[bi], r_xtm[slot]], writes=[r_xtm[slot]])
                    release(hd)
                normA(slot)
                hoist_glu()
                normB(h2T, r_h2T, C_GFFN)
                dfs = None
                pend3 = None

                def conv3(args):
                    s_, q, ci, u, dsl, r_dsl = args
                    bc = nbank()
                    m0 = (s_ % 2) * 12 + q * 3

                    def c3(e, bc=bc, u=u, m0=m0, dsl=dsl):
                        last = None
                        for j in range(3):
                            last = e.matmul(banks[bc][:, c0:TW], lhsT=dsl[:, (m0 + j) * P:(m0 + j + 1) * P],
                                            rhs=ub[u][:, j:j + TW], start=(j == 0), stop=(j == 2))
                        return last
                    B.op("pe", c3, reads=[r_dsl, r_ub[u]], writes=[r_bank[bc]])
                    if q < 2:
                        B.op("act", lambda e, bc=bc, q=q, ci=ci: e.activation(
                            out=sgt[:, q, :], in_=banks[bc][:, c0:TW], func=AF.Silu, bias=colap(C_FB + ci)),
                            reads=[r_bank[bc], r_cols], writes=[r_sgt[q]])
                    else:
                        fc = 2 * s_ + (q - 2)
                        B.op("dve", lambda e, bc=bc, q=q, ci=ci, fc=fc: e.scalar_tensor_tensor(
                            out=gT[:, fc, :], in0=banks[bc][:, c0:TW], scalar=colap(C_FB + ci), in1=sgt[:, q - 2, :],
                            op0=ALU.add, op1=ALU.mult), reads=[r_bank[bc], r_cols, r_sgt[q - 2]],
                            writes=r_gT[fc])

                old_dfs = None
                for s in range(11):
                    if not halo and s % 2 == 0:
                        old_dfs = dfs
                        dfs = next_slab("DF%d" % (s // 2))
                    sl, r_sl, hd = next_slab("UP%d" % s)
                    for q in range(4):
                        ci = 2 * s + q if q < 2 else 22 + 2 * s + (q - 2)
                        bu = nbank()
                        mm_feat(bu, sl, r_sl, q * P, [(k, h2T[:, k, c0:TW]) for k in range(KC)], flat(r_h2T))
                        u = nxt("ub", 3)
                        B.op("act", lambda e, bu=bu, u=u: e.activation(out=ub[u][:, 2 + c0:2 + TW], in_=banks[bu][:, c0:TW],
                                                                        func=AF.Copy),
                             reads=[r_bank[bu]], writes=[r_ub[u]])
                        if halo:
                            B.op("pool", lambda e, u=u, ci=ci: e.tensor_scalar(
                                out=uhalo[:, ci, :], in0=ub[u][:, TW:TW + 2], scalar1=colap(C_FLAG), scalar2=None,
                                op0=ALU.mult), reads=[r_ub[u], r_cols], writes=[r_uhalo])
                            continue
                        B.op("pool", lambda e, u=u, ci=ci: e.tensor_copy(out=ub[u][:, 0:2], in_=uhalo[:, ci, :]),
                             reads=[r_uhalo], writes=[r_ub[u]])
                        B.op("pool", lambda e, u=u, ci=ci: e.tensor_copy(out=uhalo[:, ci, :],
                                                                         in_=ub[u][:, TW:TW + 2]),
                             reads=[r_ub[u]], writes=[r_uhalo])
                        if pend3 is not None:
                            conv3(pend3)
                        pend3 = (s, q, ci, u, dfs[0], dfs[1])
                        if old_dfs is not None and q == 0:
                            release(old_dfs[2])
                            old_dfs = None
                    release(hd)
                if pend3 is not None:
                    conv3(pend3)
                if dfs is not None:
                    release(dfs[2])
                C0[0] = 0
                if halo:
                    return
                for hf in range(2):
                    bks = [nbank() for _ in range(NB)]
                    for b_ in bks:
                        pinned.add(b_)
                    for pt in range(3):
                        sl, r_sl, hd = next_slab("DN%d" % (hf * 3 + pt))
                        nk = 8 if pt < 2 else 6
                        sv = sl[:, :].rearrange("p (kc c) -> p kc c", c=512)

                        def dn(e, pt=pt, nk=nk, sv=sv, bks=bks):
                            last = None
                            for kk in range(nk):
                                fc = 8 * pt + kk
                                for b in range(NB):
                                    last = e.matmul(banks[bks[b]][:, :], lhsT=gT[:, fc, b * P:(b + 1) * P],
                                                    rhs=sv[:, kk, :], start=(fc == 0), stop=(fc == 21))
                            return last
                        B.op("pe", dn, reads=[r_sl] + flat(r_gT[8 * pt:8 * pt + nk]),
                             writes=[r_bank[b_] for b_ in bks])
                        release(hd)
                    for b in range(NB):
                        B.op("dve", lambda e, b=b, hf=hf, bks=bks, slot=slot: e.tensor_tensor(
                            out=xtm[slot][:, b, hf * 512:(hf + 1) * 512],
                            in0=xtm[slot][:, b, hf * 512:(hf + 1) * 512], in1=banks[bks[b]][:, :], op=ALU.add),
                            reads=[r_bank[bks[b]], r_xtm[slot]], writes=[r_xtm[slot]])
                    for b_ in bks:
                        pinned.discard(b_)
                rms_stats(slot)
                for b in range(NB):
                    B.op("dve", lambda e, b=b, slot=slot: e.scalar_tensor_tensor(
                        out=xtm[slot][:, b, :], in0=xtm[slot][:, b, :], scalar=rstd[:, b:b + 1], in1=gfin[:, :],
                        op0=ALU.mult, op1=ALU.mult), reads=[r_xtm[slot], r_rstd, r_gfin], writes=[r_xtm[slot]])
                o0 = (mi - 1) * TW
                dst = out_d[o0:o0 + TW, :].rearrange("(b p) f -> p b f", p=P)
                r_out = rs("outst%d" % slot)
                ev = dma_op("sp", lambda e, dst=dst, slot=slot: [e.dma_start(out=dst, in_=xtm[slot][:, :, :])],
                            r_out, 1, reads=[r_xtm[slot]])
                out_events.append(ev)

            ntile = npre + nmain
            x_loaded = set()

            def ensure_x(t):
                if t < ntile and t not in x_loaded:
                    x_loaded.add(t)
                    load_x(t)
            ensure_x(0)
            ensure_x(1)
            front_normA(0)
            front_normB()
            if npre == 0:
                lazy_diag(99)
                front_glu(256)
            for ti in range(ntile):
                nxt_main = (ti + 1 >= npre)
                has_next = (ti + 1 < ntile)
                if ti < npre:
                    ensure_x(ti + 2)
                ensure_x(ti + 1)

                def hoist_normA(ti=ti, has_next=has_next):
                    if has_next:
                        front_normA(ti + 1)

                def hoist_normB(has_next=has_next):
                    if has_next:
                        old = C0[0]
                        C0[0] = 0
                        front_normB()
                        C0[0] = old

                def hoist_glu(ti=ti, has_next=has_next, nxt_main=nxt_main):
                    if has_next and nxt_main:
                        front_glu(256 if ti + 1 == npre else 0)

                def lazy(ti=ti):
                    lazy_diag(99 if ti == npre - 1 else 2)
                if ti < npre:
                    pre_body(ti, hoist_normA, hoist_normB, hoist_glu, lazy)
                else:
                    main_body(ti, ti - npre, hoist_normA, hoist_normB, hoist_glu)
            if not dry:
                assert st["pos"] == len(seq), (st["pos"], len(seq))
                fin = {}
                for k, v in out_events:
                    fin[k] = max(fin.get(k, 0), v)
                B.q["sp"].append((list(fin.items()), lambda e: e.nop(), None))
            return rec

        seq_rec = emit(Builder(None, None, dry=True), None)
        B = Builder(nc, stack)
        emit(B, seq_rec)

        with nc.Block() as block:
            @block.tensor
            def _(e):
                B.run("pe", e)

            @block.scalar
            def _(e):
                B.run("act", e)

            @block.vector
            def _(e):
                B.run("dve", e)

            @block.gpsimd
            def _(e):
                B.run("pool", e)

            @block.sync
            def _(e):
                B.run("sp", e)
    return nc


def host_consts():
    cst = np.zeros((P, 384), np.float32)
    cst[:, 0:128] = np.eye(P, dtype=np.float32)
    s = np.arange(P)[:, None]
    t = np.arange(P)[None, :]
    cst[:, 128:256] = np.where((s > t) & (s // 64 == t // 64), -1.0 / 16.0, 0.0)
    cst[:, 256:384] = 1.0
    mt = np.zeros((P, 2), np.float32)
    mt[:64, 0] = -1.0 / 16.0
    mt[64:, 1] = -1.0 / 16.0
    return cst, mt


def host_cols(p, flag):
    cols = np.zeros((P, NCOLS), np.float32)

    def put(off, vec):
        v = np.asarray(vec, np.float32).reshape(-1, P).T
        cols[:, off:off + v.shape[1]] = v
    put(C_GMIX, p["norm_mix"][0])
    put(C_GFFN, p["norm_ffn"][0])
    put(C_BM, p["b_merge"][0])
    put(C_CB, p["conv_dw_b"][0])
    put(C_LNG, p["conv_ln_g"][0])
    put(C_LNB, p["conv_ln_b"][0])
    put(C_GN, p["gla_norm"][0].reshape(-1))
    put(C_FB, p["ffn_dw_b"][0])
    cd = np.asarray(p["conv_dw"][0], np.float32)
    cols[:, C_C31:C_C31 + 248] = cd.reshape(31, 8, P).transpose(2, 1, 0).reshape(P, 248)
    fd = np.asarray(p["ffn_dw"][0], np.float32)
    cols[:, C_F3:C_F3 + 132] = fd.reshape(3, 44, P).transpose(2, 1, 0).reshape(P, 132)
    cols[:, C_FLAG] = flag
    cols[:, C_EPS] = EPS
    cols[:, C_ONE] = 1.0
    return cols


def host_diag(p):
    dg = np.zeros((14, P, 4096), np.float32)
    idx = np.arange(P)
    cd = np.asarray(p["conv_dw"][0], np.float32)
    for c in range(8):
        for j in range(31):
            dg[c, idx, j * P + idx] = cd[j, c * P:(c + 1) * P]
    fd = np.asarray(p["ffn_dw"][0], np.float32)
    for f in range(6):
        m = 0
        for ss in (2 * f, 2 * f + 1):
            if ss > 10:
                break
            for q in range(4):
                ci = 2 * ss + q if q < 2 else 22 + 2 * ss + (q - 2)
                for j in range(3):
                    dg[8 + f, idx, m * P + idx] = fd[j, ci * P:(ci + 1) * P]
                    m += 1
    return dg


def make_in_maps(p, xs, flags):
    cst, mt = host_consts()
    gfin = np.ascontiguousarray(np.broadcast_to(np.asarray(p["norm_final"], np.float32)[None, :], (P, D)))
    wgk = np.concatenate([np.asarray(p["w_gk2"][0], np.float32), np.asarray(p["b_gk"][0], np.float32)[None, :]], 0)
    shared = {
        "w_in": np.ascontiguousarray(np.asarray(p["w_in"][0], np.float32)),
        "w_conv_out": np.ascontiguousarray(np.asarray(p["w_conv_out"][0], np.float32)),
        "w_gla_out": np.ascontiguousarray(np.asarray(p["w_gla_out"][0], np.float32)),
        "w_out": np.ascontiguousarray(np.asarray(p["w_out"][0], np.float32)),
        "w_up": np.ascontiguousarray(np.asarray(p["w_up"][0], np.float32)),
        "w_down": np.ascontiguousarray(np.asarray(p["w_down"][0], np.float32)),
        "gfin": gfin, "cst": cst, "mtot": mt, "wgk": np.ascontiguousarray(wgk),
    }
    maps = []
    for xc, fl in zip(xs, flags):
        m = dict(shared)
        m["x"] = np.ascontiguousarray(xc, dtype=np.float32)
        m["cols"] = host_cols(p, fl)
        maps.append(m)
    return maps


NPRE, NMAIN = 7, 9
_CACHE = {}


def kernel(**inputs):
    x = np.asarray(inputs["x"], np.float32)
    bsz, seq, _ = x.shape
    half = seq // 2
    xs, flags = [], []
    for c in range(8):
        b, hf = c // 2, c % 2
        if hf == 1:
            xs.append(x[b])
            flags.append(1.0)
        else:
            xs.append(np.concatenate([np.zeros((half, D), np.float32), x[b, :half]], 0))
            flags.append(0.0)
    maps = make_in_maps(inputs, xs, flags)
    if "nc" not in _CACHE:
        _CACHE["nc"] = build_program(NPRE, NMAIN)
    res = run_bass_kernel_spmd(_CACHE["nc"], maps, core_ids=list(range(8)))
    y = np.empty((bsz, seq, D), np.float32)
    for c in range(8):
        b, hf = c // 2, c % 2
        y[b, hf * half:(hf + 1) * half] = res.results[c]["out"]
    return y
```

```python
import contextlib
import numpy as np
import concourse.bass as bass
import concourse.mybir as mybir
from concourse.bass_utils import run_bass_kernel_spmd

F32 = mybir.dt.float32
BF16 = mybir.dt.bfloat16
AF = mybir.ActivationFunctionType
ALU = mybir.AluOpType

P = 128
D = 1024
KC = 8
TW = 512
NB = 4
IN_DIM = 7184
DFF = 2816
EPS = 1e-6
O_CVAL, O_CGATE, O_Q, O_K, O_V, O_GO, O_GL, O_M = 0, 1024, 2048, 2560, 3072, 4096, 5120, 5136

C_GMIX, C_GFFN, C_BM, C_CB, C_LNG, C_LNB, C_GN, C_FB = 0, 8, 16, 32, 40, 48, 56, 64
C_C31, C_F3, C_FLAG, C_EPS, C_ONE, NCOLS = 108, 356, 488, 489, 490, 496

NSLOT = 5
ENGS = ("pe", "act", "dve", "pool", "sp")


class Res:
    __slots__ = ("name", "last_w", "readers", "const", "sem", "semval")

    def __init__(self, name, const=False):
        self.name = name
        self.last_w = None
        self.readers = {}
        self.const = const
        self.sem = None
        self.semval = 0


class Builder:
    def __init__(self, nc, stack, dry=False):
        self.nc = nc
        self.stack = stack
        self.dry = dry
        self.q = {e: [] for e in ENGS}
        self.cnt = {e: 0 for e in ENGS}
        self.waited = {e: {} for e in ENGS}
        self.sems = {}
        for e in ENGS:
            self.sems[e] = None if dry else stack.enter_context(nc.semaphore("sem_" + e))
        self.nres = 0

    def res(self, name, const=False):
        return Res(name, const)

    def _dma_sem(self, r):
        if r.sem is None:
            key = "d%d" % len(self.sems)
            self.sems[key] = None if self.dry else self.stack.enter_context(self.nc.semaphore(key))
            r.sem = key
        return r.sem

    def _waits(self, eng, reads, writes):
        need = {}

        def add(ev, own_ok):
            if ev is None:
                return
            k, v = ev
            if k == eng:
                if eng == "pe" or not own_ok:
                    return
            if need.get(k, 0) < v:
                need[k] = v

        for r in reads:
            add(r.last_w, True)
        for r in writes:
            add(r.last_w, True)
            for k, v in r.readers.items():
                add((k, v), False)
        out = []
        w = self.waited[eng]
        for k, v in need.items():
            if w.get(k, 0) < v:
                w[k] = v
                out.append((k, v))
        return out

    def _commit(self, ev, reads, writes):
        for r in reads:
            if not r.const:
                k, v = ev
                if r.readers.get(k, 0) < v:
                    r.readers[k] = v
        for r in writes:
            r.last_w = ev
            r.readers = {}

    def op(self, eng, fn, reads=(), writes=()):
        waits = self._waits(eng, reads, writes)
        self.cnt[eng] += 1
        ev = (eng, self.cnt[eng])
        self.q[eng].append((waits, fn, None))
        self._commit(ev, reads, writes)

    def dma(self, eng, fn, semres, reads=(), writes=()):
        waits = self._waits(eng, reads, writes)
        key = self._dma_sem(semres)
        holder = [0]
        self.q[eng].append((waits, fn, (key, holder)))
        return key, holder

    def run(self, eng, e):
        for waits, fn, dm in self.q[eng]:
            for k, v in waits:
                e.wait_ge(self.sems[k], v)
            r = fn(e)
            if dm is None:
                r.then_inc(self.sems[eng], 1)
            else:
                for ins in r:
                    ins.then_inc(self.sems[dm[0]], 16)


def slab_specs():
    S = {}
    for s in range(4):
        S["GLU%d" % s] = [("w_in", 0, 8, O_CVAL + 256 * s, 256, 0), ("w_in", 0, 8, O_CGATE + 256 * s, 256, 256)]
    S["K"] = [("w_in", 0, 8, O_K, 512, 0)]
    S["V0"] = [("w_in", 0, 8, O_V, 512, 0)]
    S["V1"] = [("w_in", 0, 8, O_V + 512, 512, 0)]
    S["Q"] = [("w_in", 0, 8, O_Q, 512, 0)]
    S["GO0"] = [("w_in", 0, 8, O_GO, 512, 0)]
    S["GO1"] = [("w_in", 0, 8, O_GO + 512, 512, 0)]
    for s in range(4):
        S["M%d" % s] = [("w_in", 0, 8, O_M + 512 * s, 512, 0)]
    for s in range(2):
        S["CO%d" % s] = [("w_conv_out", 0, 8, 512 * s, 512, 0)]
        S["GLO%d" % s] = [("w_gla_out", 0, 8, 512 * s, 512, 0)]
        S["WO%d" % s] = [("w_out", 0, 8, 512 * s, 512, 0)]
    for s in range(11):
        S["UP%d" % s] = [("w_up", 0, 8, 256 * s, 256, 0), ("w_up", 0, 8, DFF + 256 * s, 256, 256)]
    for hf in range(2):
        for pt in range(3):
            nk = 8 if pt < 2 else 6
            S["DN%d" % (hf * 3 + pt)] = [("w_down", 8 * pt * 128, nk, 512 * hf, 512, 0)]
    for c in range(8):
        S["D31_%d" % c] = "diag"
    for f in range(6):
        S["DF%d" % f] = "diag"
    return S


def build_program(npre, nmain):
    NTOK = (npre + nmain) * TW
    NOUT = (nmain - 1) * TW
    nc = bass.Bass("TRN2", target_bir_lowering=False)

    def din(name, shape):
        return nc.dram_tensor(name, shape, F32, kind="ExternalInput").ap()

    x_d = din("x", [NTOK, D])
    W = {
        "w_in": din("w_in", [D, IN_DIM]),
        "w_conv_out": din("w_conv_out", [D, D]),
        "w_gla_out": din("w_gla_out", [D, D]),
        "w_out": din("w_out", [D, D]),
        "w_up": din("w_up", [D, 2 * DFF]),
        "w_down": din("w_down", [DFF, D]),
    }
    cols_d = din("cols", [P, NCOLS])
    gfin_d = din("gfin", [P, D])
    cst_d = din("cst", [P, 384])
    wgk_d = din("wgk", [17, 512])
    mt_d = din("mtot", [P, 2])
    out_d = nc.dram_tensor("out", [NOUT, D], F32, kind="ExternalOutput").ap()

    specs = slab_specs()
    names = list(specs.keys())
    sidx = {n: i for i, n in enumerate(names)}
    scr = nc.dram_tensor("wscr", [len(names), P, 4096], BF16, kind="Internal").ap()

    with contextlib.ExitStack() as stack:

        def sb(name, shape, dt):
            return stack.enter_context(nc.sbuf_tensor("sb_" + name, shape, dt))

        cols = sb("cols", [P, NCOLS], F32)
        gfin = sb("gfin", [P, D], F32)
        cst = sb("cst", [P, 384], F32)
        mt = sb("mtot", [P, 2], F32)
        identb = sb("identb", [P, P], BF16)
        onesb = sb("onesb", [P, P], BF16)
        mrevb = sb("mrevb", [P, P], BF16)
        mtb = sb("mtb", [P, 2], BF16)
        wgk = sb("wgk", [17, 512], BF16)
        wgl = sb("wgl", [P, KC, 16], BF16)
        xtm = [sb("xtm%d" % i, [P, NB, D], F32) for i in range(2)]
        arA = sb("arenaA", [P, 24 * 512], BF16)
        arB = sb("arenaB", [P, 16 * 512], BF16)
        arC = sb("arenaC", [P, 16 * 512], BF16)
        cT = arA[:, 0:16 * 512].bitcast(F32).rearrange("p (k c) -> p k c", c=TW)
        sT = arA[:, 16 * 512:24 * 512].rearrange("p (k c) -> p k c", c=TW)
        htm = sb("htm", [P, NB, D], BF16)
        gT = arA[:, 0:22 * 512].rearrange("p (k c) -> p k c", c=TW)
        oT = arB[:, :].bitcast(F32).rearrange("p (k c) -> p k c", c=TW)
        h2T = arB[:, 0:8 * 512].rearrange("p (k c) -> p k c", c=TW)
        mixT = arB[:, 8 * 512:16 * 512].rearrange("p (k c) -> p k c", c=TW)
        vv = arC[:, 0:8 * 512].rearrange("p (b f) -> p b f", f=D)
        kd = arC[:, 8 * 512:12 * 512].rearrange("p (b f) -> p b f", f=512)
        qT = arC[:, 12 * 512:16 * 512].rearrange("p (h c) -> p h c", c=TW)
        gab = arC[:, :].rearrange("p (k c) -> p k c", c=TW)
        vv1 = arA[:, 0:8 * 512].rearrange("p (b f) -> p b f", f=D)
        kd1 = arA[:, 8 * 512:12 * 512].rearrange("p (b f) -> p b f", f=512)
        vvS, kdS = [vv, vv1], [kd, kd1]
        hT = sb("hT", [P, KC, TW], BF16)
        aT = sb("aT", [P, KC, 32 + TW], BF16)
        glT = sb("glT", [32, TW], BF16)
        NTF = 8
        tmpf = [sb("tmpf%d" % i, [P, TW], F32) for i in range(NTF)]
        dtotS = [sb("dtot0", [P, 32], F32), sb("dtot1", [P, 32], F32)]
        S_f = sb("S_f", [P, 4, 256], F32)
        S_b = sb("S_b", [P, 4, 256], BF16)
        sg = sb("sg", [P, KC, TW], BF16)
        tmpb = [sb("tmpb%d" % i, [P, TW], BF16) for i in range(4)]
        ub = [sb("ub%d" % i, [P, TW + 2], BF16) for i in range(3)]
        uhalo = sb("uhalo", [P, 44, 2], BF16)
        sgt = sb("sgt", [P, 2, TW], BF16)
        junk = sb("junk", [P, D], BF16)
        ssq = sb("ssq", [P, 8], F32)
        rstd = sb("rstd", [P, 8], F32)
        NCOLS_OF = {}
        slabs = [sb("slab%d" % i, [P, 4096], BF16) for i in range(NSLOT)]
        banks = [stack.enter_context(nc.psum_tensor("bank%d" % i, [P, 512], F32)) for i in range(8)]
        banks_bf = [b.bitcast(BF16) for b in banks]

        def emit(B, seq):
            dry = seq is None
            rec = []
            R = {}

            def rs(name, const=False):
                if name not in R:
                    R[name] = B.res(name, const)
                return R[name]

            r_bank = [rs("bank%d" % i) for i in range(8)]
            r_slab = [rs("slab%d" % i) for i in range(NSLOT)]
            r_scr = [rs("scr%d" % i) for i in range(len(names))]
            gA = [rs("gA%d" % i) for i in range(24)]
            gB = [rs("gB%d" % i) for i in range(16)]
            gC = [rs("gC%d" % i) for i in range(16)]
            r_cT = [[gA[2 * k], gA[2 * k + 1]] for k in range(KC)]
            r_sT = [[gA[16 + k]] for k in range(KC)]
            r_htm = [[rs("htm%d" % b)] for b in range(NB)]
            r_gT = [[gA[j]] for j in range(22)]
            r_oT = [[gB[2 * k], gB[2 * k + 1]] for k in range(KC)]
            r_oT_all = list(gB)
            r_h2T = [[gB[k]] for k in range(KC)]
            r_mixT = [[gB[8 + k]] for k in range(KC)]
            r_vvS = [[[gC[2 * b], gC[2 * b + 1]] for b in range(NB)], [[gA[2 * b], gA[2 * b + 1]] for b in range(NB)]]
            r_kdS = [[[gC[8 + b]] for b in range(NB)], [[gA[8 + b]] for b in range(NB)]]
            r_qT = [[gC[12 + h]] for h in range(4)]
            r_gab = [[gC[k]] for k in range(16)]
            r_hT = [[rs("hT%d" % k)] for k in range(KC)]
            r_aT = [[rs("aT%d" % k)] for k in range(KC)]
            r_sg = [[rs("sg%d" % k)] for k in range(KC)]
            r_tmpf = [rs("tmpf%d" % i) for i in range(NTF)]
            r_tmpb = [rs("tmpb%d" % i) for i in range(4)]
            r_ub = [rs("ub%d" % i) for i in range(3)]
            r_sgt = [rs("sgt0"), rs("sgt1")]
            r_xtm = [rs("xtm0"), rs("xtm1")]
            r_glT, r_S, r_Sb, r_uhalo = rs("glT"), rs("S_f"), rs("S_b"), rs("uhalo")
            r_dtotS, r_ssq, r_rstd = [rs("dtot0"), rs("dtot1")], rs("ssq"), rs("rstd")

            def flat(lst):
                out = []
                for x_ in lst:
                    out += x_
                return out

            bank_rr = [0]
            pinned = set()
            C0 = [0]

            def nbank():
                while True:
                    i = bank_rr[0] % 8
                    bank_rr[0] += 1
                    if i not in pinned:
                        return i

            rrc = {"tf": 0, "tb": 0, "ub": 0}

            def nxt(key, n):
                v = rrc[key] % n
                rrc[key] += 1
                return v

            def colap(c):
                return cols[:, c:c + 1]

            def dma_op(eng, fn, semres, n, reads=(), writes=()):
                B.dma(eng, fn, semres, reads=reads, writes=writes)
                semres.semval += 16 * n
                ev = (semres.sem, semres.semval)
                B._commit(ev, reads, writes)
                return ev

            r_cols, r_gfin, r_cst, r_mt = rs("cols", True), rs("gfin", True), rs("cst", True), rs("mtot", True)
            r_wgl, r_identb, r_onesb, r_wgk = rs("wgl", True), rs("identb", True), rs("onesb", True), rs("wgk", True)
            dma_op("sp", lambda e: [e.dma_start(out=cols[:, :], in_=cols_d[:, :])], r_cols, 1, writes=[r_cols])
            dma_op("sp", lambda e: [e.dma_start(out=cst[:, :], in_=cst_d[:, :])], r_cst, 1, writes=[r_cst])
            dma_op("sp", lambda e: [e.dma_start(out=gfin[:, :], in_=gfin_d[:, :])], r_gfin, 1, writes=[r_gfin])
            dma_op("sp", lambda e: [e.dma_start(out=mt[:, :], in_=mt_d[:, :])], r_mt, 1, writes=[r_mt])
            dma_op("sp", lambda e: [e.dma_start(out=tmpf[0][0:17, :], in_=wgk_d[:, :])], r_tmpf[0], 1,
                   writes=[r_tmpf[0]])
            dma_op("pool", lambda e: [e.dma_start(
                out=wgl[:, :, :], in_=W["w_in"][:, O_GL:O_GL + 16].rearrange("(kc p) c -> p kc c", p=P))],
                r_wgl, 1, writes=[r_wgl])
            B.op("dve", lambda e: e.tensor_copy(out=identb[:, :], in_=cst[:, 0:128]), reads=[r_cst], writes=[r_identb])
            B.op("dve", lambda e: e.tensor_copy(out=onesb[:, :], in_=cst[:, 256:384]), reads=[r_cst], writes=[r_onesb])
            r_mrevb, r_mtb = rs("mrevb", True), rs("mtb", True)
            B.op("dve", lambda e: e.tensor_copy(out=mrevb[:, :], in_=cst[:, 128:256]), reads=[r_cst], writes=[r_mrevb])
            B.op("dve", lambda e: e.tensor_copy(out=mtb[:, :], in_=mt[:, :]), reads=[r_mt], writes=[r_mtb])
            B.op("dve", lambda e: e.tensor_copy(out=wgk[:, :], in_=tmpf[0][0:17, :]), reads=[r_tmpf[0]],
                 writes=[r_wgk])
            B.op("dve", lambda e: e.memset(glT[:, :], 1.0), writes=[r_glT])
            B.op("dve", lambda e: e.memset(aT[:, :, :], 0.0), writes=flat(r_aT))
            B.op("dve", lambda e: e.memset(S_f[:, :, :], 0.0), writes=[r_S])
            B.op("dve", lambda e: e.memset(uhalo[:, :, :], 0.0), writes=[r_uhalo])

            def convert(name):
                i = sidx[name]
                pieces = specs[name]
                dstv = scr[i].rearrange("p (kc c) -> p kc c", c=512)

                def fn(e, pieces=pieces, dstv=dstv):
                    out = []
                    for (src, r0, nk, c0, ncol, coff) in pieces:
                        s_ap = W[src][r0:r0 + nk * P, c0:c0 + ncol].rearrange("(kc p) c -> p kc c", p=P)
                        out.append(e.dma_start(out=dstv[:, 0:nk, coff:coff + ncol], in_=s_ap))
                    return out
                dma_op("pool", fn, r_scr[i], len(pieces), writes=[r_scr[i]])

            slot_rr = [0]

            dhalf = [0]

            dhalf = [0]

            def build_diag(name, runs):
                i = sidx[name]
                h = dhalf[0] % 2
                dhalf[0] += 1
                base = h * 4096
                res_h = gB[8 * h:8 * h + 8]
                nblk = 0
                for (m0, col0, cnt) in runs:
                    nblk = max(nblk, m0 + cnt)
                    ov = arB[:, base + m0 * P:base + (m0 + cnt) * P].rearrange("p (j m) -> p j m", m=P)
                    ia = identb[:, :].unsqueeze(1).broadcast_to([P, cnt, P])
                    wa = cols[:, col0:col0 + cnt].unsqueeze(2).broadcast_to([P, cnt, P])
                    B.op("dve", lambda e, ov=ov, ia=ia, wa=wa: e.tensor_tensor(out=ov, in0=ia, in1=wa, op=ALU.mult),
                         reads=[r_identb, r_cols], writes=res_h)
                n = nblk * P
                dma_op("sp", lambda e, i=i, n=n, base=base: [e.dma_start(out=scr[i][:, 0:n],
                                                                         in_=arB[:, base:base + n])],
                       r_scr[i], 1, reads=res_h, writes=[r_scr[i]])

            order1 = ["K", "V0", "V1"] + ["GLU%d" % s for s in range(4)] + ["Q", "GO0", "GO1"]
            order2 = (["M%d" % s for s in range(4)] + ["CO0", "CO1", "GLO0", "GLO1", "WO0", "WO1"] +
                      ["UP%d" % s for s in range(11)] + ["DN%d" % s for s in range(6)])
            for n_ in order1[:3]:
                convert(n_)
            conv_jobs = order1[3:] + order2

            def df_runs(f):
                out = []
                for ss in (2 * f, 2 * f + 1):
                    if ss > 10:
                        break
                    mb = (ss - 2 * f) * 12
                    out.append((mb, C_F3 + (2 * ss) * 3, 6))
                    out.append((mb + 6, C_F3 + (22 + 2 * ss) * 3, 6))
                return out
            diag_jobs = ([("D31_%d" % c, [(0, C_C31 + c * 31, 31)]) for c in range(8)] +
                         [("DF%d" % f, df_runs(f)) for f in range(6)])

            def lazy_diag(n):
                for _ in range(n):
                    if diag_jobs:
                        nm, ci_ = diag_jobs.pop(0)
                        build_diag(nm, ci_)
                for _ in range(n * 4 if n < 99 else 999):
                    if conv_jobs:
                        convert(conv_jobs.pop(0))

            def ncols_of(nm):
                if nm.startswith("D31"):
                    return 31 * P
                if nm.startswith("DF"):
                    return (24 if nm != "DF5" else 12) * P
                if nm in ("DN2", "DN5"):
                    return 6 * 512
                return 4096
            st = {"issued": 0, "pos": 0}
            slot_of = {}
            live = set()

            def next_slab(name):
                pos = st["pos"]
                st["pos"] += 1
                live.add(pos)
                if dry:
                    rec.append(name)
                    return slabs[0], r_slab[0], pos
                assert seq[pos] == name, (seq[pos], name)
                lim = min(len(seq), min(live) + NSLOT)
                assert pos < lim, "too many live slabs"
                while st["issued"] < lim:
                    nm = seq[st["issued"]]
                    if nm in conv_jobs:
                        conv_jobs.remove(nm)
                        convert(nm)
                    for dj in list(diag_jobs):
                        if dj[0] == nm:
                            diag_jobs.remove(dj)
                            build_diag(dj[0], dj[1])
                    ncols = ncols_of(nm)
                    i = sidx[nm]
                    slot = slot_rr[0] % NSLOT
                    slot_rr[0] += 1
                    sl = slabs[slot]
                    dma_op("sp", lambda e, sl=sl, i=i, ncols=ncols: [e.dma_start(out=sl[:, 0:ncols],
                                                                                  in_=scr[i][:, 0:ncols])],
                           r_slab[slot], 1, reads=[r_scr[i]], writes=[r_slab[slot]])
                    slot_of[st["issued"]] = slot
                    st["issued"] += 1
                s_ = slot_of[pos]
                return slabs[s_], r_slab[s_], pos

            def release(h):
                live.discard(h)

            def load_x(ti):
                slot = ti % 2
                t0 = ti * TW
                src = x_d[t0:t0 + TW, :].rearrange("(b p) f -> p b f", p=P)
                dma_op("sp", lambda e, src=src, slot=slot: [e.dma_start(out=xtm[slot][:, :, :], in_=src)],
                       r_xtm[slot], 1, writes=[r_xtm[slot]])

            def rms_stats(slot):
                for b in range(NB):
                    B.op("act", lambda e, b=b, slot=slot: e.activation(
                        out=junk[:, :], in_=xtm[slot][:, b, :], func=AF.Square, accum_out=ssq[:, b:b + 1]),
                        reads=[r_xtm[slot]], writes=[r_ssq])
                B.op("act", lambda e: e.activation(out=rstd[:, 4:8], in_=ssq[:, 0:4], func=AF.Ln,
                                                    bias=colap(C_EPS), scale=1.0 / D),
                     reads=[r_ssq, r_cols], writes=[r_rstd])
                B.op("act", lambda e: e.activation(out=rstd[:, 0:4], in_=rstd[:, 4:8], func=AF.Exp, scale=-0.5),
                     reads=[r_rstd], writes=[r_rstd])

            def normA(slot):
                rms_stats(slot)
                for b in range(NB):
                    B.op("dve", lambda e, b=b, slot=slot: e.tensor_scalar(
                        out=htm[:, b, :], in0=xtm[slot][:, b, :], scalar1=rstd[:, b:b + 1], scalar2=None,
                        op0=ALU.mult), reads=[r_xtm[slot], r_rstd], writes=r_htm[b])

            def normB(dstT, r_dstT, gcol):
                for fc in range(KC):
                    bi = nbank()

                    def tr(e, fc=fc, bi=bi):
                        last = None
                        for b in range(NB):
                            last = e.transpose(out=banks_bf[bi][:, b * P:(b + 1) * P],
                                               in_=htm[:, b, fc * P:(fc + 1) * P], identity=identb[:, :])
                        return last
                    B.op("pe", tr, reads=flat(r_htm) + [r_identb], writes=[r_bank[bi]])
                    B.op("act", lambda e, fc=fc, bi=bi: e.activation(
                        out=dstT[:, fc, :], in_=banks_bf[bi][:, 0:TW], func=AF.Identity, scale=colap(gcol + fc)),
                        reads=[r_bank[bi], r_cols], writes=r_dstT[fc])

            def mm_feat(bi, slab, r_sl, col0, rhs_list, r_rhs):
                sv = slab[:, :].rearrange("p (kc c) -> p kc c", c=512)
                n = len(rhs_list)
                c0 = C0[0]

                def fn(e):
                    last = None
                    for i, (k, ap) in enumerate(rhs_list):
                        last = e.matmul(banks[bi][:, c0:TW], lhsT=sv[:, k, col0:col0 + P], rhs=ap,
                                        start=(i == 0), stop=(i == n - 1))
                    return last
                B.op("pe", fn, reads=[r_sl] + list(r_rhs), writes=[r_bank[bi]])

            def mm_tok(bi, lhs_list, r_act, slab, r_sl):
                sv = slab[:, :].rearrange("p (kc c) -> p kc c", c=512)
                n = len(lhs_list)

                def fn(e):
                    last = None
                    for i, (k, ap) in enumerate(lhs_list):
                        last = e.matmul(banks[bi][:, :], lhsT=ap, rhs=sv[:, k, :], start=(i == 0),
                                        stop=(i == n - 1))
                    return last
                B.op("pe", fn, reads=[r_sl] + list(r_act), writes=[r_bank[bi]])

            def hT_rhs():
                return [(k, hT[:, k, C0[0]:TW]) for k in range(KC)]

            def hT_blk(b):
                return [(k, hT[:, k, b * P:(b + 1) * P]) for k in range(KC)]

            def rstd_from(dst, r_dst, src_ap, r_src, scale):
                B.op("act", lambda e: e.activation(out=dst, in_=src_ap, func=AF.Ln, bias=colap(C_EPS), scale=scale),
                     reads=list(r_src) + [r_cols], writes=[r_dst])
                B.op("act", lambda e: e.activation(out=dst, in_=dst, func=AF.Exp, scale=-0.5),
                     reads=[r_dst], writes=[r_dst])

            def front_normA(ti):
                normA(ti % 2)

            def front_normB():
                normB(hT, r_hT, C_GMIX)

            def front_glu(c0g=0):
                c0 = c0g
                old_c0 = C0[0]
                C0[0] = c0g
                for s in range(4):
                    sl, r_sl, hd = next_slab("GLU%d" % s)
                    for j in range(2):
                        cc = 2 * s + j
                        ba, bg = nbank(), nbank()
                        mm_feat(ba, sl, r_sl, j * P, hT_rhs(), flat(r_hT))
                        mm_feat(bg, sl, r_sl, 256 + j * P, hT_rhs(), flat(r_hT))
                        t = nxt("tb", 4)
                        B.op("act", lambda e, bg=bg, t=t: e.activation(out=tmpb[t][:, c0:TW], in_=banks[bg][:, c0:TW],
                                                                         func=AF.Sigmoid),
                             reads=[r_bank[bg]], writes=[r_tmpb[t]])
                        B.op("dve", lambda e, ba=ba, t=t, cc=cc: e.tensor_tensor(
                            out=aT[:, cc, 32 + c0:32 + TW], in0=banks[ba][:, c0:TW], in1=tmpb[t][:, c0:TW], op=ALU.mult),
                            reads=[r_bank[ba], r_tmpb[t]], writes=r_aT[cc])
                    release(hd)
                C0[0] = old_c0

            def proj_phase(main, vset=0, filler=None, after_glow=None):
                vv_, kd_, dtot_ = vvS[vset], kdS[vset], dtotS[vset]
                r_vv, r_kd, r_dtot = r_vvS[vset], r_kdS[vset], r_dtotS[vset]

                def fill(n):
                    if filler is not None:
                        for _ in range(n):
                            if filler:
                                filler.pop(0)()

                def vgrp(vs, b, sl, r_sl):
                    bi = nbank()
                    mm_tok(bi, hT_blk(b), flat(r_hT), sl, r_sl)
                    B.op("act", lambda e, bi=bi, b=b, vs=vs: e.activation(
                        out=vv_[:, b, vs * 512:(vs + 1) * 512], in_=banks[bi][:, :], func=AF.Copy),
                        reads=[r_bank[bi]], writes=r_vv[b])

                def qgrp(h, sl, r_sl):
                    c0 = C0[0]
                    bi = nbank()
                    mm_feat(bi, sl, r_sl, h * P, hT_rhs(), flat(r_hT))
                    B.op("act", lambda e, bi=bi, h=h: e.activation(out=qT[:, h, c0:TW], in_=banks[bi][:, c0:TW],
                                                                    func=AF.Copy, scale=float(128 ** -0.5)),
                         reads=[r_bank[bi]], writes=r_qT[h])

                def gogrp(gc, j, sl, r_sl):
                    c0 = C0[0]
                    bi = nbank()
                    mm_feat(bi, sl, r_sl, j * P, hT_rhs(), flat(r_hT))
                    B.op("act", lambda e, bi=bi, gc=gc: e.activation(out=sg[:, gc, c0:TW], in_=banks[bi][:, c0:TW],
                                                                      func=AF.Silu),
                         reads=[r_bank[bi]], writes=r_sg[gc])

                bi = nbank()

                def fn(e, bi=bi):
                    last = None
                    for k in range(KC):
                        last = e.matmul(banks[bi][0:16, :], lhsT=wgl[:, k, :], rhs=hT[:, k, :],
                                        start=(k == 0), stop=(k == KC - 1))
                    return last
                B.op("pe", fn, reads=[r_wgl] + flat(r_hT), writes=[r_bank[bi]])
                B.op("act", lambda e, bi=bi: e.activation(out=glT[0:16, :], in_=banks[bi][0:16, :], func=AF.Copy),
                     reads=[r_bank[bi]], writes=[r_glT])
                fill(2)
                if after_glow is not None:
                    after_glow()
                slV0, r_slV0, hV0 = next_slab("V0")
                for b in range(NB):
                    vgrp(0, b, slV0, r_slV0)
                release(hV0)
                slabK, r_slK, hK = next_slab("K")
                bt = nbank()
                pinned.add(bt)
                slV1, r_slV1, hV1 = next_slab("V1")
                vdone = [0]
                for pair in range(2):
                    blks = (2 * pair, 2 * pair + 1)
                    las, Es = {}, {}
                    for b in blks:
                        bi = nbank()
                        B.op("pe", lambda e, b=b, bi=bi: e.matmul(banks[bi][:, :], lhsT=glT[0:17, b * P:(b + 1) * P],
                                                                  rhs=wgk[0:17, :], start=True, stop=True),
                             reads=[r_glT, r_wgk], writes=[r_bank[bi]])
                        f_e, f_la = nxt("tf", NTF), nxt("tb", 4)
                        las[b] = f_la
                        B.op("act", lambda e, bi=bi, f_e=f_e: e.activation(out=tmpf[f_e][:, :], in_=banks[bi][:, :],
                                                                            func=AF.Exp, scale=-1.0),
                             reads=[r_bank[bi]], writes=[r_tmpf[f_e]])
                        B.op("act", lambda e, f_e=f_e, f_la=f_la: e.activation(
                            out=tmpb[f_la][:, :], in_=tmpf[f_e][:, :], func=AF.Ln, bias=colap(C_ONE), scale=1.0),
                            reads=[r_tmpf[f_e], r_cols], writes=[r_tmpb[f_la]])
                    for _ in range(2):
                        vgrp(1, vdone[0], slV1, r_slV1)
                        vdone[0] += 1
                    fill(1)
                    for b in blks:
                        f_la = las[b]
                        bi2 = nbank()
                        B.op("pe", lambda e, bi2=bi2, f_la=f_la: e.matmul(banks[bi2][:, :], lhsT=mrevb[:, :],
                                                                          rhs=tmpb[f_la][:, :], start=True, stop=True),
                             reads=[r_tmpb[f_la], r_mrevb], writes=[r_bank[bi2]])

                        def tot(e, b=b, f_la=f_la, bt=bt):
                            last = None
                            for h in range(4):
                                c0 = h * 8 + b * 2
                                last = e.matmul(banks[bt][:, c0:c0 + 2], lhsT=tmpb[f_la][:, h * P:(h + 1) * P],
                                                rhs=mtb[:, :], start=True, stop=True)
                            return last
                        B.op("pe", tot, reads=[r_tmpb[f_la], r_mtb], writes=[r_bank[bt]])
                        f_E = nxt("tf", NTF)
                        Es[b] = f_E
                        B.op("act", lambda e, bi2=bi2, f_E=f_E: e.activation(out=tmpf[f_E][:, :],
                                                                              in_=banks[bi2][:, :], func=AF.Exp),
                             reads=[r_bank[bi2]], writes=[r_tmpf[f_E]])
                    fill(1)
                    for b in blks:
                        f_E = Es[b]
                        bk = nbank()
                        mm_tok(bk, hT_blk(b), flat(r_hT), slabK, r_slK)
                        B.op("dve", lambda e, bk=bk, f_E=f_E, b=b: e.tensor_tensor(
                            out=kd_[:, b, :], in0=banks[bk][:, :], in1=tmpf[f_E][:, :], op=ALU.mult),
                            reads=[r_bank[bk], r_tmpf[f_E]], writes=r_kd[b])
                release(hK)
                release(hV1)
                B.op("act", lambda e, bt=bt: e.activation(out=dtot_[:, :], in_=banks[bt][:, 0:32], func=AF.Exp),
                     reads=[r_bank[bt]], writes=[r_dtot])
                pinned.discard(bt)
                if main:
                    sl, r_sl, hd = next_slab("Q")
                    for h in range(4):
                        qgrp(h, sl, r_sl)
                    release(hd)
                    for s in range(2):
                        sl, r_sl, hd = next_slab("GO%d" % s)
                        for j in range(4):
                            gogrp(4 * s + j, j, sl, r_sl)
                        release(hd)

            def gla_kv(c, vset=0):
                kd, vv, dtot = kdS[vset], vvS[vset], dtotS[vset]
                r_kd, r_vv, r_dtot = r_kdS[vset], r_vvS[vset], r_dtotS[vset]
                b, r0 = c // 2, (c % 2) * 64
                bkv = [nbank(), nbank()]

                def kvfn(e, b=b, r0=r0, bkv=bkv):
                    last = None
                    for h in range(4):
                        last = e.matmul(banks[bkv[h // 2]][:, (h % 2) * 256:(h % 2) * 256 + 256],
                                        lhsT=kd[r0:r0 + 64, b, h * P:(h + 1) * P],
                                        rhs=vv[r0:r0 + 64, b, h * 256:(h + 1) * 256], start=True, stop=True)
                    return last
                B.op("pe", kvfn, reads=r_kd[b] + r_vv[b], writes=[r_bank[bkv[0]], r_bank[bkv[1]]])
                for h in range(4):
                    B.op("dve", lambda e, h=h, c=c, bkv=bkv: e.scalar_tensor_tensor(
                        out=S_f[:, h, :], in0=S_f[:, h, :], scalar=dtot[:, h * 8 + c:h * 8 + c + 1],
                        in1=banks[bkv[h // 2]][:, (h % 2) * 256:(h % 2) * 256 + 256], op0=ALU.mult, op1=ALU.add),
                        reads=[r_S, r_dtot, r_bank[bkv[h // 2]]], writes=[r_S])

            def gla_sb(c):
                B.op("act", lambda e: e.activation(out=S_b[:, :, :], in_=S_f[:, :, :], func=AF.Copy),
                     reads=[r_S], writes=[r_Sb])

            def gla_o(c):
                bo = nbank()

                def ofn(e, c=c, bo=bo):
                    last = None
                    for h in range(4):
                        for hf in range(2):
                            j = 2 * h + hf
                            last = e.matmul(banks[bo][:, j * 64:(j + 1) * 64],
                                            lhsT=S_b[:, h, hf * P:(hf + 1) * P], rhs=qT[:, h, c * 64:(c + 1) * 64],
                                            start=True, stop=True)
                    return last
                B.op("pe", ofn, reads=[r_Sb] + flat(r_qT), writes=[r_bank[bo]])
                B.op("act", lambda e, c=c, bo=bo: e.activation(
                    out=oT[:, :, c * 64:(c + 1) * 64], in_=banks[bo][:, :].rearrange("p (a b) -> p a b", b=64),
                    func=AF.Copy), reads=[r_bank[bo]], writes=r_oT_all)

            NPE = 26

            def conv_dve(cc):
                c0 = C0[0]
                fa = nxt("tf", NTF)
                for j in range(NPE, 31):
                    wcol = colap(C_C31 + cc * 31 + j)
                    if j == NPE:
                        B.op("dve", lambda e, cc=cc, j=j, fa=fa, wcol=wcol: e.tensor_scalar(
                            out=tmpf[fa][:, c0:TW], in0=aT[:, cc, j + 2 + c0:j + 2 + TW], scalar1=wcol, scalar2=None,
                            op0=ALU.mult), reads=r_aT[cc] + [r_cols], writes=[r_tmpf[fa]])
                    else:
                        B.op("dve", lambda e, cc=cc, j=j, fa=fa, wcol=wcol: e.scalar_tensor_tensor(
                            out=tmpf[fa][:, c0:TW], in0=aT[:, cc, j + 2 + c0:j + 2 + TW], scalar=wcol,
                            in1=tmpf[fa][:, c0:TW], op0=ALU.mult, op1=ALU.add),
                            reads=r_aT[cc] + [r_cols, r_tmpf[fa]], writes=[r_tmpf[fa]])
                return fa

            def conv_chunk(cc, b1, b2, fa):
                c0 = C0[0]
                sl, r_sl, hd = next_slab("D31_%d" % cc)
                bi = nbank()

                def cv(e, cc=cc, bi=bi, sl=sl):
                    last = None
                    for j in range(NPE):
                        last = e.matmul(banks[bi][:, c0:TW], lhsT=sl[:, j * P:(j + 1) * P],
                                        rhs=aT[:, cc, j + 2 + c0:j + 2 + TW], start=(j == 0), stop=(j == NPE - 1))
                    return last
                B.op("pe", cv, reads=[r_sl] + r_aT[cc], writes=[r_bank[bi]])
                release(hd)
                B.op("dve", lambda e, cc=cc, bi=bi, fa=fa: e.scalar_tensor_tensor(
                    out=cT[:, cc, c0:TW], in0=banks[bi][:, c0:TW], scalar=colap(C_CB + cc), in1=tmpf[fa][:, c0:TW],
                    op0=ALU.add, op1=ALU.add), reads=[r_bank[bi], r_cols, r_tmpf[fa]], writes=r_cT[cc])
                t1, t2 = nxt("tb", 4), nxt("tb", 4)
                B.op("act", lambda e, cc=cc, t1=t1: e.activation(out=tmpb[t1][:, c0:TW], in_=cT[:, cc, c0:TW],
                                                                  func=AF.Copy),
                     reads=r_cT[cc], writes=[r_tmpb[t1]])
                B.op("act", lambda e, cc=cc, t2=t2: e.activation(out=tmpb[t2][:, c0:TW], in_=cT[:, cc, c0:TW],
                                                                  func=AF.Square),
                     reads=r_cT[cc], writes=[r_tmpb[t2]])
                return (cc, t1, t2)

            def conv_stats(args, b1, b2):
                c0 = C0[0]
                cc, t1, t2 = args
                B.op("pe", lambda e, cc=cc, t1=t1: e.matmul(banks[b1][:, c0:TW], lhsT=onesb[:, :], rhs=tmpb[t1][:, c0:TW],
                                                            start=(cc == 0), stop=(cc == KC - 1)),
                     reads=[r_onesb, r_tmpb[t1]], writes=[r_bank[b1]])
                B.op("pe", lambda e, cc=cc, t2=t2: e.matmul(banks[b2][:, c0:TW], lhsT=onesb[:, :], rhs=tmpb[t2][:, c0:TW],
                                                            start=(cc == 0), stop=(cc == KC - 1)),
                     reads=[r_onesb, r_tmpb[t2]], writes=[r_bank[b2]])

            def mgates(s):
                c0 = C0[0]
                sl, r_sl, hd = next_slab("M%d" % s)
                for j in range(4):
                    mc = 4 * s + j
                    bi = nbank()
                    mm_feat(bi, sl, r_sl, j * P, hT_rhs(), flat(r_hT))
                    B.op("act", lambda e, bi=bi, mc=mc: e.activation(out=gab[:, mc, c0:TW], in_=banks[bi][:, c0:TW],
                                                                      func=AF.Sigmoid, bias=colap(C_BM + mc)),
                         reads=[r_bank[bi], r_cols], writes=r_gab[mc])
                release(hd)

            def branch(nm, srcT, r_srcT, goff, accumulate):
                c0 = C0[0]
                for s in range(2):
                    sl, r_sl, hd = next_slab("%s%d" % (nm, s))
                    for j in range(4):
                        dc = 4 * s + j
                        bi = nbank()
                        mm_feat(bi, sl, r_sl, j * P, [(k, srcT[:, k, c0:TW]) for k in range(KC)], flat(r_srcT))
                        if not accumulate:
                            B.op("dve", lambda e, bi=bi, dc=dc: e.tensor_tensor(
                                out=mixT[:, dc, c0:TW], in0=banks[bi][:, c0:TW], in1=gab[:, goff + dc, c0:TW], op=ALU.mult),
                                reads=[r_bank[bi]] + r_gab[goff + dc], writes=r_mixT[dc])
                        else:
                            t = nxt("tb", 4)
                            B.op("dve", lambda e, bi=bi, dc=dc, t=t: e.tensor_tensor(
                                out=tmpb[t][:, c0:TW], in0=banks[bi][:, c0:TW], in1=gab[:, goff + dc, c0:TW], op=ALU.mult),
                                reads=[r_bank[bi]] + r_gab[goff + dc], writes=[r_tmpb[t]])
                            B.op("dve", lambda e, dc=dc, t=t: e.tensor_tensor(
                                out=mixT[:, dc, c0:TW], in0=mixT[:, dc, c0:TW], in1=tmpb[t][:, c0:TW], op=ALU.add),
                                reads=[r_tmpb[t]] + r_mixT[dc], writes=r_mixT[dc])
                    release(hd)

            out_events = []

            pend_kv = []

            def pre_body(ti, hoist_normA, hoist_normB, hoist_glu, lazy):
                vset = ti % 2
                proj_phase(False, vset, pend_kv, after_glow=hoist_normA)
                hoist_normB()
                while pend_kv:
                    pend_kv.pop(0)()
                lazy()
                for c in range(8):
                    pend_kv.append(lambda c=c, vset=vset: gla_kv(c, vset))
                if ti == npre - 1:
                    while pend_kv:
                        pend_kv.pop(0)()
                    hoist_glu()

            def main_body(ti, mi, hoist_normA, hoist_normB, hoist_glu):
                halo = (mi == 0)
                slot = ti % 2
                c0 = 384 if halo else 0
                b0 = c0 // P
                C0[0] = c0
                proj_phase(True)
                b1, b2 = nbank(), nbank()
                pinned.add(b1)
                pinned.add(b2)
                pend = None
                fa_next = conv_dve(0)
                for c in range(8):
                    gla_kv(c)
                    if c * 64 >= c0:
                        gla_sb(c)
                    fa_cur = fa_next
                    if c + 1 < 8:
                        fa_next = conv_dve(c + 1)
                    cur = conv_chunk(c, b1, b2, fa_cur)
                    if pend is not None:
                        conv_stats(pend, b1, b2)
                    pend = cur
                    if c * 64 >= c0:
                        gla_o(c)
                conv_stats(pend, b1, b2)
                B.op("dve", lambda e: e.tensor_copy(out=aT[:, :, 0:32], in_=aT[:, :, TW:TW + 32]),
                     reads=flat(r_aT), writes=flat(r_aT))
                f_mu, f_var, f_rs, f_nmr = nxt("tf", NTF), nxt("tf", NTF), nxt("tf", NTF), nxt("tf", NTF)
                B.op("dve", lambda e: e.tensor_scalar(out=tmpf[f_mu][:, c0:TW], in0=banks[b1][:, c0:TW], scalar1=1.0 / D,
                                                      scalar2=None, op0=ALU.mult),
                     reads=[r_bank[b1]], writes=[r_tmpf[f_mu]])
                B.op("dve", lambda e: e.tensor_tensor(out=tmpf[f_var][:, c0:TW], in0=tmpf[f_mu][:, c0:TW],
                                                      in1=tmpf[f_mu][:, c0:TW], op=ALU.mult),
                     reads=[r_tmpf[f_mu]], writes=[r_tmpf[f_var]])
                B.op("dve", lambda e: e.scalar_tensor_tensor(out=tmpf[f_var][:, c0:TW], in0=banks[b2][:, c0:TW],
                                                             scalar=1.0 / D, in1=tmpf[f_var][:, c0:TW], op0=ALU.mult,
                                                             op1=ALU.subtract),
                     reads=[r_bank[b2], r_tmpf[f_var]], writes=[r_tmpf[f_var]])
                pinned.discard(b1)
                pinned.discard(b2)
                rstd_from(tmpf[f_rs][:, c0:TW], r_tmpf[f_rs], tmpf[f_var][:, c0:TW], [r_tmpf[f_var]], 1.0)
                B.op("dve", lambda e: e.scalar_tensor_tensor(out=tmpf[f_nmr][:, c0:TW], in0=tmpf[f_mu][:, c0:TW], scalar=-1.0,
                                                             in1=tmpf[f_rs][:, c0:TW], op0=ALU.mult, op1=ALU.mult),
                     reads=[r_tmpf[f_mu], r_tmpf[f_rs]], writes=[r_tmpf[f_nmr]])
                o_rs = []

                def o_sq(h):
                    ts = []
                    for hf in range(2):
                        t = nxt("tb", 4)
                        ts.append(t)
                        B.op("act", lambda e, h=h, hf=hf, t=t: e.activation(out=tmpb[t][:, c0:TW],
                                                                             in_=oT[:, 2 * h + hf, c0:TW], func=AF.Square),
                             reads=r_oT[2 * h + hf], writes=[r_tmpb[t]])
                    return ts

                def o_stat_mm(h, ts):
                    bi = nbank()
                    pinned.add(bi)

                    def stf(e, bi=bi, ts=ts):
                        e.matmul(banks[bi][:, c0:TW], lhsT=onesb[:, :], rhs=tmpb[ts[0]][:, c0:TW], start=True, stop=False)
                        return e.matmul(banks[bi][:, c0:TW], lhsT=onesb[:, :], rhs=tmpb[ts[1]][:, c0:TW], start=False,
                                        stop=True)
                    B.op("pe", stf, reads=[r_onesb, r_tmpb[ts[0]], r_tmpb[ts[1]]], writes=[r_bank[bi]])
                    return bi

                def o_rstd(bi):
                    f = nxt("tf", NTF)
                    rstd_from(tmpf[f][:, c0:TW], r_tmpf[f], banks[bi][:, c0:TW], [r_bank[bi]], 1.0 / 256)
                    o_rs.append(f)
                    pinned.discard(bi)
                hoist_normA()
                tsa, tsb = o_sq(0), o_sq(1)
                mgates(0)
                sb_ = [o_stat_mm(0, tsa), o_stat_mm(1, tsb)]
                tsa, tsb = o_sq(2), o_sq(3)
                mgates(1)
                sb_ += [o_stat_mm(2, tsa), o_stat_mm(3, tsb)]
                for bi_ in sb_:
                    o_rstd(bi_)
                for cc in range(KC):
                    B.op("dve", lambda e, cc=cc: e.tensor_tensor(out=cT[:, cc, c0:TW], in0=cT[:, cc, c0:TW],
                                                                 in1=tmpf[f_rs][:, c0:TW], op=ALU.mult),
                         reads=r_cT[cc] + [r_tmpf[f_rs]], writes=r_cT[cc])
                    B.op("dve", lambda e, cc=cc: e.tensor_tensor(out=cT[:, cc, c0:TW], in0=cT[:, cc, c0:TW],
                                                                 in1=tmpf[f_nmr][:, c0:TW], op=ALU.add),
                         reads=r_cT[cc] + [r_tmpf[f_nmr]], writes=r_cT[cc])
                    B.op("act", lambda e, cc=cc: e.activation(out=sT[:, cc, c0:TW], in_=cT[:, cc, c0:TW], func=AF.Silu,
                                                              bias=colap(C_LNB + cc), scale=colap(C_LNG + cc)),
                         reads=r_cT[cc] + [r_cols], writes=r_sT[cc])
                for h in range(4):
                    f = o_rs[h]
                    for hf in range(2):
                        vc = 2 * h + hf
                        B.op("dve", lambda e, vc=vc, f=f: e.tensor_tensor(out=oT[:, vc, c0:TW], in0=oT[:, vc, c0:TW],
                                                                           in1=tmpf[f][:, c0:TW], op=ALU.mult),
                             reads=r_oT[vc] + [r_tmpf[f]], writes=r_oT[vc])
                        B.op("dve", lambda e, vc=vc: e.scalar_tensor_tensor(
                            out=sg[:, vc, c0:TW], in0=oT[:, vc, c0:TW], scalar=colap(C_GN + vc), in1=sg[:, vc, c0:TW],
                            op0=ALU.mult, op1=ALU.mult), reads=r_oT[vc] + [r_cols] + r_sg[vc], writes=r_sg[vc])
                mgates(2)
                mgates(3)
                hoist_normB()
                branch("CO", sT, r_sT, 0, False)
                branch("GLO", sg, r_sg, 8, True)
                for hf in range(2):
                    sl, r_sl, hd = next_slab("WO%d" % hf)
                    for b in range(b0, NB):
                        bi = nbank()
                        lhs = [(k, mixT[:, k, b * P:(b + 1) * P]) for k in range(KC)]
                        mm_tok(bi, lhs, flat(r_mixT), sl, r_sl)
                        B.op("dve", lambda e, bi=bi, b=b, hf=hf, slot=slot: e.tensor_tensor(
                            out=xtm[slot][:, b, hf * 512:(hf + 1) * 512],
                            in0=xtm[slot][:, b, hf * 512:(hf + 1) * 512], in1=banks[bi][:, :], op=ALU.add),
                            reads=[r_bank[bi], r_xtm[slot]], writes=[r_xtm[slot]])
                    release(hd)
                normA(slot)
                hoist_glu()
                normB(h2T, r_h2T, C_GFFN)
                dfs = None
                pend3 = None

                def conv3(args):
                    s_, q, ci, u, dsl, r_dsl = args
                    bc = nbank()
                    m0 = (s_ % 2) * 12 + q * 3

                    def c3(e, bc=bc, u=u, m0=m0, dsl=dsl):
                        last = None
                        for j in range(3):
                            last = e.matmul(banks[bc][:, c0:TW], lhsT=dsl[:, (m0 + j) * P:(m0 + j + 1) * P],
                                            rhs=ub[u][:, j:j + TW], start=(j == 0), stop=(j == 2))
                        return last
                    B.op("pe", c3, reads=[r_dsl, r_ub[u]], writes=[r_bank[bc]])
                    if q < 2:
                        B.op("act", lambda e, bc=bc, q=q, ci=ci: e.activation(
                            out=sgt[:, q, :], in_=banks[bc][:, c0:TW], func=AF.Silu, bias=colap(C_FB + ci)),
                            reads=[r_bank[bc], r_cols], writes=[r_sgt[q]])
                    else:
                        fc = 2 * s_ + (q - 2)
                        B.op("dve", lambda e, bc=bc, q=q, ci=ci, fc=fc: e.scalar_tensor_tensor(
                            out=gT[:, fc, :], in0=banks[bc][:, c0:TW], scalar=colap(C_FB + ci), in1=sgt[:, q - 2, :],
                            op0=ALU.add, op1=ALU.mult), reads=[r_bank[bc], r_cols, r_sgt[q - 2]],
                            writes=r_gT[fc])

                old_dfs = None
                for s in range(11):
                    if not halo and s % 2 == 0:
                        old_dfs = dfs
                        dfs = next_slab("DF%d" % (s // 2))
                    sl, r_sl, hd = next_slab("UP%d" % s)
                    for q in range(4):
                        ci = 2 * s + q if q < 2 else 22 + 2 * s + (q - 2)
                        bu = nbank()
                        mm_feat(bu, sl, r_sl, q * P, [(k, h2T[:, k, c0:TW]) for k in range(KC)], flat(r_h2T))
                        u = nxt("ub", 3)
                        B.op("act", lambda e, bu=bu, u=u: e.activation(out=ub[u][:, 2 + c0:2 + TW], in_=banks[bu][:, c0:TW],
                                                                        func=AF.Copy),
                             reads=[r_bank[bu]], writes=[r_ub[u]])
                        if halo:
                            B.op("pool", lambda e, u=u, ci=ci: e.tensor_scalar(
                                out=uhalo[:, ci, :], in0=ub[u][:, TW:TW + 2], scalar1=colap(C_FLAG), scalar2=None,
                                op0=ALU.mult), reads=[r_ub[u], r_cols], writes=[r_uhalo])
                            continue
                        B.op("pool", lambda e, u=u, ci=ci: e.tensor_copy(out=ub[u][:, 0:2], in_=uhalo[:, ci, :]),
                             reads=[r_uhalo], writes=[r_ub[u]])
                        B.op("pool", lambda e, u=u, ci=ci: e.tensor_copy(out=uhalo[:, ci, :],
                                                                         in_=ub[u][:, TW:TW + 2]),
                             reads=[r_ub[u]], writes=[r_uhalo])
                        if pend3 is not None:
                            conv3(pend3)
                        pend3 = (s, q, ci, u, dfs[0], dfs[1])
                        if old_dfs is not None and q == 0:
                            release(old_dfs[2])
                            old_dfs = None
                    release(hd)
                if pend3 is not None:
                    conv3(pend3)
                if dfs is not None:
                    release(dfs[2])
                C0[0] = 0
                if halo:
                    return
                for hf in range(2):
                    bks = [nbank() for _ in range(NB)]
                    for b_ in bks:
                        pinned.add(b_)
                    for pt in range(3):
                        sl, r_sl, hd = next_slab("DN%d" % (hf * 3 + pt))
                        nk = 8 if pt < 2 else 6
                        sv = sl[:, :].rearrange("p (kc c) -> p kc c", c=512)

                        def dn(e, pt=pt, nk=nk, sv=sv, bks=bks):
                            last = None
                            for kk in range(nk):
                                fc = 8 * pt + kk
                                for b in range(NB):
                                    last = e.matmul(banks[bks[b]][:, :], lhsT=gT[:, fc, b * P:(b + 1) * P],
                                                    rhs=sv[:, kk, :], start=(fc == 0), stop=(fc == 21))
                            return last
                        B.op("pe", dn, reads=[r_sl] + flat(r_gT[8 * pt:8 * pt + nk]),
                             writes=[r_bank[b_] for b_ in bks])
                        release(hd)
                    for b in range(NB):
                        B.op("dve", lambda e, b=b, hf=hf, bks=bks, slot=slot: e.tensor_tensor(
                            out=xtm[slot][:, b, hf * 512:(hf + 1) * 512],
                            in0=xtm[slot][:, b, hf * 512:(hf + 1) * 512], in1=banks[bks[b]][:, :], op=ALU.add),
                            reads=[r_bank[bks[b]], r_xtm[slot]], writes=[r_xtm[slot]])
                    for b_ in bks:
                        pinned.discard(b_)
                rms_stats(slot)
                for b in range(NB):
                    B.op("dve", lambda e, b=b, slot=slot: e.scalar_tensor_tensor(
                        out=xtm[slot][:, b, :], in0=xtm[slot][:, b, :], scalar=rstd[:, b:b + 1], in1=gfin[:, :],
                        op0=ALU.mult, op1=ALU.mult), reads=[r_xtm[slot], r_rstd, r_gfin], writes=[r_xtm[slot]])
                o0 = (mi - 1) * TW
                dst = out_d[o0:o0 + TW, :].rearrange("(b p) f -> p b f", p=P)
                r_out = rs("outst%d" % slot)
                ev = dma_op("sp", lambda e, dst=dst, slot=slot: [e.dma_start(out=dst, in_=xtm[slot][:, :, :])],
                            r_out, 1, reads=[r_xtm[slot]])
                out_events.append(ev)

            ntile = npre + nmain
            x_loaded = set()

            def ensure_x(t):
                if t < ntile and t not in x_loaded:
                    x_loaded.add(t)
                    load_x(t)
            ensure_x(0)
            ensure_x(1)
            front_normA(0)
            front_normB()
            if npre == 0:
                lazy_diag(99)
                front_glu(256)
            for ti in range(ntile):
                nxt_main = (ti + 1 >= npre)
                has_next = (ti + 1 < ntile)
                if ti < npre:
                    ensure_x(ti + 2)
                ensure_x(ti + 1)

                def hoist_normA(ti=ti, has_next=has_next):
                    if has_next:
                        front_normA(ti + 1)

                def hoist_normB(has_next=has_next):
                    if has_next:
                        old = C0[0]
                        C0[0] = 0
                        front_normB()
                        C0[0] = old

                def hoist_glu(ti=ti, has_next=has_next, nxt_main=nxt_main):
                    if has_next and nxt_main:
                        front_glu(256 if ti + 1 == npre else 0)

                def lazy(ti=ti):
                    lazy_diag(99 if ti == npre - 1 else 2)
                if ti < npre:
                    pre_body(ti, hoist_normA, hoist_normB, hoist_glu, lazy)
                else:
                    main_body(ti, ti - npre, hoist_normA, hoist_normB, hoist_glu)
            if not dry:
                assert st["pos"] == len(seq), (st["pos"], len(seq))
                fin = {}
                for k, v in out_events:
                    fin[k] = max(fin.get(k, 0), v)
                B.q["sp"].append((list(fin.items()), lambda e: e.nop(), None))
            return rec

        seq_rec = emit(Builder(None, None, dry=True), None)
        B = Builder(nc, stack)
        emit(B, seq_rec)

        with nc.Block() as block:
            @block.tensor
            def _(e):
                B.run("pe", e)

            @block.scalar
            def _(e):
                B.run("act", e)

            @block.vector
            def _(e):
                B.run("dve", e)

            @block.gpsimd
            def _(e):
                B.run("pool", e)

            @block.sync
            def _(e):
                B.run("sp", e)
    return nc


def host_consts():
    cst = np.zeros((P, 384), np.float32)
    cst[:, 0:128] = np.eye(P, dtype=np.float32)
    s = np.arange(P)[:, None]
    t = np.arange(P)[None, :]
    cst[:, 128:256] = np.where((s > t) & (s // 64 == t // 64), -1.0 / 16.0, 0.0)
    cst[:, 256:384] = 1.0
    mt = np.zeros((P, 2), np.float32)
    mt[:64, 0] = -1.0 / 16.0
    mt[64:, 1] = -1.0 / 16.0
    return cst, mt


def host_cols(p, flag):
    cols = np.zeros((P, NCOLS), np.float32)

    def put(off, vec):
        v = np.asarray(vec, np.float32).reshape(-1, P).T
        cols[:, off:off + v.shape[1]] = v
    put(C_GMIX, p["norm_mix"][0])
    put(C_GFFN, p["norm_ffn"][0])
    put(C_BM, p["b_merge"][0])
    put(C_CB, p["conv_dw_b"][0])
    put(C_LNG, p["conv_ln_g"][0])
    put(C_LNB, p["conv_ln_b"][0])
    put(C_GN, p["gla_norm"][0].reshape(-1))
    put(C_FB, p["ffn_dw_b"][0])
    cd = np.asarray(p["conv_dw"][0], np.float32)
    cols[:, C_C31:C_C31 + 248] = cd.reshape(31, 8, P).transpose(2, 1, 0).reshape(P, 248)
    fd = np.asarray(p["ffn_dw"][0], np.float32)
    cols[:, C_F3:C_F3 + 132] = fd.reshape(3, 44, P).transpose(2, 1, 0).reshape(P, 132)
    cols[:, C_FLAG] = flag
    cols[:, C_EPS] = EPS
    cols[:, C_ONE] = 1.0
    return cols


def host_diag(p):
    dg = np.zeros((14, P, 4096), np.float32)
    idx = np.arange(P)
    cd = np.asarray(p["conv_dw"][0], np.float32)
    for c in range(8):
        for j in range(31):
            dg[c, idx, j * P + idx] = cd[j, c * P:(c + 1) * P]
    fd = np.asarray(p["ffn_dw"][0], np.float32)
    for f in range(6):
        m = 0
        for ss in (2 * f, 2 * f + 1):
            if ss > 10:
                break
            for q in range(4):
                ci = 2 * ss + q if q < 2 else 22 + 2 * ss + (q - 2)
                for j in range(3):
                    dg[8 + f, idx, m * P + idx] = fd[j, ci * P:(ci + 1) * P]
                    m += 1
    return dg


def make_in_maps(p, xs, flags):
    cst, mt = host_consts()
    gfin = np.ascontiguousarray(np.broadcast_to(np.asarray(p["norm_final"], np.float32)[None, :], (P, D)))
    wgk = np.concatenate([np.asarray(p["w_gk2"][0], np.float32), np.asarray(p["b_gk"][0], np.float32)[None, :]], 0)
    shared = {
        "w_in": np.ascontiguousarray(np.asarray(p["w_in"][0], np.float32)),
        "w_conv_out": np.ascontiguousarray(np.asarray(p["w_conv_out"][0], np.float32)),
        "w_gla_out": np.ascontiguousarray(np.asarray(p["w_gla_out"][0], np.float32)),
        "w_out": np.ascontiguousarray(np.asarray(p["w_out"][0], np.float32)),
        "w_up": np.ascontiguousarray(np.asarray(p["w_up"][0], np.float32)),
        "w_down": np.ascontiguousarray(np.asarray(p["w_down"][0], np.float32)),
        "gfin": gfin, "cst": cst, "mtot": mt, "wgk": np.ascontiguousarray(wgk),
    }
    maps = []
    for xc, fl in zip(xs, flags):
        m = dict(shared)
        m["x"] = np.ascontiguousarray(xc, dtype=np.float32)
        m["cols"] = host_cols(p, fl)
        maps.append(m)
    return maps


NPRE, NMAIN = 7, 9
_CACHE = {}


def kernel(**inputs):
    x = np.asarray(inputs["x"], np.float32)
    bsz, seq, _ = x.shape
    half = seq // 2
    xs, flags = [], []
    for c in range(8):
        b, hf = c // 2, c % 2
        if hf == 1:
            xs.append(x[b])
            flags.append(1.0)
        else:
            xs.append(np.concatenate([np.zeros((half, D), np.float32), x[b, :half]], 0))
            flags.append(0.0)
    maps = make_in_maps(inputs, xs, flags)
    if "nc" not in _CACHE:
        _CACHE["nc"] = build_program(NPRE, NMAIN)
    res = run_bass_kernel_spmd(_CACHE["nc"], maps, core_ids=list(range(8)))
    y = np.empty((bsz, seq, D), np.float32)
    for c in range(8):
        b, hf = c // 2, c % 2
        y[b, hf * half:(hf + 1) * half] = res.results[c]["out"]
    return y
```

```python
import contextlib
import numpy as np
import concourse.bass as bass
import concourse.mybir as mybir
from concourse.bass_utils import run_bass_kernel_spmd

F32 = mybir.dt.float32
BF16 = mybir.dt.bfloat16
AF = mybir.ActivationFunctionType
ALU = mybir.AluOpType

P = 128
D = 1024
KC = 8
TW = 512
NB = 4
IN_DIM = 7184
DFF = 2816
EPS = 1e-6
O_CVAL, O_CGATE, O_Q, O_K, O_V, O_GO, O_GL, O_M = 0, 1024, 2048, 2560, 3072, 4096, 5120, 5136

C_GMIX, C_GFFN, C_BM, C_CB, C_LNG, C_LNB, C_GN, C_FB = 0, 8, 16, 32, 40, 48, 56, 64
C_C31, C_F3, C_FLAG, C_EPS, C_ONE, NCOLS = 108, 356, 488, 489, 490, 496

NSLOT = 5
ENGS = ("pe", "act", "dve", "pool", "sp")


class Res:
    __slots__ = ("name", "last_w", "readers", "const", "sem", "semval")

    def __init__(self, name, const=False):
        self.name = name
        self.last_w = None
        self.readers = {}
        self.const = const
        self.sem = None
        self.semval = 0


class Builder:
    def __init__(self, nc, stack, dry=False):
        self.nc = nc
        self.stack = stack
        self.dry = dry
        self.q = {e: [] for e in ENGS}
        self.cnt = {e: 0 for e in ENGS}
        self.waited = {e: {} for e in ENGS}
        self.sems = {}
        for e in ENGS:
            self.sems[e] = None if dry else stack.enter_context(nc.semaphore("sem_" + e))
        self.nres = 0

    def res(self, name, const=False):
        return Res(name, const)

    def _dma_sem(self, r):
        if r.sem is None:
            key = "d%d" % len(self.sems)
            self.sems[key] = None if self.dry else self.stack.enter_context(self.nc.semaphore(key))
            r.sem = key
        return r.sem

    def _waits(self, eng, reads, writes):
        need = {}

        def add(ev, own_ok):
            if ev is None:
                return
            k, v = ev
            if k == eng:
                if eng == "pe" or not own_ok:
                    return
            if need.get(k, 0) < v:
                need[k] = v

        for r in reads:
            add(r.last_w, True)
        for r in writes:
            add(r.last_w, True)
            for k, v in r.readers.items():
                add((k, v), False)
        out = []
        w = self.waited[eng]
        for k, v in need.items():
            if w.get(k, 0) < v:
                w[k] = v
                out.append((k, v))
        return out

    def _commit(self, ev, reads, writes):
        for r in reads:
            if not r.const:
                k, v = ev
                if r.readers.get(k, 0) < v:
                    r.readers[k] = v
        for r in writes:
            r.last_w = ev
            r.readers = {}

    def op(self, eng, fn, reads=(), writes=()):
        waits = self._waits(eng, reads, writes)
        self.cnt[eng] += 1
        ev = (eng, self.cnt[eng])
        self.q[eng].append((waits, fn, None))
        self._commit(ev, reads, writes)

    def dma(self, eng, fn, semres, reads=(), writes=()):
        waits = self._waits(eng, reads, writes)
        key = self._dma_sem(semres)
        holder = [0]
        self.q[eng].append((waits, fn, (key, holder)))
        return key, holder

    def run(self, eng, e):
        for waits, fn, dm in self.q[eng]:
            for k, v in waits:
                e.wait_ge(self.sems[k], v)
            r = fn(e)
            if dm is None:
                r.then_inc(self.sems[eng], 1)
            else:
                for ins in r:
                    ins.then_inc(self.sems[dm[0]], 16)


def slab_specs():
    S = {}
    for s in range(4):
        S["GLU%d" % s] = [("w_in", 0, 8, O_CVAL + 256 * s, 256, 0), ("w_in", 0, 8, O_CGATE + 256 * s, 256, 256)]
    S["K"] = [("w_in", 0, 8, O_K, 512, 0)]
    S["V0"] = [("w_in", 0, 8, O_V, 512, 0)]
    S["V1"] = [("w_in", 0, 8, O_V + 512, 512, 0)]
    S["Q"] = [("w_in", 0, 8, O_Q, 512, 0)]
    S["GO0"] = [("w_in", 0, 8, O_GO, 512, 0)]
    S["GO1"] = [("w_in", 0, 8, O_GO + 512, 512, 0)]
    for s in range(4):
        S["M%d" % s] = [("w_in", 0, 8, O_M + 512 * s, 512, 0)]
    for s in range(2):
        S["CO%d" % s] = [("w_conv_out", 0, 8, 512 * s, 512, 0)]
        S["GLO%d" % s] = [("w_gla_out", 0, 8, 512 * s, 512, 0)]
        S["WO%d" % s] = [("w_out", 0, 8, 512 * s, 512, 0)]
    for s in range(11):
        S["UP%d" % s] = [("w_up", 0, 8, 256 * s, 256, 0), ("w_up", 0, 8, DFF + 256 * s, 256, 256)]
    for hf in range(2):
        for pt in range(3):
            nk = 8 if pt < 2 else 6
            S["DN%d" % (hf * 3 + pt)] = [("w_down", 8 * pt * 128, nk, 512 * hf, 512, 0)]
    for c in range(8):
        S["D31_%d" % c] = "diag"
    for f in range(6):
        S["DF%d" % f] = "diag"
    return S


def build_program(npre, nmain):
    NTOK = (npre + nmain) * TW
    NOUT = (nmain - 1) * TW
    nc = bass.Bass("TRN2", target_bir_lowering=False)

    def din(name, shape):
        return nc.dram_tensor(name, shape, F32, kind="ExternalInput").ap()

    x_d = din("x", [NTOK, D])
    W = {
        "w_in": din("w_in", [D, IN_DIM]),
        "w_conv_out": din("w_conv_out", [D, D]),
        "w_gla_out": din("w_gla_out", [D, D]),
        "w_out": din("w_out", [D, D]),
        "w_up": din("w_up", [D, 2 * DFF]),
        "w_down": din("w_down", [DFF, D]),
    }
    cols_d = din("cols", [P, NCOLS])
    gfin_d = din("gfin", [P, D])
    cst_d = din("cst", [P, 384])
    wgk_d = din("wgk", [17, 512])
    mt_d = din("mtot", [P, 2])
    out_d = nc.dram_tensor("out", [NOUT, D], F32, kind="ExternalOutput").ap()

    specs = slab_specs()
    names = list(specs.keys())
    sidx = {n: i for i, n in enumerate(names)}
    scr = nc.dram_tensor("wscr", [len(names), P, 4096], BF16, kind="Internal").ap()

    with contextlib.ExitStack() as stack:

        def sb(name, shape, dt):
            return stack.enter_context(nc.sbuf_tensor("sb_" + name, shape, dt))

        cols = sb("cols", [P, NCOLS], F32)
        gfin = sb("gfin", [P, D], F32)
        cst = sb("cst", [P, 384], F32)
        mt = sb("mtot", [P, 2], F32)
        identb = sb("identb", [P, P], BF16)
        onesb = sb("onesb", [P, P], BF16)
        mrevb = sb("mrevb", [P, P], BF16)
        mtb = sb("mtb", [P, 2], BF16)
        wgk = sb("wgk", [17, 512], BF16)
        wgl = sb("wgl", [P, KC, 16], BF16)
        xtm = [sb("xtm%d" % i, [P, NB, D], F32) for i in range(2)]
        arA = sb("arenaA", [P, 24 * 512], BF16)
        arB = sb("arenaB", [P, 16 * 512], BF16)
        arC = sb("arenaC", [P, 16 * 512], BF16)
        cT = arA[:, 0:16 * 512].bitcast(F32).rearrange("p (k c) -> p k c", c=TW)
        sT = arA[:, 16 * 512:24 * 512].rearrange("p (k c) -> p k c", c=TW)
        htm = sb("htm", [P, NB, D], BF16)
        gT = arA[:, 0:22 * 512].rearrange("p (k c) -> p k c", c=TW)
        oT = arB[:, :].bitcast(F32).rearrange("p (k c) -> p k c", c=TW)
        h2T = arB[:, 0:8 * 512].rearrange("p (k c) -> p k c", c=TW)
        mixT = arB[:, 8 * 512:16 * 512].rearrange("p (k c) -> p k c", c=TW)
        vv = arC[:, 0:8 * 512].rearrange("p (b f) -> p b f", f=D)
        kd = arC[:, 8 * 512:12 * 512].rearrange("p (b f) -> p b f", f=512)
        qT = arC[:, 12 * 512:16 * 512].rearrange("p (h c) -> p h c", c=TW)
        gab = arC[:, :].rearrange("p (k c) -> p k c", c=TW)
        vv1 = arA[:, 0:8 * 512].rearrange("p (b f) -> p b f", f=D)
        kd1 = arA[:, 8 * 512:12 * 512].rearrange("p (b f) -> p b f", f=512)
        vvS, kdS = [vv, vv1], [kd, kd1]
        hT = sb("hT", [P, KC, TW], BF16)
        aT = sb("aT", [P, KC, 32 + TW], BF16)
        glT = sb("glT", [32, TW], BF16)
        NTF = 8
        tmpf = [sb("tmpf%d" % i, [P, TW], F32) for i in range(NTF)]
        dtotS = [sb("dtot0", [P, 32], F32), sb("dtot1", [P, 32], F32)]
        S_f = sb("S_f", [P, 4, 256], F32)
        S_b = sb("S_b", [P, 4, 256], BF16)
        sg = sb("sg", [P, KC, TW], BF16)
        tmpb = [sb("tmpb%d" % i, [P, TW], BF16) for i in range(4)]
        ub = [sb("ub%d" % i, [P, TW + 2], BF16) for i in range(3)]
        uhalo = sb("uhalo", [P, 44, 2], BF16)
        sgt = sb("sgt", [P, 2, TW], BF16)
        junk = sb("junk", [P, D], BF16)
        ssq = sb("ssq", [P, 8], F32)
        rstd = sb("rstd", [P, 8], F32)
        NCOLS_OF = {}
        slabs = [sb("slab%d" % i, [P, 4096], BF16) for i in range(NSLOT)]
        banks = [stack.enter_context(nc.psum_tensor("bank%d" % i, [P, 512], F32)) for i in range(8)]
        banks_bf = [b.bitcast(BF16) for b in banks]

        def emit(B, seq):
            dry = seq is None
            rec = []
            R = {}

            def rs(name, const=False):
                if name not in R:
                    R[name] = B.res(name, const)
                return R[name]

            r_bank = [rs("bank%d" % i) for i in range(8)]
            r_slab = [rs("slab%d" % i) for i in range(NSLOT)]
            r_scr = [rs("scr%d" % i) for i in range(len(names))]
            gA = [rs("gA%d" % i) for i in range(24)]
            gB = [rs("gB%d" % i) for i in range(16)]
            gC = [rs("gC%d" % i) for i in range(16)]
            r_cT = [[gA[2 * k], gA[2 * k + 1]] for k in range(KC)]
            r_sT = [[gA[16 + k]] for k in range(KC)]
            r_htm = [[rs("htm%d" % b)] for b in range(NB)]
            r_gT = [[gA[j]] for j in range(22)]
            r_oT = [[gB[2 * k], gB[2 * k + 1]] for k in range(KC)]
            r_oT_all = list(gB)
            r_h2T = [[gB[k]] for k in range(KC)]
            r_mixT = [[gB[8 + k]] for k in range(KC)]
            r_vvS = [[[gC[2 * b], gC[2 * b + 1]] for b in range(NB)], [[gA[2 * b], gA[2 * b + 1]] for b in range(NB)]]
            r_kdS = [[[gC[8 + b]] for b in range(NB)], [[gA[8 + b]] for b in range(NB)]]
            r_qT = [[gC[12 + h]] for h in range(4)]
            r_gab = [[gC[k]] for k in range(16)]
            r_hT = [[rs("hT%d" % k)] for k in range(KC)]
            r_aT = [[rs("aT%d" % k)] for k in range(KC)]
            r_sg = [[rs("sg%d" % k)] for k in range(KC)]
            r_tmpf = [rs("tmpf%d" % i) for i in range(NTF)]
            r_tmpb = [rs("tmpb%d" % i) for i in range(4)]
            r_ub = [rs("ub%d" % i) for i in range(3)]
            r_sgt = [rs("sgt0"), rs("sgt1")]
            r_xtm = [rs("xtm0"), rs("xtm1")]
            r_glT, r_S, r_Sb, r_uhalo = rs("glT"), rs("S_f"), rs("S_b"), rs("uhalo")
            r_dtotS, r_ssq, r_rstd = [rs("dtot0"), rs("dtot1")], rs("ssq"), rs("rstd")

            def flat(lst):
                out = []
                for x_ in lst:
                    out += x_
                return out

            bank_rr = [0]
            pinned = set()
            C0 = [0]

            def nbank():
                while True:
                    i = bank_rr[0] % 8
                    bank_rr[0] += 1
                    if i not in pinned:
                        return i

            rrc = {"tf": 0, "tb": 0, "ub": 0}

            def nxt(key, n):
                v = rrc[key] % n
                rrc[key] += 1
                return v

            def colap(c):
                return cols[:, c:c + 1]

            def dma_op(eng, fn, semres, n, reads=(), writes=()):
                B.dma(eng, fn, semres, reads=reads, writes=writes)
                semres.semval += 16 * n
                ev = (semres.sem, semres.semval)
                B._commit(ev, reads, writes)
                return ev

            r_cols, r_gfin, r_cst, r_mt = rs("cols", True), rs("gfin", True), rs("cst", True), rs("mtot", True)
            r_wgl, r_identb, r_onesb, r_wgk = rs("wgl", True), rs("identb", True), rs("onesb", True), rs("wgk", True)
            dma_op("sp", lambda e: [e.dma_start(out=cols[:, :], in_=cols_d[:, :])], r_cols, 1, writes=[r_cols])
            dma_op("sp", lambda e: [e.dma_start(out=cst[:, :], in_=cst_d[:, :])], r_cst, 1, writes=[r_cst])
            dma_op("sp", lambda e: [e.dma_start(out=gfin[:, :], in_=gfin_d[:, :])], r_gfin, 1, writes=[r_gfin])
            dma_op("sp", lambda e: [e.dma_start(out=mt[:, :], in_=mt_d[:, :])], r_mt, 1, writes=[r_mt])
            dma_op("sp", lambda e: [e.dma_start(out=tmpf[0][0:17, :], in_=wgk_d[:, :])], r_tmpf[0], 1,
                   writes=[r_tmpf[0]])
            dma_op("pool", lambda e: [e.dma_start(
                out=wgl[:, :, :], in_=W["w_in"][:, O_GL:O_GL + 16].rearrange("(kc p) c -> p kc c", p=P))],
                r_wgl, 1, writes=[r_wgl])
            B.op("dve", lambda e: e.tensor_copy(out=identb[:, :], in_=cst[:, 0:128]), reads=[r_cst], writes=[r_identb])
            B.op("dve", lambda e: e.tensor_copy(out=onesb[:, :], in_=cst[:, 256:384]), reads=[r_cst], writes=[r_onesb])
            r_mrevb, r_mtb = rs("mrevb", True), rs("mtb", True)
            B.op("dve", lambda e: e.tensor_copy(out=mrevb[:, :], in_=cst[:, 128:256]), reads=[r_cst], writes=[r_mrevb])
            B.op("dve", lambda e: e.tensor_copy(out=mtb[:, :], in_=mt[:, :]), reads=[r_mt], writes=[r_mtb])
            B.op("dve", lambda e: e.tensor_copy(out=wgk[:, :], in_=tmpf[0][0:17, :]), reads=[r_tmpf[0]],
                 writes=[r_wgk])
            B.op("dve", lambda e: e.memset(glT[:, :], 1.0), writes=[r_glT])
            B.op("dve", lambda e: e.memset(aT[:, :, :], 0.0), writes=flat(r_aT))
            B.op("dve", lambda e: e.memset(S_f[:, :, :], 0.0), writes=[r_S])
            B.op("dve", lambda e: e.memset(uhalo[:, :, :], 0.0), writes=[r_uhalo])

            def convert(name):
                i = sidx[name]
                pieces = specs[name]
                dstv = scr[i].rearrange("p (kc c) -> p kc c", c=512)

                def fn(e, pieces=pieces, dstv=dstv):
                    out = []
                    for (src, r0, nk, c0, ncol, coff) in pieces:
                        s_ap = W[src][r0:r0 + nk * P, c0:c0 + ncol].rearrange("(kc p) c -> p kc c", p=P)
                        out.append(e.dma_start(out=dstv[:, 0:nk, coff:coff + ncol], in_=s_ap))
                    return out
                dma_op("pool", fn, r_scr[i], len(pieces), writes=[r_scr[i]])

            slot_rr = [0]

            dhalf = [0]

            dhalf = [0]

            def build_diag(name, runs):
                i = sidx[name]
                h = dhalf[0] % 2
                dhalf[0] += 1
                base = h * 4096
                res_h = gB[8 * h:8 * h + 8]
                nblk = 0
                for (m0, col0, cnt) in runs:
                    nblk = max(nblk, m0 + cnt)
                    ov = arB[:, base + m0 * P:base + (m0 + cnt) * P].rearrange("p (j m) -> p j m", m=P)
                    ia = identb[:, :].unsqueeze(1).broadcast_to([P, cnt, P])
                    wa = cols[:, col0:col0 + cnt].unsqueeze(2).broadcast_to([P, cnt, P])
                    B.op("dve", lambda e, ov=ov, ia=ia, wa=wa: e.tensor_tensor(out=ov, in0=ia, in1=wa, op=ALU.mult),
                         reads=[r_identb, r_cols], writes=res_h)
                n = nblk * P
                dma_op("sp", lambda e, i=i, n=n, base=base: [e.dma_start(out=scr[i][:, 0:n],
                                                                         in_=arB[:, base:base + n])],
                       r_scr[i], 1, reads=res_h, writes=[r_scr[i]])

            order1 = ["K", "V0", "V1"] + ["GLU%d" % s for s in range(4)] + ["Q", "GO0", "GO1"]
            order2 = (["M%d" % s for s in range(4)] + ["CO0", "CO1", "GLO0", "GLO1", "WO0", "WO1"] +
                      ["UP%d" % s for s in range(11)] + ["DN%d" % s for s in range(6)])
            for n_ in order1[:3]:
                convert(n_)
            conv_jobs = order1[3:] + order2

            def df_runs(f):
                out = []
                for ss in (2 * f, 2 * f + 1):
                    if ss > 10:
                        break
                    mb = (ss - 2 * f) * 12
                    out.append((mb, C_F3 + (2 * ss) * 3, 6))
                    out.append((mb + 6, C_F3 + (22 + 2 * ss) * 3, 6))
                return out
            diag_jobs = ([("D31_%d" % c, [(0, C_C31 + c * 31, 31)]) for c in range(8)] +
                         [("DF%d" % f, df_runs(f)) for f in range(6)])

            def lazy_diag(n):
                for _ in range(n):
                    if diag_jobs:
                        nm, ci_ = diag_jobs.pop(0)
                        build_diag(nm, ci_)
                for _ in range(n * 4 if n < 99 else 999):
                    if conv_jobs:
                        convert(conv_jobs.pop(0))

            def ncols_of(nm):
                if nm.startswith("D31"):
                    return 31 * P
                if nm.startswith("DF"):
                    return (24 if nm != "DF5" else 12) * P
                if nm in ("DN2", "DN5"):
                    return 6 * 512
                return 4096
            st = {"issued": 0, "pos": 0}
            slot_of = {}
            live = set()

            def next_slab(name):
                pos = st["pos"]
                st["pos"] += 1
                live.add(pos)
                if dry:
                    rec.append(name)
                    return slabs[0], r_slab[0], pos
                assert seq[pos] == name, (seq[pos], name)
                lim = min(len(seq), min(live) + NSLOT)
                assert pos < lim, "too many live slabs"
                while st["issued"] < lim:
                    nm = seq[st["issued"]]
                    if nm in conv_jobs:
                        conv_jobs.remove(nm)
                        convert(nm)
                    for dj in list(diag_jobs):
                        if dj[0] == nm:
                            diag_jobs.remove(dj)
                            build_diag(dj[0], dj[1])
                    ncols = ncols_of(nm)
                    i = sidx[nm]
                    slot = slot_rr[0] % NSLOT
                    slot_rr[0] += 1
                    sl = slabs[slot]
                    dma_op("sp", lambda e, sl=sl, i=i, ncols=ncols: [e.dma_start(out=sl[:, 0:ncols],
                                                                                  in_=scr[i][:, 0:ncols])],
                           r_slab[slot], 1, reads=[r_scr[i]], writes=[r_slab[slot]])
                    slot_of[st["issued"]] = slot
                    st["issued"] += 1
                s_ = slot_of[pos]
                return slabs[s_], r_slab[s_], pos

            def release(h):
                live.discard(h)

            def load_x(ti):
                slot = ti % 2
                t0 = ti * TW
                src = x_d[t0:t0 + TW, :].rearrange("(b p) f -> p b f", p=P)
                dma_op("sp", lambda e, src=src, slot=slot: [e.dma_start(out=xtm[slot][:, :, :], in_=src)],
                       r_xtm[slot], 1, writes=[r_xtm[slot]])

            def rms_stats(slot):
                for b in range(NB):
                    B.op("act", lambda e, b=b, slot=slot: e.activation(
                        out=junk[:, :], in_=xtm[slot][:, b, :], func=AF.Square, accum_out=ssq[:, b:b + 1]),
                        reads=[r_xtm[slot]], writes=[r_ssq])
                B.op("act", lambda e: e.activation(out=rstd[:, 4:8], in_=ssq[:, 0:4], func=AF.Ln,
                                                    bias=colap(C_EPS), scale=1.0 / D),
                     reads=[r_ssq, r_cols], writes=[r_rstd])
                B.op("act", lambda e: e.activation(out=rstd[:, 0:4], in_=rstd[:, 4:8], func=AF.Exp, scale=-0.5),
                     reads=[r_rstd], writes=[r_rstd])

            def normA(slot):
                rms_stats(slot)
                for b in range(NB):
                    B.op("dve", lambda e, b=b, slot=slot: e.tensor_scalar(
                        out=htm[:, b, :], in0=xtm[slot][:, b, :], scalar1=rstd[:, b:b + 1], scalar2=None,
                        op0=ALU.mult), reads=[r_xtm[slot], r_rstd], writes=r_htm[b])

            def normB(dstT, r_dstT, gcol):
                for fc in range(KC):
                    bi = nbank()

                    def tr(e, fc=fc, bi=bi):
                        last = None
                        for b in range(NB):
                            last = e.transpose(out=banks_bf[bi][:, b * P:(b + 1) * P],
                                               in_=htm[:, b, fc * P:(fc + 1) * P], identity=identb[:, :])
                        return last
                    B.op("pe", tr, reads=flat(r_htm) + [r_identb], writes=[r_bank[bi]])
                    B.op("act", lambda e, fc=fc, bi=bi: e.activation(
                        out=dstT[:, fc, :], in_=banks_bf[bi][:, 0:TW], func=AF.Identity, scale=colap(gcol + fc)),
                        reads=[r_bank[bi], r_cols], writes=r_dstT[fc])

            def mm_feat(bi, slab, r_sl, col0, rhs_list, r_rhs):
                sv = slab[:, :].rearrange("p (kc c) -> p kc c", c=512)
                n = len(rhs_list)
                c0 = C0[0]

                def fn(e):
                    last = None
                    for i, (k, ap) in enumerate(rhs_list):
                        last = e.matmul(banks[bi][:, c0:TW], lhsT=sv[:, k, col0:col0 + P], rhs=ap,
                                        start=(i == 0), stop=(i == n - 1))
                    return last
                B.op("pe", fn, reads=[r_sl] + list(r_rhs), writes=[r_bank[bi]])

            def mm_tok(bi, lhs_list, r_act, slab, r_sl):
                sv = slab[:, :].rearrange("p (kc c) -> p kc c", c=512)
                n = len(lhs_list)

                def fn(e):
                    last = None
                    for i, (k, ap) in enumerate(lhs_list):
                        last = e.matmul(banks[bi][:, :], lhsT=ap, rhs=sv[:, k, :], start=(i == 0),
                                        stop=(i == n - 1))
                    return last
                B.op("pe", fn, reads=[r_sl] + list(r_act), writes=[r_bank[bi]])

            def hT_rhs():
                return [(k, hT[:, k, C0[0]:TW]) for k in range(KC)]

            def hT_blk(b):
                return [(k, hT[:, k, b * P:(b + 1) * P]) for k in range(KC)]

            def rstd_from(dst, r_dst, src_ap, r_src, scale):
                B.op("act", lambda e: e.activation(out=dst, in_=src_ap, func=AF.Ln, bias=colap(C_EPS), scale=scale),
                     reads=list(r_src) + [r_cols], writes=[r_dst])
                B.op("act", lambda e: e.activation(out=dst, in_=dst, func=AF.Exp, scale=-0.5),
                     reads=[r_dst], writes=[r_dst])

            def front_normA(ti):
                normA(ti % 2)

            def front_normB():
                normB(hT, r_hT, C_GMIX)

            def front_glu(c0g=0):
                c0 = c0g
                old_c0 = C0[0]
                C0[0] = c0g
                for s in range(4):
                    sl, r_sl, hd = next_slab("GLU%d" % s)
                    for j in range(2):
                        cc = 2 * s + j
                        ba, bg = nbank(), nbank()
                        mm_feat(ba, sl, r_sl, j * P, hT_rhs(), flat(r_hT))
                        mm_feat(bg, sl, r_sl, 256 + j * P, hT_rhs(), flat(r_hT))
                        t = nxt("tb", 4)
                        B.op("act", lambda e, bg=bg, t=t: e.activation(out=tmpb[t][:, c0:TW], in_=banks[bg][:, c0:TW],
                                                                         func=AF.Sigmoid),
                             reads=[r_bank[bg]], writes=[r_tmpb[t]])
                        B.op("dve", lambda e, ba=ba, t=t, cc=cc: e.tensor_tensor(
                            out=aT[:, cc, 32 + c0:32 + TW], in0=banks[ba][:, c0:TW], in1=tmpb[t][:, c0:TW], op=ALU.mult),
                            reads=[r_bank[ba], r_tmpb[t]], writes=r_aT[cc])
                    release(hd)
                C0[0] = old_c0

            def proj_phase(main, vset=0, filler=None, after_glow=None):
                vv_, kd_, dtot_ = vvS[vset], kdS[vset], dtotS[vset]
                r_vv, r_kd, r_dtot = r_vvS[vset], r_kdS[vset], r_dtotS[vset]

                def fill(n):
                    if filler is not None:
                        for _ in range(n):
                            if filler:
                                filler.pop(0)()

                def vgrp(vs, b, sl, r_sl):
                    bi = nbank()
                    mm_tok(bi, hT_blk(b), flat(r_hT), sl, r_sl)
                    B.op("act", lambda e, bi=bi, b=b, vs=vs: e.activation(
                        out=vv_[:, b, vs * 512:(vs + 1) * 512], in_=banks[bi][:, :], func=AF.Copy),
                        reads=[r_bank[bi]], writes=r_vv[b])

                def qgrp(h, sl, r_sl):
                    c0 = C0[0]
                    bi = nbank()
                    mm_feat(bi, sl, r_sl, h * P, hT_rhs(), flat(r_hT))
                    B.op("act", lambda e, bi=bi, h=h: e.activation(out=qT[:, h, c0:TW], in_=banks[bi][:, c0:TW],
                                                                    func=AF.Copy, scale=float(128 ** -0.5)),
                         reads=[r_bank[bi]], writes=r_qT[h])

                def gogrp(gc, j, sl, r_sl):
                    c0 = C0[0]
                    bi = nbank()
                    mm_feat(bi, sl, r_sl, j * P, hT_rhs(), flat(r_hT))
                    B.op("act", lambda e, bi=bi, gc=gc: e.activation(out=sg[:, gc, c0:TW], in_=banks[bi][:, c0:TW],
                                                                      func=AF.Silu),
                         reads=[r_bank[bi]], writes=r_sg[gc])

                bi = nbank()

                def fn(e, bi=bi):
                    last = None
                    for k in range(KC):
                        last = e.matmul(banks[bi][0:16, :], lhsT=wgl[:, k, :], rhs=hT[:, k, :],
                                        start=(k == 0), stop=(k == KC - 1))
                    return last
                B.op("pe", fn, reads=[r_wgl] + flat(r_hT), writes=[r_bank[bi]])
                B.op("act", lambda e, bi=bi: e.activation(out=glT[0:16, :], in_=banks[bi][0:16, :], func=AF.Copy),
                     reads=[r_bank[bi]], writes=[r_glT])
                fill(2)
                if after_glow is not None:
                    after_glow()
                slV0, r_slV0, hV0 = next_slab("V0")
                for b in range(NB):
                    vgrp(0, b, slV0, r_slV0)
                release(hV0)
                slabK, r_slK, hK = next_slab("K")
                bt = nbank()
                pinned.add(bt)
                slV1, r_slV1, hV1 = next_slab("V1")
                vdone = [0]
                for pair in range(2):
                    blks = (2 * pair, 2 * pair + 1)
                    las, Es = {}, {}
                    for b in blks:
                        bi = nbank()
                        B.op("pe", lambda e, b=b, bi=bi: e.matmul(banks[bi][:, :], lhsT=glT[0:17, b * P:(b + 1) * P],
                                                                  rhs=wgk[0:17, :], start=True, stop=True),
                             reads=[r_glT, r_wgk], writes=[r_bank[bi]])
                        f_e, f_la = nxt("tf", NTF), nxt("tb", 4)
                        las[b] = f_la
                        B.op("act", lambda e, bi=bi, f_e=f_e: e.activation(out=tmpf[f_e][:, :], in_=banks[bi][:, :],
                                                                            func=AF.Exp, scale=-1.0),
                             reads=[r_bank[bi]], writes=[r_tmpf[f_e]])
                        B.op("act", lambda e, f_e=f_e, f_la=f_la: e.activation(
                            out=tmpb[f_la][:, :], in_=tmpf[f_e][:, :], func=AF.Ln, bias=colap(C_ONE), scale=1.0),
                            reads=[r_tmpf[f_e], r_cols], writes=[r_tmpb[f_la]])
                    for _ in range(2):
                        vgrp(1, vdone[0], slV1, r_slV1)
                        vdone[0] += 1
                    fill(1)
                    for b in blks:
                        f_la = las[b]
                        bi2 = nbank()
                        B.op("pe", lambda e, bi2=bi2, f_la=f_la: e.matmul(banks[bi2][:, :], lhsT=mrevb[:, :],
                                                                          rhs=tmpb[f_la][:, :], start=True, stop=True),
                             reads=[r_tmpb[f_la], r_mrevb], writes=[r_bank[bi2]])

                        def tot(e, b=b, f_la=f_la, bt=bt):
                            last = None
                            for h in range(4):
                                c0 = h * 8 + b * 2
                                last = e.matmul(banks[bt][:, c0:c0 + 2], lhsT=tmpb[f_la][:, h * P:(h + 1) * P],
                                                rhs=mtb[:, :], start=True, stop=True)
                            return last
                        B.op("pe", tot, reads=[r_tmpb[f_la], r_mtb], writes=[r_bank[bt]])
                        f_E = nxt("tf", NTF)
                        Es[b] = f_E
                        B.op("act", lambda e, bi2=bi2, f_E=f_E: e.activation(out=tmpf[f_E][:, :],
                                                                              in_=banks[bi2][:, :], func=AF.Exp),
                             reads=[r_bank[bi2]], writes=[r_tmpf[f_E]])
                    fill(1)
                    for b in blks:
                        f_E = Es[b]
                        bk = nbank()
                        mm_tok(bk, hT_blk(b), flat(r_hT), slabK, r_slK)
                        B.op("dve", lambda e, bk=bk, f_E=f_E, b=b: e.tensor_tensor(
                            out=kd_[:, b, :], in0=banks[bk][:, :], in1=tmpf[f_E][:, :], op=ALU.mult),
                            reads=[r_bank[bk], r_tmpf[f_E]], writes=r_kd[b])
                release(hK)
                release(hV1)
                B.op("act", lambda e, bt=bt: e.activation(out=dtot_[:, :], in_=banks[bt][:, 0:32], func=AF.Exp),
                     reads=[r_bank[bt]], writes=[r_dtot])
                pinned.discard(bt)
                if main:
                    sl, r_sl, hd = next_slab("Q")
                    for h in range(4):
                        qgrp(h, sl, r_sl)
                    release(hd)
                    for s in range(2):
                        sl, r_sl, hd = next_slab("GO%d" % s)
                        for j in range(4):
                            gogrp(4 * s + j, j, sl, r_sl)
                        release(hd)

            def gla_kv(c, vset=0):
                kd, vv, dtot = kdS[vset], vvS[vset], dtotS[vset]
                r_kd, r_vv, r_dtot = r_kdS[vset], r_vvS[vset], r_dtotS[vset]
                b, r0 = c // 2, (c % 2) * 64
                bkv = [nbank(), nbank()]

                def kvfn(e, b=b, r0=r0, bkv=bkv):
                    last = None
                    for h in range(4):
                        last = e.matmul(banks[bkv[h // 2]][:, (h % 2) * 256:(h % 2) * 256 + 256],
                                        lhsT=kd[r0:r0 + 64, b, h * P:(h + 1) * P],
                                        rhs=vv[r0:r0 + 64, b, h * 256:(h + 1) * 256], start=True, stop=True)
                    return last
                B.op("pe", kvfn, reads=r_kd[b] + r_vv[b], writes=[r_bank[bkv[0]], r_bank[bkv[1]]])
                for h in range(4):
                    B.op("dve", lambda e, h=h, c=c, bkv=bkv: e.scalar_tensor_tensor(
                        out=S_f[:, h, :], in0=S_f[:, h, :], scalar=dtot[:, h * 8 + c:h * 8 + c + 1],
                        in1=banks[bkv[h // 2]][:, (h % 2) * 256:(h % 2) * 256 + 256], op0=ALU.mult, op1=ALU.add),
                        reads=[r_S, r_dtot, r_bank[bkv[h // 2]]], writes=[r_S])

            def gla_sb(c):
                B.op("act", lambda e: e.activation(out=S_b[:, :, :], in_=S_f[:, :, :], func=AF.Copy),
                     reads=[r_S], writes=[r_Sb])

            def gla_o(c):
                bo = nbank()

                def ofn(e, c=c, bo=bo):
                    last = None
                    for h in range(4):
                        for hf in range(2):
                            j = 2 * h + hf
                            last = e.matmul(banks[bo][:, j * 64:(j + 1) * 64],
                                            lhsT=S_b[:, h, hf * P:(hf + 1) * P], rhs=qT[:, h, c * 64:(c + 1) * 64],
                                            start=True, stop=True)
                    return last
                B.op("pe", ofn, reads=[r_Sb] + flat(r_qT), writes=[r_bank[bo]])
                B.op("act", lambda e, c=c, bo=bo: e.activation(
                    out=oT[:, :, c * 64:(c + 1) * 64], in_=banks[bo][:, :].rearrange("p (a b) -> p a b", b=64),
                    func=AF.Copy), reads=[r_bank[bo]], writes=r_oT_all)

            NPE = 26

            def conv_dve(cc):
                c0 = C0[0]
                fa = nxt("tf", NTF)
                for j in range(NPE, 31):
                    wcol = colap(C_C31 + cc * 31 + j)
                    if j == NPE:
                        B.op("dve", lambda e, cc=cc, j=j, fa=fa, wcol=wcol: e.tensor_scalar(
                            out=tmpf[fa][:, c0:TW], in0=aT[:, cc, j + 2 + c0:j + 2 + TW], scalar1=wcol, scalar2=None,
                            op0=ALU.mult), reads=r_aT[cc] + [r_cols], writes=[r_tmpf[fa]])
                    else:
                        B.op("dve", lambda e, cc=cc, j=j, fa=fa, wcol=wcol: e.scalar_tensor_tensor(
                            out=tmpf[fa][:, c0:TW], in0=aT[:, cc, j + 2 + c0:j + 2 + TW], scalar=wcol,
                            in1=tmpf[fa][:, c0:TW], op0=ALU.mult, op1=ALU.add),
                            reads=r_aT[cc] + [r_cols, r_tmpf[fa]], writes=[r_tmpf[fa]])
                return fa

            def conv_chunk(cc, b1, b2, fa):
                c0 = C0[0]
                sl, r_sl, hd = next_slab("D31_%d" % cc)
                bi = nbank()

                def cv(e, cc=cc, bi=bi, sl=sl):
                    last = None
                    for j in range(NPE):
                        last = e.matmul(banks[bi][:, c0:TW], lhsT=sl[:, j * P:(j + 1) * P],
                                        rhs=aT[:, cc, j + 2 + c0:j + 2 + TW], start=(j == 0), stop=(j == NPE - 1))
                    return last
                B.op("pe", cv, reads=[r_sl] + r_aT[cc], writes=[r_bank[bi]])
                release(hd)
                B.op("dve", lambda e, cc=cc, bi=bi, fa=fa: e.scalar_tensor_tensor(
                    out=cT[:, cc, c0:TW], in0=banks[bi][:, c0:TW], scalar=colap(C_CB + cc), in1=tmpf[fa][:, c0:TW],
                    op0=ALU.add, op1=ALU.add), reads=[r_bank[bi], r_cols, r_tmpf[fa]], writes=r_cT[cc])
                t1, t2 = nxt("tb", 4), nxt("tb", 4)
                B.op("act", lambda e, cc=cc, t1=t1: e.activation(out=tmpb[t1][:, c0:TW], in_=cT[:, cc, c0:TW],
                                                                  func=AF.Copy),
                     reads=r_cT[cc], writes=[r_tmpb[t1]])
                B.op("act", lambda e, cc=cc, t2=t2: e.activation(out=tmpb[t2][:, c0:TW], in_=cT[:, cc, c0:TW],
                                                                  func=AF.Square),
                     reads=r_cT[cc], writes=[r_tmpb[t2]])
                return (cc, t1, t2)

            def conv_stats(args, b1, b2):
                c0 = C0[0]
                cc, t1, t2 = args
                B.op("pe", lambda e, cc=cc, t1=t1: e.matmul(banks[b1][:, c0:TW], lhsT=onesb[:, :], rhs=tmpb[t1][:, c0:TW],
                                                            start=(cc == 0), stop=(cc == KC - 1)),
                     reads=[r_onesb, r_tmpb[t1]], writes=[r_bank[b1]])
                B.op("pe", lambda e, cc=cc, t2=t2: e.matmul(banks[b2][:, c0:TW], lhsT=onesb[:, :], rhs=tmpb[t2][:, c0:TW],
                                                            start=(cc == 0), stop=(cc == KC - 1)),
                     reads=[r_onesb, r_tmpb[t2]], writes=[r_bank[b2]])

            def mgates(s):
                c0 = C0[0]
                sl, r_sl, hd = next_slab("M%d" % s)
                for j in range(4):
                    mc = 4 * s + j
                    bi = nbank()
                    mm_feat(bi, sl, r_sl, j * P, hT_rhs(), flat(r_hT))
                    B.op("act", lambda e, bi=bi, mc=mc: e.activation(out=gab[:, mc, c0:TW], in_=banks[bi][:, c0:TW],
                                                                      func=AF.Sigmoid, bias=colap(C_BM + mc)),
                         reads=[r_bank[bi], r_cols], writes=r_gab[mc])
                release(hd)

            def branch(nm, srcT, r_srcT, goff, accumulate):
                c0 = C0[0]
                for s in range(2):
                    sl, r_sl, hd = next_slab("%s%d" % (nm, s))
                    for j in range(4):
                        dc = 4 * s + j
                        bi = nbank()
                        mm_feat(bi, sl, r_sl, j * P, [(k, srcT[:, k, c0:TW]) for k in range(KC)], flat(r_srcT))
                        if not accumulate:
                            B.op("dve", lambda e, bi=bi, dc=dc: e.tensor_tensor(
                                out=mixT[:, dc, c0:TW], in0=banks[bi][:, c0:TW], in1=gab[:, goff + dc, c0:TW], op=ALU.mult),
                                reads=[r_bank[bi]] + r_gab[goff + dc], writes=r_mixT[dc])
                        else:
                            t = nxt("tb", 4)
                            B.op("dve", lambda e, bi=bi, dc=dc, t=t: e.tensor_tensor(
                                out=tmpb[t][:, c0:TW], in0=banks[bi][:, c0:TW], in1=gab[:, goff + dc, c0:TW], op=ALU.mult),
                                reads=[r_bank[bi]] + r_gab[goff + dc], writes=[r_tmpb[t]])
                            B.op("dve", lambda e, dc=dc, t=t: e.tensor_tensor(
                                out=mixT[:, dc, c0:TW], in0=mixT[:, dc, c0:TW], in1=tmpb[t][:, c0:TW], op=ALU.add),
                                reads=[r_tmpb[t]] + r_mixT[dc], writes=r_mixT[dc])
                    release(hd)

            out_events = []

            pend_kv = []

            def pre_body(ti, hoist_normA, hoist_normB, hoist_glu, lazy):
                vset = ti % 2
                proj_phase(False, vset, pend_kv, after_glow=hoist_normA)
                lazy()
                hoist_normB()
                while pend_kv:
                    pend_kv.pop(0)()
                for c in range(8):
                    pend_kv.append(lambda c=c, vset=vset: gla_kv(c, vset))
                if ti == npre - 1:
                    while pend_kv:
                        pend_kv.pop(0)()
                    hoist_glu()

            def main_body(ti, mi, hoist_normA, hoist_normB, hoist_glu):
                halo = (mi == 0)
                slot = ti % 2
                c0 = 384 if halo else 0
                b0 = c0 // P
                C0[0] = c0
                proj_phase(True)
                b1, b2 = nbank(), nbank()
                pinned.add(b1)
                pinned.add(b2)
                pend = None
                fa_next = conv_dve(0)
                for c in range(8):
                    gla_kv(c)
                    if c * 64 >= c0:
                        gla_sb(c)
                    fa_cur = fa_next
                    if c + 1 < 8:
                        fa_next = conv_dve(c + 1)
                    cur = conv_chunk(c, b1, b2, fa_cur)
                    if pend is not None:
                        conv_stats(pend, b1, b2)
                    pend = cur
                    if c * 64 >= c0:
                        gla_o(c)
                conv_stats(pend, b1, b2)
                B.op("dve", lambda e: e.tensor_copy(out=aT[:, :, 0:32], in_=aT[:, :, TW:TW + 32]),
                     reads=flat(r_aT), writes=flat(r_aT))
                f_mu, f_var, f_rs, f_nmr = nxt("tf", NTF), nxt("tf", NTF), nxt("tf", NTF), nxt("tf", NTF)
                B.op("dve", lambda e: e.tensor_scalar(out=tmpf[f_mu][:, c0:TW], in0=banks[b1][:, c0:TW], scalar1=1.0 / D,
                                                      scalar2=None, op0=ALU.mult),
                     reads=[r_bank[b1]], writes=[r_tmpf[f_mu]])
                B.op("dve", lambda e: e.tensor_tensor(out=tmpf[f_var][:, c0:TW], in0=tmpf[f_mu][:, c0:TW],
                                                      in1=tmpf[f_mu][:, c0:TW], op=ALU.mult),
                     reads=[r_tmpf[f_mu]], writes=[r_tmpf[f_var]])
                B.op("dve", lambda e: e.scalar_tensor_tensor(out=tmpf[f_var][:, c0:TW], in0=banks[b2][:, c0:TW],
                                                             scalar=1.0 / D, in1=tmpf[f_var][:, c0:TW], op0=ALU.mult,
                                                             op1=ALU.subtract),
                     reads=[r_bank[b2], r_tmpf[f_var]], writes=[r_tmpf[f_var]])
                pinned.discard(b1)
                pinned.discard(b2)
                rstd_from(tmpf[f_rs][:, c0:TW], r_tmpf[f_rs], tmpf[f_var][:, c0:TW], [r_tmpf[f_var]], 1.0)
                B.op("dve", lambda e: e.scalar_tensor_tensor(out=tmpf[f_nmr][:, c0:TW], in0=tmpf[f_mu][:, c0:TW], scalar=-1.0,
                                                             in1=tmpf[f_rs][:, c0:TW], op0=ALU.mult, op1=ALU.mult),
                     reads=[r_tmpf[f_mu], r_tmpf[f_rs]], writes=[r_tmpf[f_nmr]])
                o_rs = []

                def o_sq(h):
                    ts = []
                    for hf in range(2):
                        t = nxt("tb", 4)
                        ts.append(t)
                        B.op("act", lambda e, h=h, hf=hf, t=t: e.activation(out=tmpb[t][:, c0:TW],
                                                                             in_=oT[:, 2 * h + hf, c0:TW], func=AF.Square),
                             reads=r_oT[2 * h + hf], writes=[r_tmpb[t]])
                    return ts

                def o_stat_mm(h, ts):
                    bi = nbank()
                    pinned.add(bi)

                    def stf(e, bi=bi, ts=ts):
                        e.matmul(banks[bi][:, c0:TW], lhsT=onesb[:, :], rhs=tmpb[ts[0]][:, c0:TW], start=True, stop=False)
                        return e.matmul(banks[bi][:, c0:TW], lhsT=onesb[:, :], rhs=tmpb[ts[1]][:, c0:TW], start=False,
                                        stop=True)
                    B.op("pe", stf, reads=[r_onesb, r_tmpb[ts[0]], r_tmpb[ts[1]]], writes=[r_bank[bi]])
                    return bi

                def o_rstd(bi):
                    f = nxt("tf", NTF)
                    rstd_from(tmpf[f][:, c0:TW], r_tmpf[f], banks[bi][:, c0:TW], [r_bank[bi]], 1.0 / 256)
                    o_rs.append(f)
                    pinned.discard(bi)
                hoist_normA()
                tsa, tsb = o_sq(0), o_sq(1)
                mgates(0)
                sb_ = [o_stat_mm(0, tsa), o_stat_mm(1, tsb)]
                tsa, tsb = o_sq(2), o_sq(3)
                mgates(1)
                sb_ += [o_stat_mm(2, tsa), o_stat_mm(3, tsb)]
                for bi_ in sb_:
                    o_rstd(bi_)
                for cc in range(KC):
                    B.op("dve", lambda e, cc=cc: e.tensor_tensor(out=cT[:, cc, c0:TW], in0=cT[:, cc, c0:TW],
                                                                 in1=tmpf[f_rs][:, c0:TW], op=ALU.mult),
                         reads=r_cT[cc] + [r_tmpf[f_rs]], writes=r_cT[cc])
                    B.op("dve", lambda e, cc=cc: e.tensor_tensor(out=cT[:, cc, c0:TW], in0=cT[:, cc, c0:TW],
                                                                 in1=tmpf[f_nmr][:, c0:TW], op=ALU.add),
                         reads=r_cT[cc] + [r_tmpf[f_nmr]], writes=r_cT[cc])
                    B.op("act", lambda e, cc=cc: e.activation(out=sT[:, cc, c0:TW], in_=cT[:, cc, c0:TW], func=AF.Silu,
                                                              bias=colap(C_LNB + cc), scale=colap(C_LNG + cc)),
                         reads=r_cT[cc] + [r_cols], writes=r_sT[cc])
                for h in range(4):
                    f = o_rs[h]
                    for hf in range(2):
                        vc = 2 * h + hf
                        B.op("dve", lambda e, vc=vc, f=f: e.tensor_tensor(out=oT[:, vc, c0:TW], in0=oT[:, vc, c0:TW],
                                                                           in1=tmpf[f][:, c0:TW], op=ALU.mult),
                             reads=r_oT[vc] + [r_tmpf[f]], writes=r_oT[vc])
                        B.op("dve", lambda e, vc=vc: e.scalar_tensor_tensor(
                            out=sg[:, vc, c0:TW], in0=oT[:, vc, c0:TW], scalar=colap(C_GN + vc), in1=sg[:, vc, c0:TW],
                            op0=ALU.mult, op1=ALU.mult), reads=r_oT[vc] + [r_cols] + r_sg[vc], writes=r_sg[vc])
                mgates(2)
                mgates(3)
                hoist_normB()
                branch("CO", sT, r_sT, 0, False)
                branch("GLO", sg, r_sg, 8, True)
                for hf in range(2):
                    sl, r_sl, hd = next_slab("WO%d" % hf)
                    for b in range(b0, NB):
                        bi = nbank()
                        lhs = [(k, mixT[:, k, b * P:(b + 1) * P]) for k in range(KC)]
                        mm_tok(bi, lhs, flat(r_mixT), sl, r_sl)
                        B.op("dve", lambda e, bi=bi, b=b, hf=hf, slot=slot: e.tensor_tensor(
                            out=xtm[slot][:, b, hf * 512:(hf + 1) * 512],
                            in0=xtm[slot][:, b, hf * 512:(hf + 1) * 512], in1=banks[bi][:, :], op=ALU.add),
                            reads=[r_bank[bi], r_xtm[slot]], writes=[r_xtm[slot]])
                    release(hd)
                normA(slot)
                hoist_glu()
                normB(h2T, r_h2T, C_GFFN)
                dfs = None
                pend3 = None

                def conv3(args):
                    s_, q, ci, u, dsl, r_dsl = args
                    bc = nbank()
                    m0 = (s_ % 2) * 12 + q * 3

                    def c3(e, bc=bc, u=u, m0=m0, dsl=dsl):
                        last = None
                        for j in range(3):
                            last = e.matmul(banks[bc][:, c0:TW], lhsT=dsl[:, (m0 + j) * P:(m0 + j + 1) * P],
                                            rhs=ub[u][:, j:j + TW], start=(j == 0), stop=(j == 2))
                        return last
                    B.op("pe", c3, reads=[r_dsl, r_ub[u]], writes=[r_bank[bc]])
                    if q < 2:
                        B.op("act", lambda e, bc=bc, q=q, ci=ci: e.activation(
                            out=sgt[:, q, :], in_=banks[bc][:, c0:TW], func=AF.Silu, bias=colap(C_FB + ci)),
                            reads=[r_bank[bc], r_cols], writes=[r_sgt[q]])
                    else:
                        fc = 2 * s_ + (q - 2)
                        B.op("dve", lambda e, bc=bc, q=q, ci=ci, fc=fc: e.scalar_tensor_tensor(
                            out=gT[:, fc, :], in0=banks[bc][:, c0:TW], scalar=colap(C_FB + ci), in1=sgt[:, q - 2, :],
                            op0=ALU.add, op1=ALU.mult), reads=[r_bank[bc], r_cols, r_sgt[q - 2]],
                            writes=r_gT[fc])

                old_dfs = None
                for s in range(11):
                    if not halo and s % 2 == 0:
                        old_dfs = dfs
                        dfs = next_slab("DF%d" % (s // 2))
                    sl, r_sl, hd = next_slab("UP%d" % s)
                    for q in range(4):
                        ci = 2 * s + q if q < 2 else 22 + 2 * s + (q - 2)
                        bu = nbank()
                        mm_feat(bu, sl, r_sl, q * P, [(k, h2T[:, k, c0:TW]) for k in range(KC)], flat(r_h2T))
                        u = nxt("ub", 3)
                        B.op("act", lambda e, bu=bu, u=u: e.activation(out=ub[u][:, 2 + c0:2 + TW], in_=banks[bu][:, c0:TW],
                                                                        func=AF.Copy),
                             reads=[r_bank[bu]], writes=[r_ub[u]])
                        if halo:
                            B.op("pool", lambda e, u=u, ci=ci: e.tensor_scalar(
                                out=uhalo[:, ci, :], in0=ub[u][:, TW:TW + 2], scalar1=colap(C_FLAG), scalar2=None,
                                op0=ALU.mult), reads=[r_ub[u], r_cols], writes=[r_uhalo])
                            continue
                        B.op("pool", lambda e, u=u, ci=ci: e.tensor_copy(out=ub[u][:, 0:2], in_=uhalo[:, ci, :]),
                             reads=[r_uhalo], writes=[r_ub[u]])
                        B.op("pool", lambda e, u=u, ci=ci: e.tensor_copy(out=uhalo[:, ci, :],
                                                                         in_=ub[u][:, TW:TW + 2]),
                             reads=[r_ub[u]], writes=[r_uhalo])
                        if pend3 is not None:
                            conv3(pend3)
                        pend3 = (s, q, ci, u, dfs[0], dfs[1])
                        if old_dfs is not None and q == 0:
                            release(old_dfs[2])
                            old_dfs = None
                    release(hd)
                if pend3 is not None:
                    conv3(pend3)
                if dfs is not None:
                    release(dfs[2])
                C0[0] = 0
                if halo:
                    return
                for hf in range(2):
                    bks = [nbank() for _ in range(NB)]
                    for b_ in bks:
                        pinned.add(b_)
                    for pt in range(3):
                        sl, r_sl, hd = next_slab("DN%d" % (hf * 3 + pt))
                        nk = 8 if pt < 2 else 6
                        sv = sl[:, :].rearrange("p (kc c) -> p kc c", c=512)

                        def dn(e, pt=pt, nk=nk, sv=sv, bks=bks):
                            last = None
                            for kk in range(nk):
                                fc = 8 * pt + kk
                                for b in range(NB):
                                    last = e.matmul(banks[bks[b]][:, :], lhsT=gT[:, fc, b * P:(b + 1) * P],
                                                    rhs=sv[:, kk, :], start=(fc == 0), stop=(fc == 21))
                            return last
                        B.op("pe", dn, reads=[r_sl] + flat(r_gT[8 * pt:8 * pt + nk]),
                             writes=[r_bank[b_] for b_ in bks])
                        release(hd)
                    for b in range(NB):
                        B.op("dve", lambda e, b=b, hf=hf, bks=bks, slot=slot: e.tensor_tensor(
                            out=xtm[slot][:, b, hf * 512:(hf + 1) * 512],
                            in0=xtm[slot][:, b, hf * 512:(hf + 1) * 512], in1=banks[bks[b]][:, :], op=ALU.add),
                            reads=[r_bank[bks[b]], r_xtm[slot]], writes=[r_xtm[slot]])
                    for b_ in bks:
                        pinned.discard(b_)
                rms_stats(slot)
                for b in range(NB):
                    B.op("dve", lambda e, b=b, slot=slot: e.scalar_tensor_tensor(
                        out=xtm[slot][:, b, :], in0=xtm[slot][:, b, :], scalar=rstd[:, b:b + 1], in1=gfin[:, :],
                        op0=ALU.mult, op1=ALU.mult), reads=[r_xtm[slot], r_rstd, r_gfin], writes=[r_xtm[slot]])
                o0 = (mi - 1) * TW
                dst = out_d[o0:o0 + TW, :].rearrange("(b p) f -> p b f", p=P)
                r_out = rs("outst%d" % slot)
                ev = dma_op("sp", lambda e, dst=dst, slot=slot: [e.dma_start(out=dst, in_=xtm[slot][:, :, :])],
                            r_out, 1, reads=[r_xtm[slot]])
                out_events.append(ev)

            ntile = npre + nmain
            x_loaded = set()

            def ensure_x(t):
                if t < ntile and t not in x_loaded:
                    x_loaded.add(t)
                    load_x(t)
            ensure_x(0)
            ensure_x(1)
            front_normA(0)
            front_normB()
            if npre == 0:
                lazy_diag(99)
                front_glu(256)
            for ti in range(ntile):
                nxt_main = (ti + 1 >= npre)
                has_next = (ti + 1 < ntile)
                if ti < npre:
                    ensure_x(ti + 2)
                ensure_x(ti + 1)

                def hoist_normA(ti=ti, has_next=has_next):
                    if has_next:
                        front_normA(ti + 1)

                def hoist_normB(has_next=has_next):
                    if has_next:
                        old = C0[0]
                        C0[0] = 0
                        front_normB()
                        C0[0] = old

                def hoist_glu(ti=ti, has_next=has_next, nxt_main=nxt_main):
                    if has_next and nxt_main:
                        front_glu(256 if ti + 1 == npre else 0)

                def lazy(ti=ti):
                    lazy_diag(99 if ti == npre - 1 else 2)
                if ti < npre:
                    pre_body(ti, hoist_normA, hoist_normB, hoist_glu, lazy)
                else:
                    main_body(ti, ti - npre, hoist_normA, hoist_normB, hoist_glu)
            if not dry:
                assert st["pos"] == len(seq), (st["pos"], len(seq))
                fin = {}
                for k, v in out_events:
                    fin[k] = max(fin.get(k, 0), v)
                B.q["sp"].append((list(fin.items()), lambda e: e.nop(), None))
            return rec

        seq_rec = emit(Builder(None, None, dry=True), None)
        B = Builder(nc, stack)
        emit(B, seq_rec)

        with nc.Block() as block:
            @block.tensor
            def _(e):
                B.run("pe", e)

            @block.scalar
            def _(e):
                B.run("act", e)

            @block.vector
            def _(e):
                B.run("dve", e)

            @block.gpsimd
            def _(e):
                B.run("pool", e)

            @block.sync
            def _(e):
                B.run("sp", e)
    return nc


def host_consts():
    cst = np.zeros((P, 384), np.float32)
    cst[:, 0:128] = np.eye(P, dtype=np.float32)
    s = np.arange(P)[:, None]
    t = np.arange(P)[None, :]
    cst[:, 128:256] = np.where((s > t) & (s // 64 == t // 64), -1.0 / 16.0, 0.0)
    cst[:, 256:384] = 1.0
    mt = np.zeros((P, 2), np.float32)
    mt[:64, 0] = -1.0 / 16.0
    mt[64:, 1] = -1.0 / 16.0
    return cst, mt


def host_cols(p, flag):
    cols = np.zeros((P, NCOLS), np.float32)

    def put(off, vec):
        v = np.asarray(vec, np.float32).reshape(-1, P).T
        cols[:, off:off + v.shape[1]] = v
    put(C_GMIX, p["norm_mix"][0])
    put(C_GFFN, p["norm_ffn"][0])
    put(C_BM, p["b_merge"][0])
    put(C_CB, p["conv_dw_b"][0])
    put(C_LNG, p["conv_ln_g"][0])
    put(C_LNB, p["conv_ln_b"][0])
    put(C_GN, p["gla_norm"][0].reshape(-1))
    put(C_FB, p["ffn_dw_b"][0])
    cd = np.asarray(p["conv_dw"][0], np.float32)
    cols[:, C_C31:C_C31 + 248] = cd.reshape(31, 8, P).transpose(2, 1, 0).reshape(P, 248)
    fd = np.asarray(p["ffn_dw"][0], np.float32)
    cols[:, C_F3:C_F3 + 132] = fd.reshape(3, 44, P).transpose(2, 1, 0).reshape(P, 132)
    cols[:, C_FLAG] = flag
    cols[:, C_EPS] = EPS
    cols[:, C_ONE] = 1.0
    return cols


def host_diag(p):
    dg = np.zeros((14, P, 4096), np.float32)
    idx = np.arange(P)
    cd = np.asarray(p["conv_dw"][0], np.float32)
    for c in range(8):
        for j in range(31):
            dg[c, idx, j * P + idx] = cd[j, c * P:(c + 1) * P]
    fd = np.asarray(p["ffn_dw"][0], np.float32)
    for f in range(6):
        m = 0
        for ss in (2 * f, 2 * f + 1):
            if ss > 10:
                break
            for q in range(4):
                ci = 2 * ss + q if q < 2 else 22 + 2 * ss + (q - 2)
                for j in range(3):
                    dg[8 + f, idx, m * P + idx] = fd[j, ci * P:(ci + 1) * P]
                    m += 1
    return dg


def make_in_maps(p, xs, flags):
    cst, mt = host_consts()
    gfin = np.ascontiguousarray(np.broadcast_to(np.asarray(p["norm_final"], np.float32)[None, :], (P, D)))
    wgk = np.concatenate([np.asarray(p["w_gk2"][0], np.float32), np.asarray(p["b_gk"][0], np.float32)[None, :]], 0)
    shared = {
        "w_in": np.ascontiguousarray(np.asarray(p["w_in"][0], np.float32)),
        "w_conv_out": np.ascontiguousarray(np.asarray(p["w_conv_out"][0], np.float32)),
        "w_gla_out": np.ascontiguousarray(np.asarray(p["w_gla_out"][0], np.float32)),
        "w_out": np.ascontiguousarray(np.asarray(p["w_out"][0], np.float32)),
        "w_up": np.ascontiguousarray(np.asarray(p["w_up"][0], np.float32)),
        "w_down": np.ascontiguousarray(np.asarray(p["w_down"][0], np.float32)),
        "gfin": gfin, "cst": cst, "mtot": mt, "wgk": np.ascontiguousarray(wgk),
    }
    maps = []
    for xc, fl in zip(xs, flags):
        m = dict(shared)
        m["x"] = np.ascontiguousarray(xc, dtype=np.float32)
        m["cols"] = host_cols(p, fl)
        maps.append(m)
    return maps


NPRE, NMAIN = 7, 9
_CACHE = {}


def kernel(**inputs):
    x = np.asarray(inputs["x"], np.float32)
    bsz, seq, _ = x.shape
    half = seq // 2
    xs, flags = [], []
    for c in range(8):
        b, hf = c // 2, c % 2
        if hf == 1:
            xs.append(x[b])
            flags.append(1.0)
        else:
            xs.append(np.concatenate([np.zeros((half, D), np.float32), x[b, :half]], 0))
            flags.append(0.0)
    maps = make_in_maps(inputs, xs, flags)
    if "nc" not in _CACHE:
        _CACHE["nc"] = build_program(NPRE, NMAIN)
    res = run_bass_kernel_spmd(_CACHE["nc"], maps, core_ids=list(range(8)))
    y = np.empty((bsz, seq, D), np.float32)
    for c in range(8):
        b, hf = c // 2, c % 2
        y[b, hf * half:(hf + 1) * half] = res.results[c]["out"]
    return y
```
